# Optimizing a Trainium2 kernel written in Bass

```python
import jax, jax.numpy as jnp
from jax import lax
import numpy as np

D_MODEL = 2048
BATCH = 8
SEQ = 2048
DEPTH = 4
DEC_BATCH = 16
DEC_SEQ = 2048
PAST_LEN = 128

GRID_W = 64
N_MIXERS = 4
MEM_TOKENS = 256
NORM_EPS = 1e-6
Q_BLOCK = 128

RET_HEADS = 8
RET_DK = D_MODEL // RET_HEADS
RET_DV = 2 * RET_DK
RET_CHUNK = 128
RET_ROPE_BASE = 10000.0
HG_HEADS = 16
HG_DK = 128
HG_DV = D_MODEL // HG_HEADS
HG_CHUNK = 32
MLA_HEADS = 16
MLA_Q_RANK = 512
MLA_KV_RANK = 512
MLA_NOPE = 128
MLA_ROPE = 64
MLA_V = 128
MLA_ROPE_BASE = 10000.0
GQA_HEADS = 16
GQA_KV_HEADS = 4
GQA_HD = 128
GQA_ROPE_BASE = 10000.0
MEM_HEADS = 4
MEM_HD = 128
D_FF = 4 * D_MODEL

N_RET = (DEPTH + N_MIXERS - 1) // N_MIXERS
N_HG = (DEPTH + N_MIXERS - 2) // N_MIXERS
N_MLA = (DEPTH + N_MIXERS - 3) // N_MIXERS
N_GQA = (DEPTH + N_MIXERS - 4) // N_MIXERS

kernel_name = 'hybrid_bidir_encoder_ret_hgrn2_mla_gqa'


def rmsnorm(x, g):
    xf = x.astype(jnp.float32)
    y = xf * lax.rsqrt(jnp.mean(xf * xf, axis=-1, keepdims=True) + NORM_EPS)
    return (y * g.astype(jnp.float32)).astype(x.dtype)


def rope(x, pos, base):
    half = x.shape[-1] // 2
    freqs = base ** (-jnp.arange(half, dtype=jnp.float32) / half)
    ang = pos[:, None] * freqs[None, :]
    cos = jnp.cos(ang)[None, :, None, :].astype(x.dtype)
    sin = jnp.sin(ang)[None, :, None, :].astype(x.dtype)
    x1, x2 = x[..., :half], x[..., half:]
    return jnp.concatenate([x1 * cos - x2 * sin, x1 * sin + x2 * cos], axis=-1)


def axial_rope(x, row_pos, col_pos, base):
    d2 = x.shape[-1] // 2
    return jnp.concatenate([rope(x[..., :d2], row_pos, base), rope(x[..., d2:], col_pos, base)], axis=-1)


def block_attention(q, k, v, scale):
    B, L, KH, G, D = q.shape
    nb = L // Q_BLOCK
    qb = q.reshape(B, nb, Q_BLOCK, KH, G, D).transpose(1, 0, 2, 3, 4, 5)

    def one(qi):
        s = jnp.einsum('bqhgd,bshd->bhgqs', qi, k, preferred_element_type=jnp.float32) * scale
        p = jax.nn.softmax(s, axis=-1).astype(v.dtype)
        return jnp.einsum('bhgqs,bshe->bqhge', p, v)

    o = lax.map(one, qb)
    return o.transpose(1, 0, 2, 3, 4, 5).reshape(B, L, KH, G, v.shape[-1])


def to_chunks(a, chunk):
    B, L, H, d = a.shape
    return a.reshape(B, L // chunk, chunk, H, d).transpose(1, 0, 2, 3, 4)


def from_chunks(a):
    N, B, C, H, d = a.shape
    return a.transpose(1, 0, 2, 3, 4).reshape(B, N * C, H, d)


def retention_scan(q, k, v, log_gamma):
    B, L, H, dk = q.shape
    dv = v.shape[-1]
    C = RET_CHUNK
    idx = jnp.arange(C, dtype=jnp.float32)
    diff = idx[:, None] - idx[None, :]
    decay_intra = jnp.where(diff[None] >= 0, jnp.exp(jnp.maximum(diff, 0.0)[None] * log_gamma[:, None, None]), 0.0)
    q_decay = jnp.exp((idx + 1.0)[:, None] * log_gamma[None, :])
    k_decay = jnp.exp((C - 1.0 - idx)[:, None] * log_gamma[None, :])
    chunk_decay = jnp.exp(C * log_gamma)

    def step(S, xs):
        qc, kc, vc = xs
        s = jnp.einsum('bihd,bjhd->bhij', qc, kc) * decay_intra[None]
        o = jnp.einsum('bhij,bjhe->bihe', s, vc) + jnp.einsum('bihd,bhde->bihe', qc * q_decay[None, :, :, None], S)
        S = S * chunk_decay[None, :, None, None] + jnp.einsum('bjhd,bjhe->bhde', kc * k_decay[None, :, :, None], vc)
        return S, o

    S0 = jnp.zeros((B, H, dk, dv), q.dtype)
    _, o = lax.scan(step, S0, (to_chunks(q, C), to_chunks(k, C), to_chunks(v, C)))
    return from_chunks(o)


def gated_scan(q, k, v, log_f):
    B, L, H, dk = q.shape
    dv = v.shape[-1]
    C = HG_CHUNK
    mask = jnp.tril(jnp.ones((C, C), dtype=bool))

    def step(S, xs):
        qc, kc, vc, lf = xs
        b = jnp.cumsum(lf, axis=1)
        q_t = qc * jnp.exp(b)
        k_t = kc * jnp.exp(-b)
        s = jnp.where(mask[None, None], jnp.einsum('bihd,bjhd->bhij', q_t, k_t), 0.0)
        o = jnp.einsum('bhij,bjhe->bihe', s, vc) + jnp.einsum('bihd,bhde->bihe', q_t, S)
        b_last = b[:, -1:]
        S = S * jnp.exp(b_last[:, 0])[..., None] + jnp.einsum('bjhd,bjhe->bhde', kc * jnp.exp(b_last - b), vc)
        return S, o

    S0 = jnp.zeros((B, H, dk, dv), q.dtype)
    _, o = lax.scan(step, S0, (to_chunks(q, C), to_chunks(k, C), to_chunks(v, C), to_chunks(log_f, C)))
    return from_chunks(o)


def retention_mixer(h, w_in, decay_logit, out_norm, w_out):
    B, L, _ = h.shape
    qk = RET_HEADS * RET_DK
    vw = RET_HEADS * RET_DV
    q, k, v, g = jnp.split(h @ w_in, [qk, 2 * qk, 2 * qk + vw], axis=-1)
    pos = jnp.arange(L, dtype=jnp.float32)
    q = rope(q.reshape(B, L, RET_HEADS, RET_DK), pos, RET_ROPE_BASE).astype(jnp.float32) * (RET_DK ** -0.5)
    k = rope(k.reshape(B, L, RET_HEADS, RET_DK), pos, RET_ROPE_BASE).astype(jnp.float32)
    v = v.reshape(B, L, RET_HEADS, RET_DV).astype(jnp.float32)
    lg = jax.nn.log_sigmoid(decay_logit.astype(jnp.float32))
    o_f = retention_scan(q, k, v, lg[0])
    o_b = jnp.flip(retention_scan(jnp.flip(q, 1), jnp.flip(k, 1), jnp.flip(v, 1), lg[1]), 1)
    o = rmsnorm(o_f + o_b, out_norm).astype(h.dtype) * jax.nn.silu(g).reshape(B, L, RET_HEADS, RET_DV)
    return o.reshape(B, L, vw) @ w_out


def hgrn2_mixer(h, w_in, lb, out_norm, w_out):
    B, L, _ = h.shape
    kw = HG_HEADS * HG_DK
    vw = HG_HEADS * HG_DV
    q, f_fw, f_bw, i, g = jnp.split(h @ w_in, [kw, 2 * kw, 3 * kw, 3 * kw + vw], axis=-1)
    lbf = lb.astype(jnp.float32)

    def gates(fz):
        f = lbf + (1.0 - lbf) * jax.nn.sigmoid(fz.astype(jnp.float32))
        return jnp.log(f).reshape(B, L, HG_HEADS, HG_DK), (1.0 - f).reshape(B, L, HG_HEADS, HG_DK)

    q = q.astype(jnp.float32).reshape(B, L, HG_HEADS, HG_DK)
    iv = i.astype(jnp.float32).reshape(B, L, HG_HEADS, HG_DV)
    lf_f, k_f = gates(f_fw)
    lf_b, k_b = gates(f_bw)
    o_f = gated_scan(q, k_f, iv, lf_f)
    o_b = jnp.flip(gated_scan(jnp.flip(q, 1), jnp.flip(k_b, 1), jnp.flip(iv, 1), jnp.flip(lf_b, 1)), 1)
    o = rmsnorm(o_f + o_b, out_norm).astype(h.dtype) * jax.nn.silu(g).reshape(B, L, HG_HEADS, HG_DV)
    return o.reshape(B, L, vw) @ w_out


def mla_mixer(h, w_in, q_norm, kv_norm, w_qb, w_kvb, qk_norm, w_out):
    B, L, _ = h.shape
    cq, ckv, k_rope = jnp.split(h @ w_in, [MLA_Q_RANK, MLA_Q_RANK + MLA_KV_RANK], axis=-1)
    q = (rmsnorm(cq, q_norm) @ w_qb).reshape(B, L, MLA_HEADS, MLA_NOPE + MLA_ROPE)
    kv = (rmsnorm(ckv, kv_norm) @ w_kvb).reshape(B, L, MLA_HEADS, MLA_NOPE + MLA_V)
    k_nope, v = kv[..., :MLA_NOPE], kv[..., MLA_NOPE:]
    k = jnp.concatenate([k_nope, jnp.broadcast_to(k_rope[:, :, None, :], (B, L, MLA_HEADS, MLA_ROPE))], axis=-1)
    q = rmsnorm(q, qk_norm[0])
    k = rmsnorm(k, qk_norm[1])
    pos = jnp.arange(L, dtype=jnp.float32)
    q = jnp.concatenate([q[..., :MLA_NOPE], rope(q[..., MLA_NOPE:], pos, MLA_ROPE_BASE)], axis=-1)
    k = jnp.concatenate([k[..., :MLA_NOPE], rope(k[..., MLA_NOPE:], pos, MLA_ROPE_BASE)], axis=-1)
    o = block_attention(q[:, :, :, None, :], k, v, (MLA_NOPE + MLA_ROPE) ** -0.5)
    return o.reshape(B, L, MLA_HEADS * MLA_V) @ w_out


def gqa_mixer(h, w_in, qk_norm, w_out):
    B, L, _ = h.shape
    qw = GQA_HEADS * GQA_HD
    kvw = GQA_KV_HEADS * GQA_HD
    q, k, v = jnp.split(h @ w_in, [qw, qw + kvw], axis=-1)
    q = rmsnorm(q.reshape(B, L, GQA_HEADS, GQA_HD), qk_norm[0])
    k = rmsnorm(k.reshape(B, L, GQA_KV_HEADS, GQA_HD), qk_norm[1])
    rows = L // GRID_W
    row_pos = jnp.repeat(jnp.arange(rows), GRID_W).astype(jnp.float32)
    col_pos = jnp.tile(jnp.arange(GRID_W), rows).astype(jnp.float32)
    q = axial_rope(q, row_pos, col_pos, GQA_ROPE_BASE)
    k = axial_rope(k, row_pos, col_pos, GQA_ROPE_BASE)
    q = q.reshape(B, L, GQA_KV_HEADS, GQA_HEADS // GQA_KV_HEADS, GQA_HD)
    o = block_attention(q, k, v.reshape(B, L, GQA_KV_HEADS, GQA_HD), GQA_HD ** -0.5)
    return o.reshape(B, L, qw) @ w_out


def memory_xattn(h, m, w_q, w_kv, qk_norm, w_out):
    B, L, _ = h.shape
    M = m.shape[1]
    q = rmsnorm((h @ w_q).reshape(B, L, MEM_HEADS, MEM_HD), qk_norm[0])
    kv = (m @ w_kv).reshape(B, M, 2, MEM_HEADS, MEM_HD)
    k = rmsnorm(kv[:, :, 0], qk_norm[1])
    v = kv[:, :, 1]
    s = jnp.einsum('blhd,bmhd->bhlm', q, k, preferred_element_type=jnp.float32) * (MEM_HD ** -0.5)
    p = jax.nn.softmax(s, axis=-1).astype(v.dtype)
    o = jnp.einsum('bhlm,bmhd->blhd', p, v)
    return o.reshape(B, L, MEM_HEADS * MEM_HD) @ w_out


def sq_relu_mlp(h, w1, w2):
    a = jax.nn.relu(h @ w1)
    return (a * a) @ w2


def trunk(x, mem, p):
    s = jax.nn.softmax(p['hg_lb'].astype(jnp.float32), axis=0)
    lb_all = jnp.cumsum(s, axis=0) - s[0]
    for i in range(DEPTH):
        kind, j = i % N_MIXERS, i // N_MIXERS
        h = rmsnorm(x, p['norm_mix'][i])
        if kind == 0:
            x = x + retention_mixer(h, p['ret_w_in'][j], p['ret_decay'][j], p['ret_out_norm'][j], p['ret_w_out'][j])
        elif kind == 1:
            x = x + hgrn2_mixer(h, p['hg_w_in'][j], lb_all[i], p['hg_out_norm'][j], p['hg_w_out'][j])
        elif kind == 2:
            x = x + mla_mixer(h, p['mla_w_in'][j], p['mla_q_norm'][j], p['mla_kv_norm'][j], p['mla_w_qb'][j],
                              p['mla_w_kvb'][j], p['mla_qk_norm'][j], p['mla_w_out'][j])
        else:
            x = x + gqa_mixer(h, p['gqa_w_in'][j], p['gqa_qk_norm'][j], p['gqa_w_out'][j])
        h = rmsnorm(x, p['norm_mem'][i])
        m = rmsnorm(mem, p['norm_memtok'][i])
        x = x + memory_xattn(h, m, p['mem_w_q'][i], p['mem_w_kv'][i], p['mem_qk_norm'][i], p['mem_w_out'][i])
        h = rmsnorm(x, p['norm_mlp'][i])
        x = x + sq_relu_mlp(h, p['mlp_w1'][i], p['mlp_w2'][i])
    return x


def setup_inputs(seed: int = 0) -> dict:
    key = jax.random.key(seed)
    ks = iter(jax.random.split(key, 40))

    def nrm(shape, scale):
        return scale * jax.random.normal(next(ks), shape, jnp.float32)

    def dense(shape):
        return nrm(shape, shape[-2] ** -0.5)

    def gain(shape):
        return 1.0 + nrm(shape, 0.02)

    hidx = jnp.arange(RET_HEADS, dtype=jnp.float32)
    ret_logit0 = jnp.log(2.0 ** (5.0 + hidx) - 1.0)
    ret_in_w = 2 * RET_HEADS * RET_DK + 2 * RET_HEADS * RET_DV
    hg_in_w = 3 * HG_HEADS * HG_DK + 2 * HG_HEADS * HG_DV
    return {
        'x_prompt': nrm((BATCH, SEQ, D_MODEL), 1.0),
        'x_sample': nrm((DEC_BATCH, DEC_SEQ, D_MODEL), 1.0),
        'mem_prompt': nrm((BATCH, MEM_TOKENS, D_MODEL), 1.0),
        'mem_sample': nrm((DEC_BATCH, MEM_TOKENS, D_MODEL), 1.0),
        'norm_mix': gain((DEPTH, D_MODEL)),
        'norm_mem': gain((DEPTH, D_MODEL)),
        'norm_memtok': gain((DEPTH, D_MODEL)),
        'norm_mlp': gain((DEPTH, D_MODEL)),
        'ret_w_in': dense((N_RET, D_MODEL, ret_in_w)),
        'ret_decay': ret_logit0[None, None, :] + nrm((N_RET, 2, RET_HEADS), 0.05),
        'ret_out_norm': gain((N_RET, RET_DV)),
        'ret_w_out': dense((N_RET, RET_HEADS * RET_DV, D_MODEL)),
        'hg_w_in': dense((N_HG, D_MODEL, hg_in_w)),
        'hg_lb': nrm((DEPTH, HG_HEADS * HG_DK), 0.1),
        'hg_out_norm': gain((N_HG, HG_DV)),
        'hg_w_out': dense((N_HG, HG_HEADS * HG_DV, D_MODEL)),
        'mla_w_in': dense((N_MLA, D_MODEL, MLA_Q_RANK + MLA_KV_RANK + MLA_ROPE)),
        'mla_q_norm': gain((N_MLA, MLA_Q_RANK)),
        'mla_kv_norm': gain((N_MLA, MLA_KV_RANK)),
        'mla_w_qb': dense((N_MLA, MLA_Q_RANK, MLA_HEADS * (MLA_NOPE + MLA_ROPE))),
        'mla_w_kvb': dense((N_MLA, MLA_KV_RANK, MLA_HEADS * (MLA_NOPE + MLA_V))),
        'mla_qk_norm': gain((N_MLA, 2, MLA_NOPE + MLA_ROPE)),
        'mla_w_out': dense((N_MLA, MLA_HEADS * MLA_V, D_MODEL)),
        'gqa_w_in': dense((N_GQA, D_MODEL, (GQA_HEADS + 2 * GQA_KV_HEADS) * GQA_HD)),
        'gqa_qk_norm': gain((N_GQA, 2, GQA_HD)),
        'gqa_w_out': dense((N_GQA, GQA_HEADS * GQA_HD, D_MODEL)),
        'mem_w_q': dense((DEPTH, D_MODEL, MEM_HEADS * MEM_HD)),
        'mem_w_kv': dense((DEPTH, D_MODEL, 2 * MEM_HEADS * MEM_HD)),
        'mem_qk_norm': gain((DEPTH, 2, MEM_HD)),
        'mem_w_out': dense((DEPTH, MEM_HEADS * MEM_HD, D_MODEL)),
        'mlp_w1': dense((DEPTH, D_MODEL, D_FF)),
        'mlp_w2': dense((DEPTH, D_FF, D_MODEL)),
    }


def reference(x_prompt, x_sample, mem_prompt, mem_sample, norm_mix, norm_mem, norm_memtok, norm_mlp,
              ret_w_in, ret_decay, ret_out_norm, ret_w_out, hg_w_in, hg_lb, hg_out_norm, hg_w_out,
              mla_w_in, mla_q_norm, mla_kv_norm, mla_w_qb, mla_w_kvb, mla_qk_norm, mla_w_out,
              gqa_w_in, gqa_qk_norm, gqa_w_out, mem_w_q, mem_w_kv, mem_qk_norm, mem_w_out, mlp_w1, mlp_w2):
    params = dict(norm_mix=norm_mix, norm_mem=norm_mem, norm_memtok=norm_memtok, norm_mlp=norm_mlp,
                  ret_w_in=ret_w_in, ret_decay=ret_decay, ret_out_norm=ret_out_norm, ret_w_out=ret_w_out,
                  hg_w_in=hg_w_in, hg_lb=hg_lb, hg_out_norm=hg_out_norm, hg_w_out=hg_w_out,
                  mla_w_in=mla_w_in, mla_q_norm=mla_q_norm, mla_kv_norm=mla_kv_norm, mla_w_qb=mla_w_qb,
                  mla_w_kvb=mla_w_kvb, mla_qk_norm=mla_qk_norm, mla_w_out=mla_w_out,
                  gqa_w_in=gqa_w_in, gqa_qk_norm=gqa_qk_norm, gqa_w_out=gqa_w_out,
                  mem_w_q=mem_w_q, mem_w_kv=mem_w_kv, mem_qk_norm=mem_qk_norm, mem_w_out=mem_w_out,
                  mlp_w1=mlp_w1, mlp_w2=mlp_w2)
    y_prompt = trunk(x_prompt, mem_prompt, params)
    y_sample = trunk(x_sample, mem_sample, params)
    return (y_prompt, y_sample)
```

```python
import numpy as np
from contextlib import ExitStack
import concourse.bass as bass
import concourse.mybir as mybir
from concourse.bass_utils import run_bass_kernel_spmd

F32 = mybir.dt.float32
BF16 = mybir.dt.bfloat16
AF = mybir.ActivationFunctionType
ALU = mybir.AluOpType
AX = mybir.AxisListType

D = 2048
MEMT = 256
EPS = 1e-6
DFF = 8192
SEM_EPOCH = 20000
NDMASEM = 12


class Res:
    __slots__ = ("w", "r")

    def __init__(self):
        self.w = None
        self.r = {}


class Rot:
    def __init__(self, items):
        self.items = items
        self.i = 0

    def next(self):
        x = self.items[self.i % len(self.items)]
        self.i += 1
        return x


class Sched:
    def __init__(self, nc, stack, self_sync=True):
        self.nc = nc
        self.stack = stack
        self.self_sync = self_sync
        self.eng = {"pe": nc.tensor, "act": nc.scalar, "dve": nc.vector, "pool": nc.gpsimd, "sp": nc.sync}
        self.sem = {}
        self.cnt = {}
        self.nsem = 0
        for e in self.eng:
            self._new_sem(e)
        self.known = {e: {} for e in self.eng}
        self.res = {}
        self.dsem = {}
        self.dcnt = {}
        self.dnext = {}
        for q in ("sp", "pool"):
            self.dsem[q] = [self._alloc_sem(f"d{q}{i}") for i in range(NDMASEM)]
            self.dcnt[q] = [0] * NDMASEM
            self.dnext[q] = 0
        self.psum = [self.stack.enter_context(nc.psum_tensor(f"ps{i}", [128, 512], F32)) for i in range(8)]
        self.psum_i = 0
        self.ninst = {e: 0 for e in self.eng}

    def _alloc_sem(self, name):
        self.nsem += 1
        return self.stack.enter_context(self.nc.semaphore(name))

    def _new_sem(self, e):
        k = self.nsem
        h = self._alloc_sem(f"s{e}{k}")
        self.sem[e] = (f"{e}{k}", h)
        self.cnt[e] = 0

    def psum_next(self):
        p = self.psum[self.psum_i % 8]
        self.psum_i += 1
        return p

    def _r(self, key):
        if not isinstance(key, (str, tuple, int)):
            key = ("T", key.name)
        r = self.res.get(key)
        if r is None:
            r = self.res[key] = Res()
        return r

    def _wait(self, e, t):
        if t is None:
            return
        k, h, v = t
        own = self.sem[e][0] == k
        if own and (e == "pe" or not self.self_sync):
            return
        if self.known[e].get(k, 0) >= v:
            return
        self.known[e][k] = v
        self.eng[e].wait_ge(h, v)

    def emit(self, e, fn, reads=(), writes=(), dmaq=None):
        rs = [self._r(k) for k in reads]
        ws = [self._r(k) for k in writes]
        for r in rs:
            self._wait(e, r.w)
        for w in ws:
            self._wait(e, w.w)
            for t in w.r.values():
                self._wait(e, t)
        if dmaq is not None:
            i = self.dnext[dmaq] % NDMASEM
            self.dnext[dmaq] += 1
            h = self.dsem[dmaq][i]
            k = f"d{dmaq}{i}"
            prev = self.dcnt[dmaq][i]
            if prev:
                self._wait(e, (k, h, prev))
            self.dcnt[dmaq][i] = prev + 16
            t = (k, h, prev + 16)
            fn().then_inc(h, 16)
        else:
            if self.cnt[e] >= SEM_EPOCH:
                self._new_sem(e)
            k, h = self.sem[e]
            self.cnt[e] += 1
            t = (k, h, self.cnt[e])
            fn().then_inc(h, 1)
        self.ninst[e] += 1
        for r in rs:
            r.r[t[0]] = t
        for w in ws:
            w.w = t
            w.r = {}
        return t

    def barrier(self):
        ticks = []
        for e in self.eng:
            k, h = self.sem[e]
            if self.cnt[e]:
                ticks.append((k, h, self.cnt[e]))
        for q in self.dsem:
            for i in range(NDMASEM):
                if self.dcnt[q][i]:
                    ticks.append((f"d{q}{i}", self.dsem[q][i], self.dcnt[q][i]))
        for e in self.eng:
            for t in ticks:
                self._wait(e, t)
        self.res = {}

    def dma(self, q, out, in_, reads, writes):
        eng = self.eng[q]
        return self.emit(q, lambda: eng.dma_start(out=out, in_=in_), reads, writes, dmaq=q)

    def mm(self, out, lhsT, rhs, start, stop, reads, writes):
        return self.emit("pe", lambda: self.nc.tensor.matmul(out, lhsT, rhs, start=start, stop=stop), reads, writes)

    def tr(self, out, in_, ident, reads, writes):
        return self.emit("pe", lambda: self.nc.tensor.transpose(out, in_, ident), reads, writes)

    def act(self, out, in_, func, reads, writes, **kw):
        return self.emit("act", lambda: self.nc.scalar.activation(out, in_, func, **kw), reads, writes)

    def copy(self, e, out, in_, reads, writes):
        if e == "act":
            return self.emit("act", lambda: self.nc.scalar.copy(out, in_), reads, writes)
        eng = self.eng[e]
        return self.emit(e, lambda: eng.tensor_copy(out, in_), reads, writes)

    def tt(self, e, out, in0, in1, op, reads, writes):
        eng = self.eng[e]
        return self.emit(e, lambda: eng.tensor_tensor(out, in0, in1, op), reads, writes)

    def ts(self, e, out, in0, s1, s2, op0, op1, reads, writes):
        eng = self.eng[e]
        if op1 is None:
            return self.emit(e, lambda: eng.tensor_scalar(out, in0, s1, None, op0), reads, writes)
        return self.emit(e, lambda: eng.tensor_scalar(out, in0, s1, s2, op0, op1), reads, writes)

    def stt(self, out, in0, scalar, in1, op0, op1, reads, writes):
        return self.emit("dve", lambda: self.nc.vector.scalar_tensor_tensor(out, in0, scalar, in1, op0, op1), reads, writes)


class Phase:
    _n = [0]

    def __init__(self, b):
        self.b = b
        self.st = ExitStack()

    def __enter__(self):
        self.st.__enter__()
        return self

    def __exit__(self, *a):
        self.b.S.barrier()
        return self.st.__exit__(*a)

    def t(self, name, shape, dt):
        Phase._n[0] += 1
        return self.st.enter_context(self.b.nc.sbuf_tensor(f"{name}_{Phase._n[0]}", shape, dt))

    def rot(self, name, n, shape, dt):
        return Rot([self.t(f"{name}{i}", shape, dt) for i in range(n)])


class Builder:
    def __init__(self, L, NSEQ, dbg=()):
        self.L = L
        self.NSEQ = NSEQ
        self.dbg = set(dbg)
        self.nc = bass.Bass("TRN2", target_bir_lowering=False)
        self.inp = {}
        self.scr = {}
        self.stack = ExitStack()
        self.flags = ()

    def din(self, name, shape, dt=F32):
        self.inp[name] = self.nc.dram_tensor(name, list(shape), dt, kind="ExternalInput").ap()
        return self.inp[name]

    def dscr(self, name, shape, dt, out=False):
        kind = "ExternalOutput" if (out or name in self.dbg) else "Internal"
        self.scr[name] = self.nc.dram_tensor(name, list(shape), dt, kind=kind).ap()
        return self.scr[name]

    def setup_consts(self):
        S, nc = self.S, self.nc
        st = self.stack
        self.ident = st.enter_context(nc.sbuf_tensor("ident", [128, 128], F32))
        S.dma("sp", self.ident[:], self.inp["c_ident"], [], [self.ident])
        self.identb = st.enter_context(nc.sbuf_tensor("identb", [128, 128], BF16))
        S.copy("dve", self.identb[:], self.ident[:], [self.ident], [self.identb])
        self.ones = st.enter_context(nc.sbuf_tensor("ones", [128, 128], BF16))
        S.emit("dve", lambda: nc.vector.memset(self.ones[:], 1.0), [], [self.ones])
        self.epsb = st.enter_context(nc.sbuf_tensor("epsb", [128, 1], F32))
        S.emit("dve", lambda: nc.vector.memset(self.epsb[:], EPS), [], [self.epsb])
        self.onecol = st.enter_context(nc.sbuf_tensor("onecol", [128, 1], F32))
        S.emit("dve", lambda: nc.vector.memset(self.onecol[:], 1.0), [], [self.onecol])
        S.barrier()

    def rstd_from_ss(self, out, ss, n, tmp, reads, writes):
        S = self.S
        p = ss.shape[0]
        S.act(tmp, ss, AF.Ln, reads, writes, scale=1.0 / n, bias=self.epsb[:p, :])
        S.act(out, tmp, AF.Exp, writes, writes, scale=-0.5)

    def norm_tile(self, ph, xt, xkey, i, gbc, hT, hkey, tmps):
        S, nc = self.S, self.nc
        junk, strot, hnrot = tmps
        s = strot.next()
        S.act(junk[:], xt, AF.Square, [xkey], [junk, s], accum_out=s[:, 0:1])
        self.rstd_from_ss(s[:, 2:3], s[:, 0:1], D, s[:, 1:2], [s], [s])
        hn = hnrot.next()
        S.stt(hn[:], xt, s[:, 2:3], gbc[:], ALU.mult, ALU.mult, [xkey, s, gbc], [hn])
        for g in range(4):
            ps = S.psum_next()
            for j in range(4):
                kc = g * 4 + j
                S.tr(ps[:, j * 128:(j + 1) * 128], hn[:, kc * 128:(kc + 1) * 128], self.ident[:], [hn], [ps])
            S.copy("act" if g % 2 == 0 else "dve", hT[:, g * 4:(g + 1) * 4, i * 128:(i + 1) * 128],
                   ps[:, :].rearrange("p (j t) -> p j t", j=4), [ps], [(hkey, i)])

    def norm_tmps(self, ph):
        return (ph.t("junk", [128, D], BF16), ph.rot("st", 2, [128, 4], F32), ph.rot("hn", 2, [128, D], F32))

    def load_gain(self, ph, row_ap):
        g = ph.t("gbc", [128, D], F32)
        self.S.dma("sp", g[:], row_ap.partition_broadcast(128), [], [g])
        return g

    def lin_fm(self, ph, W, blocks, actT, KC, ntt, akeys, epi, wbufs=None):
        S = self.S
        if wbufs is None:
            wbufs = ph.rot("wfm", 3, [128, KC, 128], BF16)
            if any(m < 128 for _, m in blocks):
                for w_ in wbufs.items:
                    self.zero(w_)
        Wv = W.rearrange("(kc p) n -> p kc n", p=128)
        for bi, (c0, m) in enumerate(blocks):
            wb = wbufs.next()
            S.dma("pool", wb[:, :, :m], Wv[:, :, c0:c0 + m], [], [wb])
            for tt in range(ntt):
                ps = S.psum_next()
                for kc in range(KC):
                    S.mm(ps[:, :], wb[:, kc, :], actT[:, kc, tt * 512:(tt + 1) * 512], kc == 0, kc == KC - 1,
                         [wb] + akeys(tt), [ps])
                epi(bi, tt, ps)

    def lin_tm(self, ph, W, n0, ncols, actT, KC, ntok, akeys, epi, wbufs=None, G=8, wload=None):
        S = self.S
        if wbufs is None:
            wbufs = ph.rot("wtm", 3, [128, G, 512], BF16)
        Wv = W.rearrange("(kc p) n -> p kc n", p=128) if wload is None else None
        banks = [S.psum_next() for _ in range(ntok)]
        ng = (KC + G - 1) // G
        for g in range(ng):
            k0 = g * G
            kn = min(G, KC - k0)
            wg = wbufs.next()
            if wload is None:
                S.dma("pool", wg[:, :kn, :ncols], Wv[:, k0:k0 + kn, n0:n0 + ncols], [], [wg])
                extra = []
            else:
                extra = wload(wg, k0, kn)
            for tt in range(ntok):
                for j in range(kn):
                    kc = k0 + j
                    S.mm(banks[tt][:, :ncols], actT[:, kc, tt * 128:(tt + 1) * 128], wg[:, j, :ncols],
                         kc == 0, kc == KC - 1, [wg] + extra + akeys(tt), [banks[tt]])
        for tt in range(ntok):
            epi(tt, banks[tt])

    def memkv(self, layer, s):
        S, nc = self.S, self.nc
        inp = self.inp
        with Phase(self) as ph:
            gbc = self.load_gain(ph, inp["norm_memtok"][layer, :])
            tmps = self.norm_tmps(ph)
            mT = ph.t("mT", [128, 16, 512], BF16)
            xts = ph.rot("xt", 2, [128, D], F32)
            for i in range(2):
                xt = xts.next()
                S.dma("sp", xt[:], inp["mem"][s, i * 128:(i + 1) * 128, :], [], [xt])
                self.norm_tile(ph, xt[:], xt, i, gbc, mT, "mT", tmps)
            akeys = lambda tt: [("mT", 0), ("mT", 1)]
            gk = inp["memg"][layer]
            gcol = ph.t("gcol", [128, 2], F32)
            S.dma("sp", gcol[:], gk, [], [gcol])
            sq = ph.rot("sq", 2, [128, 256], BF16)
            rs = ph.rot("rs", 2, [128, 256], F32)
            tmp = ph.rot("tmp", 2, [128, 256], F32)
            ko = ph.rot("ko", 2, [128, 256], BF16)
            wkv = inp["mem_w_kv"][layer]

            def epi_k(bi, tt, ps):
                q = sq.next()
                S.act(q[:], ps[:, :256], AF.Square, [ps], [q])
                ps2 = S.psum_next()
                S.mm(ps2[:, :256], self.ones[:], q[:], True, True, [q, self.ones], [ps2])
                r, t_ = rs.next(), tmp.next()
                self.rstd_from_ss(r[:], ps2[:, :256], 128, t_[:], [ps2], [r, t_])
                o = ko.next()
                S.stt(o[:], ps[:, :256], gcol[:, 1:2], r[:], ALU.mult, ALU.mult, [ps, r, gcol], [o])
                S.dma("sp", self.scr["MK"][s, bi * 128:(bi + 1) * 128, :], o[:], [o], [])

            wb = ph.rot("wfm", 3, [128, 16, 128], BF16)
            Wv = wkv.rearrange("(kc p) n -> p kc n", p=128)
            for h in range(4):
                w_ = wb.next()
                S.dma("pool", w_[:], Wv[:, :, h * 128:(h + 1) * 128], [], [w_])
                ps = S.psum_next()
                for kc in range(16):
                    S.mm(ps[:, :256], w_[:, kc, :], mT[:, kc, 0:256], kc == 0, kc == 15, [w_] + akeys(0), [ps])
                epi_k(h, 0, ps)
            vo = ph.rot("vo", 2, [128, 512], BF16)

            def epi_v(tt, ps):
                o = vo.next()
                S.copy("act", o[:], ps[:, :], [ps], [o])
                S.dma("sp", self.scr["MV"][s, tt * 128:(tt + 1) * 128, :], o[:], [o], [])

            self.lin_tm(ph, wkv, 512, 512, mT, 16, 2, akeys, epi_v)

    def tail_block(self, layer, s, tb, x_src, x_dst, oT_src, Ko, w_out, parts=("mix", "mem", "mlp")):
        S, nc = self.S, self.nc
        inp = self.inp
        with Phase(self) as ph:
            xres = ph.t("xres", [128, 4, D], F32)
            for i in range(4):
                S.dma("sp", xres[:, i, :], x_src[s, tb * 512 + i * 128: tb * 512 + (i + 1) * 128, :], [], [("xres", i)])

            def epi_res(nb):
                def f(tt, ps):
                    sl = xres[:, tt, nb * 512:(nb + 1) * 512]
                    S.tt("dve", sl, ps[:, :], sl, ALU.add, [ps, ("xres", tt)], [("xres", tt)])
                return f

            if "mix" in parts and oT_src is not None:
                with Phase(self) as p2:
                    KCo = Ko // 128
                    oT = p2.t("oT", [128, KCo, 512], BF16)
                    S.dma("sp", oT[:], oT_src.rearrange("(kc p) t -> p kc t", p=128)[:, :, tb * 512:(tb + 1) * 512], [], [oT])
                    wb = p2.rot("wtm", 3, [128, 8, 512], BF16)
                    for nb in range(4):
                        self.lin_tm(p2, w_out, nb * 512, 512, oT, KCo, 4, lambda tt: [oT], epi_res(nb), wbufs=wb)
            if "mem" in parts:
                with Phase(self) as p2:
                    gbc = self.load_gain(p2, inp["norm_mem"][layer, :])
                    tmps = self.norm_tmps(p2)
                    hT = p2.t("hT", [128, 16, 512], BF16)
                    for i in range(4):
                        self.norm_tile(p2, xres[:, i, :], ("xres", i), i, gbc, hT, "hT", tmps)
                    hkeys = lambda tt: [("hT", i) for i in range(4)]
                    gcol = p2.t("gcol", [128, 2], F32)
                    S.dma("sp", gcol[:], inp["memg"][layer], [], [gcol])
                    mk = p2.t("mk", [128, 4, 256], BF16)
                    S.dma("sp", mk[:], self.scr["MK"][s].rearrange("(h p) m -> p h m", p=128), [], [mk])
                    mv = p2.t("mv", [128, 2, 512], BF16)
                    S.dma("sp", mv[:], self.scr["MV"][s].rearrange("(c p) e -> p c e", p=128), [], [mv])
                    qT = p2.t("qT", [128, 4, 512], BF16)
                    sq = p2.rot("sq", 2, [128, 512], BF16)
                    rs = p2.rot("rs", 2, [128, 512], F32)
                    tmp = p2.rot("tmp", 2, [128, 512], F32)

                    def epi_q(bi, tt, ps):
                        q = sq.next()
                        S.act(q[:], ps[:, :], AF.Square, [ps], [q])
                        ps2 = S.psum_next()
                        S.mm(ps2[:, :], self.ones[:], q[:], True, True, [q, self.ones], [ps2])
                        r, t_ = rs.next(), tmp.next()
                        self.rstd_from_ss(r[:], ps2[:, :], 128, t_[:], [ps2], [r, t_])
                        S.stt(qT[:, bi, :], ps[:, :], gcol[:, 0:1], r[:], ALU.mult, ALU.mult, [ps, r, gcol], [("qT", bi)])

                    self.lin_fm(p2, inp["mem_w_q"][layer], [(h * 128, 128) for h in range(4)], hT, 16, 1, hkeys, epi_q)
                    oT = p2.t("oTm", [128, 4, 512], BF16)
                    pT = p2.rot("pT", 4, [128, 512], BF16)
                    rden = p2.rot("rden", 2, [128, 512], F32)
                    for h in range(4):
                        pts = []
                        for mc in range(2):
                            ps = S.psum_next()
                            S.mm(ps[:, :], mk[:, h, mc * 128:(mc + 1) * 128], qT[:, h, :], True, True, [mk, ("qT", h)], [ps])
                            p_ = pT.next()
                            S.act(p_[:], ps[:, :], AF.Exp, [ps], [p_], scale=128 ** -0.5)
                            pts.append(p_)
                        pso = S.psum_next()
                        psd = S.psum_next()
                        for mc in range(2):
                            S.mm(pso[:, :], mv[:, mc, h * 128:(h + 1) * 128], pts[mc][:], mc == 0, mc == 1, [mv, pts[mc]], [pso])
                        for mc in range(2):
                            S.mm(psd[:, :], self.ones[:], pts[mc][:], mc == 0, mc == 1, [self.ones, pts[mc]], [psd])
                        rd = rden.next()
                        S.emit("dve", lambda: nc.vector.reciprocal(rd[:], psd[:, :]), [psd], [rd])
                        S.tt("dve", oT[:, h, :], pso[:, :], rd[:], ALU.mult, [pso, rd], [("oTm", h)])
                    wb = p2.rot("wtm", 3, [128, 8, 512], BF16)
                    okeys = lambda tt: [("oTm", h) for h in range(4)]
                    for nb in range(4):
                        self.lin_tm(p2, inp["mem_w_out"][layer], nb * 512, 512, oT, 4, 4, okeys, epi_res(nb), wbufs=wb)
            if "mlp" in parts:
                with Phase(self) as p2:
                    gbc = self.load_gain(p2, inp["norm_mlp"][layer, :])
                    aT = p2.t("aT", [128, 64, 512], BF16)
                    hT = p2.t("hT", [128, 16, 512], BF16)
                    with Phase(self) as p3:
                        tmps = self.norm_tmps(p3)
                        for i in range(4):
                            self.norm_tile(p3, xres[:, i, :], ("xres", i), i, gbc, hT, "hT", tmps)
                    hkeys = lambda tt: [("hT", i) for i in range(4)]
                    rl = p2.rot("rl", 3, [128, 512], F32)

                    def epi_a(bi, tt, ps):
                        r = rl.next()
                        S.act(r[:], ps[:, :], AF.Relu, [ps], [r])
                        S.tt("dve", aT[:, bi, :], r[:], r[:], ALU.mult, [r], [("aT", bi)])

                    self.lin_fm(p2, inp["mlp_w1"][layer], [(j * 128, 128) for j in range(64)], hT, 16, 1, hkeys, epi_a)
                    wb = p2.rot("wtm", 3, [128, 8, 512], BF16)
                    for nb in range(4):
                        self.lin_tm(p2, inp["mlp_w2"][layer], nb * 512, 512, aT, 64, 4,
                                    lambda tt: [("aT", j) for j in range(64)] if tt == 0 else [], epi_res(nb), wbufs=wb)
            for i in range(4):
                S.dma("sp", x_dst[s, tb * 512 + i * 128: tb * 512 + (i + 1) * 128, :], xres[:, i, :], [("xres", i)], [])


    def attn_core(self, ph, kq, V, vkey, nsc, ntt, scale, store):
        S, nc = self.S, self.nc
        if not hasattr(ph, "abufs"):
            ph.abufs = (ph.rot("pT", 4, [128, 512], BF16), ph.rot("rden", 2, [128, 512], F32), ph.rot("ot", 3, [128, 512], BF16))
        pT, rden, ot = ph.abufs
        acc = Rot([(S.psum[4], S.psum[5]), (S.psum[6], S.psum[7])])
        sbank = Rot([S.psum[i] for i in range(4)])
        for tt in range(ntt):
            pso, psd = acc.next()
            for sc in range(nsc):
                ps = sbank.next()
                for pi, (kf, qf, keys) in enumerate(kq):
                    S.mm(ps[:, :], kf(sc), qf(tt), pi == 0, pi == len(kq) - 1, keys, [ps])
                p_ = pT.next()
                S.act(p_[:], ps[:, :], AF.Exp, [ps], [p_], scale=scale)
                S.mm(pso[:, :], V(sc), p_[:], sc == 0, sc == nsc - 1, [p_, vkey], [pso])
                S.mm(psd[:, :], self.ones[:], p_[:], sc == 0, sc == nsc - 1, [p_, self.ones], [psd])
            rd = rden.next()
            S.emit("dve", lambda: nc.vector.reciprocal(rd[:], psd[:, :]), [psd], [rd])
            o = ot.next()
            S.tt("dve", o[:], pso[:, :], rd[:], ALU.mult, [pso, rd], [o])
            store(tt, o)

    def load_x_norm(self, ph, x_src, s, gain_row, hT, hkey, L):
        S = self.S
        gbc = self.load_gain(ph, gain_row)
        with Phase(self) as p3:
            tmps = self.norm_tmps(p3)
            xts = p3.rot("xt", 2, [128, D], F32)
            for i in range(L // 128):
                xt = xts.next()
                S.dma("sp", xt[:], x_src[s, i * 128:(i + 1) * 128, :], [], [xt])
                self.norm_tile(p3, xt[:], xt, i, gbc, hT, hkey, tmps)

    def qk_rope_epi(self, ph, cosT, sinT, perm, P):
        S = self.S
        sq = ph.rot("sq", 2, [128, 512], BF16)
        rs = ph.rot("rs", 2, [128, 512], F32)
        tmp = ph.rot("tmp", 2, [128, 512], F32)
        qg = ph.rot("qg", 2, [128, 512], BF16)
        if P < 128:
            for q_ in qg.items:
                self.zero(q_)
        t1 = ph.rot("t1", 2, [128, 512], F32)
        t2 = ph.rot("t2", 2, [128, 512], F32)

        def f(ps, gcol, gkey, tt, out_ap, okey, rstd=None, pskey=None):
            pk = pskey if pskey is not None else ps
            if rstd is None:
                assert P == 128
                q = sq.next()
                S.act(q[:P, :], ps[:P, :], AF.Square, [ps], [q])
                ps2 = S.psum_next()
                S.mm(ps2[:, :], self.ones[:P, :], q[:P, :], True, True, [q, self.ones], [ps2])
                r, t_ = rs.next(), tmp.next()
                self.rstd_from_ss(r[:], ps2[:, :], P, t_[:], [ps2], [r, t_])
                rstd = r
            g = qg.next()
            S.act(g[:P, :], ps[:P, :], AF.Copy, [pk, gkey], [g], scale=gcol)
            ps3 = S.psum_next()
            S.mm(ps3[:, :], perm[:, :], g[:, :], True, True, [g, perm], [ps3])
            a, b_ = t1.next(), t2.next()
            S.tt("dve", a[:P, :], g[:P, :], cosT[:P, tt * 512:(tt + 1) * 512], ALU.mult, [g, cosT], [a])
            S.tt("dve", b_[:P, :], ps3[:P, :], sinT[:P, tt * 512:(tt + 1) * 512], ALU.mult, [ps3, sinT], [b_])
            S.tt("dve", a[:P, :], a[:P, :], b_[:P, :], ALU.add, [a, b_], [a])
            if rstd == "none":
                S.copy("act", out_ap, a[:P, :], [a], [okey])
            else:
                S.tt("dve", out_ap, a[:P, :], rstd[:P, :], ALU.mult, [a, rstd], [okey])
            return rstd
        return f

    def zero(self, t):
        self.S.emit("pool", lambda: self.nc.gpsimd.memset(t[:], 0.0), [], [t])

    def load_perm(self, ph, name, P):
        S = self.S
        pf = ph.t("permf", [128, 128], F32)
        self.zero(pf)
        S.dma("sp", pf[:P, :P], self.inp[name], [], [pf])
        pb = ph.t("permb", [128, 128], BF16)
        S.copy("dve", pb[:], pf[:], [pf], [pb])
        return pb

    def gqa_head(self, layer, s, x_src):
        S, nc, L, inp = self.S, self.nc, self.L, self.inp
        ntt = L // 512
        w_in = inp["gqa_w_in"][0]
        with Phase(self) as ph:
            hT = ph.t("hT", [128, 16, L], BF16)
            self.load_x_norm(ph, x_src, s, inp["norm_mix"][layer, :], hT, "hT", L)
            hk = lambda tt: [("hT", tt * 4 + i) for i in range(4)]
            cosT = ph.t("cosT", [128, L], F32)
            sinT = ph.t("sinT", [128, L], F32)
            S.dma("sp", cosT[:], inp["c_gqa_cos"], [], [cosT])
            S.dma("sp", sinT[:], inp["c_gqa_sin"], [], [sinT])
            perm = self.load_perm(ph, "c_perm128", 128)
            gcol = ph.t("gcol", [128, 2], F32)
            S.dma("sp", gcol[:], inp["gqag"], [], [gcol])
            epi = self.qk_rope_epi(ph, cosT, sinT, perm, 128)
            ob = ph.rot("ob", 3, [128, 512], BF16)

            def epi_qk(bi, tt, ps):
                isq = bi < 16
                o = ob.next()
                epi(ps, gcol[:, 0:1] if isq else gcol[:, 1:2], gcol, tt, o[:], o)
                dst = self.scr["QT"][bi * 128:(bi + 1) * 128, :] if isq else self.scr["KT"][(bi - 16) * 128:(bi - 15) * 128, :]
                S.dma("sp", dst[:, tt * 512:(tt + 1) * 512], o[:], [o], [])

            if "noqk" not in self.flags:
                self.lin_fm(ph, w_in, [(h * 128, 128) for h in range(20)], hT, 16, ntt, hk, epi_qk)
            vo = ph.rot("vo", 3, [128, 512], BF16)
            wbv = ph.rot("wtm", 3, [128, 8, 512], BF16)
            for t4 in range(L // 512 if "nov" not in self.flags else 0):
                def epi_v(tt, ps, t4=t4):
                    o = vo.next()
                    S.copy("act", o[:], ps[:, :], [ps], [o])
                    r0 = t4 * 512 + tt * 128
                    S.dma("sp", self.scr["VV"][r0:r0 + 128, 0:512], o[:], [o], [])
                self.lin_tm(ph, w_in, 2560, 512, hT[:, :, t4 * 512:(t4 + 1) * 512], 16, 4,
                            lambda tt, t4=t4: [("hT", t4 * 4 + tt)], epi_v, wbufs=wbv)

    def gqa_mix(self, s):
        S, nc, L = self.S, self.nc, self.L
        nsc, ntt = L // 128, L // 512
        with Phase(self) as ph:
            kTs = ph.rot("kT", 2, [128, L], BF16)
            vs = ph.rot("v", 2, [128, nsc, 128], BF16)
            qTs = ph.rot("qT", 2, [128, L], BF16)
            for g in range(4):
                kT, v = kTs.next(), vs.next()
                S.dma("sp", kT[:], self.scr["KT"][g * 128:(g + 1) * 128, :], [], [kT])
                S.dma("sp", v[:], self.scr["VV"][:, g * 128:(g + 1) * 128].rearrange("(c p) e -> p c e", p=128), [], [v])
                for hh in range(4):
                    h = g * 4 + hh
                    qT = qTs.next()
                    S.dma("sp", qT[:], self.scr["QT"][h * 128:(h + 1) * 128, :], [], [qT])

                    def store(tt, o, h=h):
                        S.dma("sp", self.scr["OT"][h * 128:(h + 1) * 128, tt * 512:(tt + 1) * 512], o[:], [o], [])

                    kq = [(lambda sc, kT=kT: kT[:, sc * 128:(sc + 1) * 128], lambda tt, qT=qT: qT[:, tt * 512:(tt + 1) * 512], [kT, qT])]
                    self.attn_core(ph, kq, lambda sc, v=v: v[:, sc, :], v, nsc, ntt, 128 ** -0.5, store)


    def mla_head(self, layer, s, x_src):
        S, nc, L, inp = self.S, self.nc, self.L, self.inp
        ntt = L // 512
        with Phase(self) as ph:
            craw = ph.t("craw", [128, 8, L], F32)
            krraw = ph.t("krraw", [64, L], F32)
            with Phase(self) as pa:
                hT = pa.t("hT", [128, 16, L], BF16)
                self.load_x_norm(pa, x_src, s, inp["norm_mix"][layer, :], hT, "hT", L)
                hk = lambda tt: [("hT", tt * 4 + i) for i in range(4)]

                def epi_c(bi, tt, ps):
                    if bi < 8:
                        S.copy("act" if (bi + tt) % 2 else "dve", craw[:, bi, tt * 512:(tt + 1) * 512], ps[:, :], [ps], [("craw", bi, tt)])
                    else:
                        S.copy("act", krraw[:, tt * 512:(tt + 1) * 512], ps[:64, :], [ps], [("krraw", tt)])

                self.lin_fm(pa, inp["mla_w_in"][0], [(j * 128, 128) for j in range(8)] + [(1024, 64)], hT, 16, ntt, hk, epi_c)
            cn = ph.t("cn", [128, 8, L], BF16)
            lg = ph.t("lg", [128, 8], F32)
            S.dma("sp", lg[:], inp["mlalat"], [], [lg])
            mg = ph.t("mg", [128, 4], F32)
            S.dma("sp", mg[:], inp["mlag"], [], [mg])
            sq = ph.rot("sq", 3, [128, 512], BF16)
            rs = ph.rot("rs", 2, [128, 512], F32)
            tmp = ph.rot("tmp", 2, [128, 512], F32)
            for half in range(2):
                for tt in range(ntt):
                    ps2 = S.psum_next()
                    for j in range(4):
                        q = sq.next()
                        c = half * 4 + j
                        S.act(q[:], craw[:, c, tt * 512:(tt + 1) * 512], AF.Square, [("craw", c, tt)], [q])
                        S.mm(ps2[:, :], self.ones[:], q[:], j == 0, j == 3, [q, self.ones], [ps2])
                    r, t_ = rs.next(), tmp.next()
                    self.rstd_from_ss(r[:], ps2[:, :], 512, t_[:], [ps2], [r, t_])
                    for j in range(4):
                        c = half * 4 + j
                        S.stt(cn[:, c, tt * 512:(tt + 1) * 512], craw[:, c, tt * 512:(tt + 1) * 512], lg[:, c:c + 1], r[:],
                              ALU.mult, ALU.mult, [("craw", c, tt), lg, r], [("cn", c, tt)])
            cosT = ph.t("cosT", [64, L], F32)
            sinT = ph.t("sinT", [64, L], F32)
            S.dma("sp", cosT[:], inp["c_mla_cos"], [], [cosT])
            S.dma("sp", sinT[:], inp["c_mla_sin"], [], [sinT])
            perm = self.load_perm(ph, "c_perm64", 64)
            epi = self.qk_rope_epi(ph, cosT, sinT, perm, 64)
            Rk = ph.t("Rk", [64, L], F32)
            sqkr = ph.t("sqkr", [128, L], BF16)
            self.zero(sqkr)
            S.barrier()
            for tt in range(ntt):
                sl = slice(tt * 512, (tt + 1) * 512)
                S.act(sqkr[:64, sl], krraw[:, sl], AF.Square, [("krraw", tt)], [("sqkr", tt)])
                epi(krraw[:, sl], mg[:64, 3:4], mg, tt, Rk[:, sl], ("Rk", tt), rstd="none", pskey=("krraw", tt))
            ob = ph.rot("ob", 3, [128, 512], BF16)
            ob2 = ph.rot("ob2", 3, [64, 512], BF16)
            wq = ph.rot("wq", 2, [128, 4, 256], BF16)
            for w_ in wq.items:
                self.zero(w_)
            Wq = inp["mla_w_qb"][0].rearrange("(kc p) n -> p kc n", p=128)
            Wkv = inp["mla_w_kvb"][0].rearrange("(kc p) n -> p kc n", p=128)

            def norm192(psn, sqr_ap, sqr_key):
                q = sq.next()
                S.act(q[:], psn[:, :], AF.Square, [psn], [q])
                ps2 = S.psum_next()
                S.mm(ps2[:, :], self.ones[:], q[:], True, False, [q, self.ones], [ps2])
                S.mm(ps2[:, :], self.ones[:, :], sqr_ap, False, True, [sqr_key, self.ones], [ps2])
                r, t_ = rs.next(), tmp.next()
                self.rstd_from_ss(r[:], ps2[:, :], 192, t_[:], [ps2], [r, t_])
                return r

            sqr = ph.rot("sqr", 2, [128, 512], BF16)
            for q_ in sqr.items:
                self.zero(q_)
            for h in range(16):
                w_ = wq.next()
                S.dma("pool", w_[:, :, 0:192], Wq[:, :, h * 192:(h + 1) * 192], [], [w_])
                for tt in range(ntt):
                    sl = slice(tt * 512, (tt + 1) * 512)
                    ck = [("cn", j, tt) for j in range(4)]
                    psn, psr = S.psum_next(), S.psum_next()
                    for j in range(4):
                        S.mm(psn[:, :], w_[:, j, 0:128], cn[:, j, sl], j == 0, j == 3, [w_] + ck, [psn])
                    for j in range(4):
                        S.mm(psr[:, :], w_[:, j, 128:256], cn[:, j, sl], j == 0, j == 3, [w_] + ck, [psr])
                    q2 = sqr.next()
                    S.act(q2[:64, :], psr[:64, :], AF.Square, [psr], [q2])
                    r = norm192(psn, q2[:], q2)
                    o = ob.next()
                    S.stt(o[:], psn[:, :], mg[:, 0:1], r[:], ALU.mult, ALU.mult, [psn, mg, r], [o])
                    S.dma("sp", self.scr["QT"][h * 192:h * 192 + 128, sl], o[:], [o], [])
                    o2 = ob2.next()
                    epi(psr[:64, :], mg[:64, 1:2], mg, tt, o2[:], o2, rstd=r, pskey=psr)
                    S.dma("sp", self.scr["QT"][h * 192 + 128:(h + 1) * 192, sl], o2[:], [o2], [])
            wk = ph.rot("wk", 2, [128, 4, 128], BF16)
            for h in range(16):
                w_ = wk.next()
                S.dma("pool", w_[:], Wkv[:, 4:8, h * 256:h * 256 + 128] if False else
                      inp["mla_w_kvb"][0].rearrange("(kc p) n -> p kc n", p=128)[:, :, h * 256:h * 256 + 128], [], [w_])
                for tt in range(ntt):
                    sl = slice(tt * 512, (tt + 1) * 512)
                    ck = [("cn", 4 + j, tt) for j in range(4)]
                    psn = S.psum_next()
                    for j in range(4):
                        S.mm(psn[:, :], w_[:, j, :], cn[:, 4 + j, sl], j == 0, j == 3, [w_] + ck, [psn])
                    r = norm192(psn, sqkr[:, sl], ("sqkr", tt))
                    o = ob.next()
                    S.stt(o[:], psn[:, :], mg[:, 2:3], r[:], ALU.mult, ALU.mult, [psn, mg, r], [o])
                    S.dma("sp", self.scr["KT"][h * 128:(h + 1) * 128, sl], o[:], [o], [])
                    o2 = ob2.next()
                    S.tt("dve", o2[:], Rk[:, sl], r[:64, :], ALU.mult, [("Rk", tt), r], [o2])
                    S.dma("sp", self.scr["KR"][h * 64:(h + 1) * 64, sl], o2[:], [o2], [])
            vo = ph.rot("vo", 3, [128, 512], BF16)
            Wv5 = inp["mla_w_kvb"][0].rearrange("(kc p) (h two e) -> p kc h two e", p=128, two=2, e=128)
            wbv = ph.rot("wtm", 3, [128, 8, 512], BF16)
            for nb in range(4):
                def wload(wg, k0, kn, nb=nb):
                    ks = []
                    for k in range(4):
                        key = ("wgk", wg.name, k)
                        S.dma("pool", wg[:, k, :].rearrange("p (h e) -> p h e", e=128), Wv5[:, k, nb * 4:(nb + 1) * 4, 1, :], [], [wg, key])
                        ks.append(key)
                    return ks
                for t4 in range(L // 512):
                    def epi_v(tt, ps, t4=t4, nb=nb):
                        o = vo.next()
                        S.copy("act", o[:], ps[:, :], [ps], [o])
                        r0 = t4 * 512 + tt * 128
                        S.dma("sp", self.scr["VV"][r0:r0 + 128, nb * 512:(nb + 1) * 512], o[:], [o], [])
                    self.lin_tm(ph, None, 0, 512, cn[:, 4:8, t4 * 512:(t4 + 1) * 512], 4, 4,
                                lambda tt, t4=t4: [("cn", 4 + j, t4) for j in range(4)], epi_v, wload=wload, wbufs=wbv)

    def mla_mix(self, s):
        S, nc, L = self.S, self.nc, self.L
        nsc, ntt = L // 128, L // 512
        with Phase(self) as ph:
            kTs = ph.rot("kT", 2, [128, L], BF16)
            kRs = ph.rot("kR", 2, [128, L], BF16)
            vs = ph.rot("v", 2, [128, nsc, 128], BF16)
            qTs = ph.rot("qT", 2, [128, L], BF16)
            qRs = ph.rot("qR", 2, [128, L], BF16)
            for t_ in kRs.items + qRs.items:
                self.zero(t_)
            S.barrier()
            for h in range(16):
                kT, kR, v, qT, qR = kTs.next(), kRs.next(), vs.next(), qTs.next(), qRs.next()
                S.dma("sp", kT[:], self.scr["KT"][h * 128:(h + 1) * 128, :], [], [kT])
                S.dma("sp", kR[:64, :], self.scr["KR"][h * 64:(h + 1) * 64, :], [], [kR])
                S.dma("sp", v[:], self.scr["VV"][:, h * 128:(h + 1) * 128].rearrange("(c p) e -> p c e", p=128), [], [v])
                S.dma("sp", qT[:], self.scr["QT"][h * 192:h * 192 + 128, :], [], [qT])
                S.dma("sp", qR[:64, :], self.scr["QT"][h * 192 + 128:(h + 1) * 192, :], [], [qR])

                def store(tt, o, h=h):
                    S.dma("sp", self.scr["OT"][h * 128:(h + 1) * 128, tt * 512:(tt + 1) * 512], o[:], [o], [])

                kq = [(lambda sc, kT=kT: kT[:, sc * 128:(sc + 1) * 128], lambda tt, qT=qT: qT[:, tt * 512:(tt + 1) * 512], [kT, qT]),
                      (lambda sc, kR=kR: kR[:, sc * 128:(sc + 1) * 128], lambda tt, qR=qR: qR[:, tt * 512:(tt + 1) * 512], [kR, qR])]
                self.attn_core(ph, kq, lambda sc, v=v: v[:, sc, :], v, nsc, ntt, 192 ** -0.5, store)


    def ret_head(self, layer, s, x_src):
        S, nc, L, inp = self.S, self.nc, self.L, self.inp
        ntt = L // 512
        w_in = inp["ret_w_in"][0]
        Wv = w_in.rearrange("(kc p) n -> p kc n", p=128)
        with Phase(self) as ph:
            hT = ph.t("hT", [128, 16, L], BF16)
            self.load_x_norm(ph, x_src, s, inp["norm_mix"][layer, :], hT, "hT", L)
            hk = lambda tt: [("hT", tt * 4 + i) for i in range(4)]
            with Phase(self) as p2:
                cosT = p2.t("cosT", [128, L], F32)
                sinT = p2.t("sinT", [128, L], F32)
                S.dma("sp", cosT[:], inp["c_ret_cos"], [], [cosT])
                S.dma("sp", sinT[:], inp["c_ret_sin"], [], [sinT])
                wq = p2.rot("wq", 2, [128, 16, 256], BF16)
                t1 = p2.rot("t1", 2, [128, 512], F32)
                t2 = p2.rot("t2", 2, [128, 512], F32)
                ob = p2.rot("ob", 4, [128, 512], BF16)
                for which in range(2):
                    dst = self.scr["QT"] if which == 0 else self.scr["KT"]
                    for h in range(8):
                        w_ = wq.next()
                        c0 = which * 2048 + h * 256
                        S.dma("pool", w_[:], Wv[:, :, c0:c0 + 256], [], [w_])
                        for tt in range(ntt):
                            sl = slice(tt * 512, (tt + 1) * 512)
                            psa, psb = S.psum_next(), S.psum_next()
                            for kc in range(16):
                                S.mm(psa[:, :], w_[:, kc, 0:128], hT[:, kc, sl], kc == 0, kc == 15, [w_] + hk(tt), [psa])
                            for kc in range(16):
                                S.mm(psb[:, :], w_[:, kc, 128:256], hT[:, kc, sl], kc == 0, kc == 15, [w_] + hk(tt), [psb])
                            a, b_ = t1.next(), t2.next()
                            S.tt("dve", a[:], psa[:, :], cosT[:, sl], ALU.mult, [psa, cosT], [a])
                            S.tt("dve", b_[:], psb[:, :], sinT[:, sl], ALU.mult, [psb, sinT], [b_])
                            o1 = ob.next()
                            S.tt("dve", o1[:], a[:], b_[:], ALU.subtract, [a, b_], [o1])
                            S.dma("sp", dst[h * 256:h * 256 + 128, sl], o1[:], [o1], [])
                            a, b_ = t1.next(), t2.next()
                            S.tt("dve", a[:], psa[:, :], sinT[:, sl], ALU.mult, [psa, sinT], [a])
                            S.tt("dve", b_[:], psb[:, :], cosT[:, sl], ALU.mult, [psb, cosT], [b_])
                            o2 = ob.next()
                            S.tt("dve", o2[:], a[:], b_[:], ALU.add, [a, b_], [o2])
                            S.dma("sp", dst[h * 256 + 128:(h + 1) * 256, sl], o2[:], [o2], [])
            with Phase(self) as p2:
                go = p2.rot("go", 3, [128, 512], BF16)

                def epi_g(bi, tt, ps):
                    o = go.next()
                    S.act(o[:], ps[:, :], AF.Silu, [ps], [o])
                    S.dma("sp", self.scr["GT"][bi * 128:(bi + 1) * 128, tt * 512:(tt + 1) * 512], o[:], [o], [])

                self.lin_fm(p2, w_in, [(8192 + j * 128, 128) for j in range(32)], hT, 16, ntt, hk, epi_g)
                vo = p2.rot("vo", 3, [128, 512], BF16)
                wb = p2.rot("wtm", 3, [128, 8, 512], BF16)
                for nb in range(8):
                    for t4 in range(L // 512):
                        def epi_v(tt, ps, t4=t4, nb=nb):
                            o = vo.next()
                            S.copy("act", o[:], ps[:, :], [ps], [o])
                            r0 = t4 * 512 + tt * 128
                            S.dma("sp", self.scr["VV"][r0:r0 + 128, nb * 512:(nb + 1) * 512], o[:], [o], [])
                        self.lin_tm(p2, w_in, 4096 + nb * 512, 512, hT[:, :, t4 * 512:(t4 + 1) * 512], 16, 4,
                                    lambda tt, t4=t4: [("hT", t4 * 4 + tt)], epi_v, wbufs=wb)

    def ret_mix(self, s):
        S, nc, L, inp = self.S, self.nc, self.L, self.inp
        nsc, ntt = L // 128, L // 512
        W = 2 * L - 128
        with Phase(self) as ph:
            lgx = ph.t("lgx", [128, 16], F32)
            S.dma("sp", lgx[:], inp["ret_decay"].rearrange("a b h -> (a b h)").partition_broadcast(128), [], [lgx])
            ax = ph.t("ax", [128, 16], F32)
            S.act(ax[:], lgx[:], AF.Abs, [lgx], [ax])
            S.act(ax[:], ax[:], AF.Exp, [ax], [ax], scale=-1.0)
            S.act(ax[:], ax[:], AF.Ln, [ax], [ax], bias=self.onecol[:, :])
            lg = ph.t("lg", [128, 16], F32)
            S.ts("dve", lg[:], lgx[:], 0.0, None, ALU.min, None, [lgx], [lg])
            S.tt("dve", lg[:], lg[:], ax[:], ALU.subtract, [lg, ax], [lg])
            nlg = ph.t("nlg", [128, 16], F32)
            S.ts("dve", nlg[:], lg[:], -1.0, None, ALU.mult, None, [lg], [nlg])
            gout = ph.t("gout", [128, 4], F32)
            S.dma("sp", gout[:], inp["retg"], [], [gout])
            diff = ph.t("diff", [128, W], F32)
            S.dma("sp", diff[:], inp["c_ret_diff"], [], [diff])
            A = ph.t("A", [128, W], F32)
            E = ph.t("E", [128, W], F32)
            qTs = ph.rot("qT", 2, [128, 2, L], BF16)
            kTs = ph.rot("kT", 2, [128, 2, L], BF16)
            vs = ph.rot("v", 2, [128, nsc, 512], BF16)
            gs = ph.rot("g", 2, [128, 4, L], BF16)
            pT = ph.rot("pT", 4, [128, 512], BF16)
            sq = ph.rot("sq", 4, [128, 512], BF16)
            rs = ph.rot("rs", 2, [128, 512], F32)
            tmp = ph.rot("tmp", 2, [128, 512], F32)
            on = ph.rot("on", 2, [128, 512], F32)
            ob = ph.rot("ob", 3, [128, 512], BF16)
            sbank = Rot([S.psum[i] for i in range(4)])
            O = [S.psum[4 + i] for i in range(4)]
            for h in range(8):
                qT, kT, v, g = qTs.next(), kTs.next(), vs.next(), gs.next()
                S.dma("sp", qT[:], self.scr["QT"][h * 256:(h + 1) * 256, :].rearrange("(c p) t -> p c t", p=128), [], [qT])
                S.dma("sp", kT[:], self.scr["KT"][h * 256:(h + 1) * 256, :].rearrange("(c p) t -> p c t", p=128), [], [kT])
                S.dma("sp", v[:], self.scr["VV"][:, h * 512:(h + 1) * 512].rearrange("(c p) e -> p c e", p=128), [], [v])
                S.dma("sp", g[:], self.scr["GT"][h * 512:(h + 1) * 512, :].rearrange("(c p) t -> p c t", p=128), [], [g])
                S.act(A[:], diff[:], AF.Relu, [diff, nlg], [A], scale=nlg[:, h:h + 1])
                S.act(E[:], diff[:], AF.Relu, [diff, lg], [E], scale=lg[:, 8 + h:9 + h])
                S.tt("dve", A[:], A[:], E[:], ALU.add, [A, E], [A])
                S.act(E[:], A[:], AF.Exp, [A], [E], scale=-1.0)
                S.ts("dve", A[:], diff[:], 0.0, 1.0 / 16, ALU.is_equal, ALU.mult, [diff], [A])
                S.stt(E[:], A[:], 1.0 / 16, E[:], ALU.add, ALU.mult, [A, E], [E])
                for tt in range(ntt):
                    sl = slice(tt * 512, (tt + 1) * 512)
                    for sc in range(nsc):
                        ps = sbank.next()
                        for c in range(2):
                            S.mm(ps[:, :], kT[:, c, sc * 128:(sc + 1) * 128], qT[:, c, sl], c == 0, c == 1, [kT, qT], [ps])
                        p_ = pT.next()
                        off = tt * 512 - sc * 128 + (L - 128)
                        S.tt("dve", p_[:], ps[:, :], E[:, off:off + 512], ALU.mult, [ps, E], [p_])
                        for ec in range(4):
                            S.mm(O[ec][:, :], v[:, sc, ec * 128:(ec + 1) * 128], p_[:], sc == 0, sc == nsc - 1, [v, p_], [O[ec]])
                    ps2 = sbank.next()
                    for ec in range(4):
                        q = sq.next()
                        S.act(q[:], O[ec][:, :], AF.Square, [O[ec]], [q])
                        S.mm(ps2[:, :], self.ones[:], q[:], ec == 0, ec == 3, [q, self.ones], [ps2])
                    r, t_ = rs.next(), tmp.next()
                    self.rstd_from_ss(r[:], ps2[:, :], 512, t_[:], [ps2], [r, t_])
                    for ec in range(4):
                        n_ = on.next()
                        S.tt("dve", n_[:], O[ec][:, :], r[:], ALU.mult, [O[ec], r], [n_])
                        o = ob.next()
                        S.stt(o[:], n_[:], gout[:, ec:ec + 1], g[:, ec, sl], ALU.mult, ALU.mult, [n_, gout, g], [o])
                        S.dma("sp", self.scr["OT"][h * 512 + ec * 128:h * 512 + (ec + 1) * 128, sl], o[:], [o], [])


    def hg_head(self, layer, s, x_src):
        S, nc, L, inp = self.S, self.nc, self.L, self.inp
        ntt = L // 512
        w_in = inp["hg_w_in"][0]
        with Phase(self) as ph:
            hT = ph.t("hT", [128, 16, L], BF16)
            self.load_x_norm(ph, x_src, s, inp["norm_mix"][layer, :], hT, "hT", L)
            hk = lambda tt: [("hT", tt * 4 + i) for i in range(4)]
            lraw = ph.t("lraw", [128, 4, 16], F32)
            S.dma("sp", lraw[:], inp["hglb"], [], [lraw])
            mx = ph.t("mx", [128, 16], F32)
            S.tt("dve", mx[:], lraw[:, 0, :], lraw[:, 1, :], ALU.max, [lraw], [mx])
            S.tt("dve", mx[:], mx[:], lraw[:, 2, :], ALU.max, [lraw, mx], [mx])
            S.tt("dve", mx[:], mx[:], lraw[:, 3, :], ALU.max, [lraw, mx], [mx])
            for j in range(4):
                S.tt("dve", lraw[:, j, :], lraw[:, j, :], mx[:], ALU.subtract, [lraw, mx], [lraw])
            S.act(lraw[:], lraw[:], AF.Exp, [lraw], [lraw])
            sm = ph.t("sm", [128, 16], F32)
            S.tt("dve", sm[:], lraw[:, 0, :], lraw[:, 1, :], ALU.add, [lraw], [sm])
            S.tt("dve", sm[:], sm[:], lraw[:, 2, :], ALU.add, [lraw, sm], [sm])
            S.tt("dve", sm[:], sm[:], lraw[:, 3, :], ALU.add, [lraw, sm], [sm])
            S.emit("dve", lambda: nc.vector.reciprocal(sm[:], sm[:]), [sm], [sm])
            lb = ph.t("lb", [128, 16], F32)
            S.copy("dve", lb[:], lraw[:, 1, :], [lraw], [lb])
            for j in range(2, layer + 1):
                S.tt("dve", lb[:], lb[:], lraw[:, j, :], ALU.add, [lraw, lb], [lb])
            S.tt("dve", lb[:], lb[:], sm[:], ALU.mult, [lb, sm], [lb])
            oml = ph.t("oml", [128, 16], F32)
            S.ts("dve", oml[:], lb[:], -1.0, 1.0, ALU.mult, ALU.add, [lb], [oml])
            qo = ph.rot("qo", 3, [128, 512], BF16)
            sg = ph.rot("sg", 3, [128, 512], F32)
            lfo = ph.rot("lfo", 3, [128, 512], F32)

            def epi(bi, tt, ps):
                sl = slice(tt * 512, (tt + 1) * 512)
                if bi < 16:
                    o = qo.next()
                    S.copy("dve", o[:], ps[:, :], [ps], [o])
                    S.dma("sp", self.scr["QT"][bi * 128:(bi + 1) * 128, sl], o[:], [o], [])
                elif bi < 48:
                    h = (bi - 16) % 16
                    g_ = sg.next()
                    S.act(g_[:], ps[:, :], AF.Sigmoid, [ps], [g_])
                    o = lfo.next()
                    S.act(o[:], g_[:], AF.Ln, [g_, oml, lb], [o], scale=oml[:, h:h + 1], bias=lb[:, h:h + 1])
                    S.dma("sp", self.scr["LF"][(bi - 16) * 128:(bi - 15) * 128, sl], o[:], [o], [])
                else:
                    o = qo.next()
                    S.act(o[:], ps[:, :], AF.Silu, [ps], [o])
                    S.dma("sp", self.scr["GT"][(bi - 48) * 128:(bi - 47) * 128, sl], o[:], [o], [])

            blocks = [(j * 128, 128) for j in range(48)] + [(8192 + j * 128, 128) for j in range(16)]
            self.lin_fm(ph, w_in, blocks, hT, 16, ntt, hk, epi)
            vo = ph.rot("vo", 3, [128, 512], BF16)
            wb = ph.rot("wtm", 3, [128, 8, 512], BF16)
            for nb in range(4):
                for t4 in range(L // 512):
                    def epi_v(tt, ps, t4=t4, nb=nb):
                        o = vo.next()
                        S.copy("act", o[:], ps[:, :], [ps], [o])
                        r0 = t4 * 512 + tt * 128
                        S.dma("sp", self.scr["VV"][r0:r0 + 128, nb * 512:(nb + 1) * 512], o[:], [o], [])
                    self.lin_tm(ph, w_in, 6144 + nb * 512, 512, hT[:, :, t4 * 512:(t4 + 1) * 512], 16, 4,
                                lambda tt, t4=t4: [("hT", t4 * 4 + tt)], epi_v, wbufs=wb)

    def hg_mix(self, s):
        S, nc, L, inp = self.S, self.nc, self.L, self.inp
        nsc, ntt, nch = L // 128, L // 512, L // 32
        with Phase(self) as ph:
            rmask = ph.t("rmask", [128, 4], F32)
            S.dma("sp", rmask[:], inp["c_hg_rowmask"], [], [rmask])
            reset = ph.t("reset", [128, L], F32)
            S.dma("sp", reset[:], inp["c_hg_reset"], [], [reset])
            masks = ph.t("masks", [128, 2, 128], F32)
            S.dma("sp", masks[:], inp["c_hg_masks"], [], [masks])
            gout = ph.t("gout", [128, 1], F32)
            S.dma("sp", gout[:], inp["hgg"], [], [gout])
            qT = ph.t("qT", [128, L], BF16)
            lf = [ph.t("lf0", [128, L], F32), ph.t("lf1", [128, L], F32)]
            v = ph.t("v", [128, nsc, 128], BF16)
            vm = ph.t("vm", [128, nsc, 4, 128], BF16)
            G = ph.t("G", [128, L], BF16)
            b_ = ph.t("b", [128, L], F32)
            e1 = ph.t("e1", [128, L], F32)
            e2 = ph.t("e2", [128, L], F32)
            kg = ph.t("kg", [128, L], F32)
            etot = [ph.t("etot0", [128, nch], F32), ph.t("etot1", [128, nch], F32)]
            q_t = [ph.t("qt0", [128, L], BF16), ph.t("qt1", [128, L], BF16)]
            k_t = [ph.t("kt0", [128, L], BF16), ph.t("kt1", [128, L], BF16)]
            kdT = ph.t("kdT", [128, nsc, 128], BF16)
            U = ph.t("U", [128, nch, 128], F32)
            Sall = [ph.t("Sall0", [128, nch, 128], BF16), ph.t("Sall1", [128, nch, 128], BF16)]
            pm = ph.rot("pm", 4, [128, 128], BF16)
            sq = ph.rot("sq", 2, [128, 512], BF16)
            rs = ph.rot("rs", 2, [128, 512], F32)
            tmp = ph.rot("tmp", 2, [128, 512], F32)
            on = ph.rot("on", 2, [128, 512], F32)
            ob = ph.rot("ob", 2, [128, 512], BF16)
            b3 = lambda t: t[:, :].rearrange("p (c k) -> p c k", k=32)
            obank = Rot([S.psum[6], S.psum[7]])
            sbank = Rot([S.psum[i] for i in range(4)])
            nbank = Rot([S.psum[4], S.psum[5]])
            for h in range(16):
                S.dma("sp", qT[:], self.scr["QT"][h * 128:(h + 1) * 128, :], [], [qT])
                for d in range(2):
                    S.dma("sp", lf[d][:], self.scr["LF"][d * 2048 + h * 128:d * 2048 + (h + 1) * 128, :], [], [lf[d]])
                S.dma("sp", v[:], self.scr["VV"][:, h * 128:(h + 1) * 128].rearrange("(c p) e -> p c e", p=128), [], [v])
                S.dma("sp", G[:], self.scr["GT"][h * 128:(h + 1) * 128, :], [], [G])
                for c in range(4):
                    S.act(vm[:, :, c, :], v[:, :, :], AF.Copy, [v, rmask], [vm], scale=rmask[:, c:c + 1])
                for d in range(2):
                    S.emit("dve", lambda: nc.vector.tensor_tensor_scan(b_[:], reset[:], lf[d][:], 0.0, ALU.mult, ALU.add),
                           [reset, lf[d]], [b_])
                    S.act(etot[d][:], b3(b_)[:, :, 31], AF.Exp, [b_], [etot[d]])
                    if d == 1:
                        S.tt("dve", e1[:], lf[d][:], b_[:], ALU.subtract, [lf[d], b_], [e1])
                        S.tt("dve", b3(b_), b3(e1), b3(b_)[:, :, 31:32].to_broadcast([128, nch, 32]), ALU.add, [e1, b_], [b_])
                    S.act(e1[:], b_[:], AF.Exp, [b_], [e1])
                    S.act(e2[:], b_[:], AF.Exp, [b_], [e2], scale=-1.0)
                    S.act(kg[:], lf[d][:], AF.Exp, [lf[d]], [kg])
                    S.ts("dve", kg[:], kg[:], -1.0, 1.0, ALU.mult, ALU.add, [kg], [kg])
                    S.tt("dve", q_t[d][:], qT[:], e1[:], ALU.mult, [qT, e1], [q_t[d]])
                    S.tt("dve", kg[:], kg[:], e2[:], ALU.mult, [kg, e2], [kg])
                    S.copy("act", k_t[d][:], kg[:], [kg], [k_t[d]])
                    S.tt("dve", b3(e1), b3(kg), etot[d][:, :].to_broadcast([128, nch, 1]).to_broadcast([128, nch, 32]) if False else
                         etot[d][:, :].rearrange("p (c o) -> p c o", o=1).to_broadcast([128, nch, 32]), ALU.mult, [kg, etot[d]], [e1])
                    for g in range(nsc):
                        if g % 4 == 0:
                            pst = S.psum_next()
                        S.tr(pst[:, (g % 4) * 128:(g % 4 + 1) * 128], e1[:, g * 128:(g + 1) * 128], self.ident[:], [e1], [pst])
                        if g % 4 == 3:
                            S.copy("act", kdT[:, g - 3:g + 1, :], pst[:, :].rearrange("p (j t) -> p j t", j=4), [pst], [kdT])
                    for g in range(nsc):
                        psu = S.psum_next()
                        for cc in range(4):
                            S.mm(psu[:, cc * 128:(cc + 1) * 128], kdT[:, g, :], vm[:, g, cc, :], True, True, [kdT, vm], [psu])
                        S.copy("act", U[:, g * 4:(g + 1) * 4, :], psu[:, :].rearrange("p (j t) -> p j t", j=4), [psu], [U])
                    order = range(1, nch) if d == 0 else range(nch - 2, -1, -1)
                    for c in order:
                        pv = c - 1 if d == 0 else c + 1
                        S.stt(U[:, c, :], U[:, pv, :], etot[d][:, c:c + 1], U[:, c, :], ALU.mult, ALU.add, [U, etot[d]], [U])
                    for q4 in range(4):
                        n4 = nch // 4
                        S.copy("act" if q4 % 2 else "dve", Sall[d][:, q4 * n4:(q4 + 1) * n4, :], U[:, q4 * n4:(q4 + 1) * n4, :], [U], [Sall[d]])
                for tt in range(ntt):
                    pso = obank.next()
                    for gi in range(4):
                        g = tt * 4 + gi
                        gs = slice(g * 128, (g + 1) * 128)
                        col = slice(gi * 128, (gi + 1) * 128)
                        pms = []
                        for d in range(2):
                            pss = sbank.next()
                            S.mm(pss[:, 0:128], k_t[d][:, gs], q_t[d][:, gs], True, True, [k_t[d], q_t[d]], [pss])
                            p_ = pm.next()
                            S.tt("dve", p_[:], pss[:, 0:128], masks[:, d, :], ALU.mult, [pss, masks], [p_])
                            pms.append(p_)
                        mms = [(pso[:, col], v[:, g, :], pms[0][:], [v, pms[0]]), (pso[:, col], v[:, g, :], pms[1][:], [v, pms[1]])]
                        for cc in range(4):
                            c = g * 4 + cc
                            cs = slice(c * 32, (c + 1) * 32)
                            ocol = slice(gi * 128 + cc * 32, gi * 128 + (cc + 1) * 32)
                            if c >= 1:
                                mms.append((pso[:, ocol], Sall[0][:, c - 1, :], q_t[0][:, cs], [Sall[0], q_t[0]]))
                            if c <= nch - 2:
                                mms.append((pso[:, ocol], Sall[1][:, c + 1, :], q_t[1][:, cs], [Sall[1], q_t[1]]))
                        for mi, (o_, l_, r_, rd_) in enumerate(mms):
                            S.mm(o_, l_, r_, mi == 0, mi == len(mms) - 1, rd_, [pso])
                    sl = slice(tt * 512, (tt + 1) * 512)
                    q = sq.next()
                    S.act(q[:], pso[:, :], AF.Square, [pso], [q])
                    ps2 = nbank.next()
                    S.mm(ps2[:, :], self.ones[:], q[:], True, True, [q, self.ones], [ps2])
                    r, t_ = rs.next(), tmp.next()
                    self.rstd_from_ss(r[:], ps2[:, :], 128, t_[:], [ps2], [r, t_])
                    n_ = on.next()
                    S.tt("dve", n_[:], pso[:, :], r[:], ALU.mult, [pso, r], [n_])
                    o = ob.next()
                    S.stt(o[:], n_[:], gout[:, 0:1], G[:, sl], ALU.mult, ALU.mult, [n_, gout, G], [o])
                    S.dma("sp", self.scr["OT"][h * 128:(h + 1) * 128, sl], o[:], [o], [])


def declare_inputs(b, L, NSEQ):
    b.din("x", [NSEQ, L, D])
    b.din("mem", [NSEQ, MEMT, D])
    for n in ("norm_mix", "norm_mem", "norm_memtok", "norm_mlp"):
        b.din(n, [4, D])
    b.din("mem_w_q", [4, D, 512])
    b.din("mem_w_kv", [4, D, 1024])
    b.din("mem_w_out", [4, 512, D])
    b.din("memg", [4, 128, 2])
    b.din("mlp_w1", [4, D, DFF])
    b.din("mlp_w2", [4, DFF, D])
    b.din("c_ident", [128, 128])
    b.din("gqa_w_in", [1, D, 3072])
    b.din("gqa_w_out", [1, D, D])
    b.din("gqag", [128, 2])
    b.din("c_gqa_cos", [128, L])
    b.din("c_gqa_sin", [128, L])
    b.din("c_perm128", [128, 128])
    b.din("ret_w_in", [1, D, 12288])
    b.din("ret_w_out", [1, 4096, D])
    b.din("ret_decay", [1, 2, 8])
    b.din("retg", [128, 4])
    b.din("c_ret_cos", [128, L])
    b.din("c_ret_sin", [128, L])
    b.din("c_ret_diff", [128, 2 * L - 128])
    b.din("hg_w_in", [1, D, 10240])
    b.din("hg_w_out", [1, D, D])
    b.din("hglb", [128, 4, 16])
    b.din("hgg", [128, 1])
    b.din("c_hg_rowmask", [128, 4])
    b.din("c_hg_reset", [128, L])
    b.din("c_hg_masks", [128, 2, 128])
    b.din("mla_w_in", [1, D, 1088])
    b.din("mla_w_qb", [1, 512, 3072])
    b.din("mla_w_kvb", [1, 512, 4096])
    b.din("mla_w_out", [1, D, D])
    b.din("mlalat", [128, 8])
    b.din("mlag", [128, 4])
    b.din("c_mla_cos", [64, L])
    b.din("c_mla_sin", [64, L])
    b.din("c_perm64", [64, 64])


def build(L=2048, NSEQ=3, layers=(0, 1, 2, 3), dbg=(), parts=("mix", "mem", "mlp")):
    b = Builder(L, NSEQ, dbg)
    b.flags = parts
    nc = b.nc
    declare_inputs(b, L, NSEQ)
    y = b.dscr("y", [NSEQ, L, D], F32, out=True)
    xr = b.dscr("XR", [NSEQ, L, D], F32)
    b.dscr("MK", [NSEQ, 512, MEMT], BF16)
    b.dscr("MV", [NSEQ, MEMT, 512], BF16)
    b.dscr("QT", [3072, L], BF16)
    b.dscr("KT", [2048 + 64, L], BF16)
    b.dscr("KR", [1024, L], BF16)
    b.dscr("GT", [4096, L], BF16)
    b.dscr("LF", [4096, L], F32)
    b.dscr("OT", [4096, L], BF16)
    b.dscr("VV", [L, 4096], BF16)
    with b.stack:
        b.S = Sched(nc, b.stack)
        b.setup_consts()
        nl = len(layers)
        for li, layer in enumerate(layers):
            src = b.inp["x"] if li == 0 else xr
            dst = y if li == nl - 1 else xr
            kind = layer % 4
            for s in range(NSEQ):
                oT, Ko, w_out = None, 0, None
                if "mix" in parts:
                    if kind == 3:
                        if "nohead" not in parts:
                            b.gqa_head(layer, s, src)
                        if "nomix" not in parts:
                            b.gqa_mix(s)
                        if "noout" not in parts:
                            oT, Ko, w_out = b.scr["OT"][0:2048, :], 2048, b.inp["gqa_w_out"][0]
                    if kind == 0:
                        b.ret_head(layer, s, src)
                        b.ret_mix(s)
                        oT, Ko, w_out = b.scr["OT"], 4096, b.inp["ret_w_out"][0]
                    if kind == 1:
                        b.hg_head(layer, s, src)
                        if "nomix" not in parts:
                            b.hg_mix(s)
                            oT, Ko, w_out = b.scr["OT"][0:2048, :], 2048, b.inp["hg_w_out"][0]
                    if kind == 2:
                        b.mla_head(layer, s, src)
                        b.mla_mix(s)
                        oT, Ko, w_out = b.scr["OT"][0:2048, :], 2048, b.inp["mla_w_out"][0]
                if "mem" in parts:
                    b.memkv(layer, s)
                for tb in range(L // 512):
                    b.tail_block(layer, s, tb, src, dst, oT, Ko, w_out, parts=parts)
        b.S.barrier()
    print("instr counts", b.S.ninst, "sems", b.S.nsem)
    return b


def rope_tables(L):
    o = {"c_ident": np.eye(128, dtype=np.float32)}
    t = np.arange(L)
    f32 = (10000.0 ** (-np.arange(32, dtype=np.float32) / np.float32(32))).astype(np.float32)
    rows = (t // 64).astype(np.float32)
    cols = (t % 64).astype(np.float32)
    cos = np.zeros((128, L), np.float32)
    sin = np.zeros((128, L), np.float32)
    for d in range(128):
        pos = rows if d < 64 else cols
        ang = (pos * f32[d % 32]).astype(np.float32)
        sgn = -1.0 if (d % 64) < 32 else 1.0
        cos[d] = np.cos(ang)
        sin[d] = sgn * np.sin(ang)
    o["c_gqa_cos"], o["c_gqa_sin"] = cos, sin
    perm = np.zeros((128, 128), np.float32)
    for m in range(128):
        perm[64 * (m // 64) + ((m % 64) + 32) % 64, m] = 1.0
    o["c_perm128"] = perm
    tf = t.astype(np.float32)
    mc = np.zeros((64, L), np.float32)
    ms = np.zeros((64, L), np.float32)
    for d in range(64):
        ang = (tf * f32[d % 32]).astype(np.float32)
        mc[d] = np.cos(ang)
        ms[d] = (-1.0 if d < 32 else 1.0) * np.sin(ang)
    o["c_mla_cos"], o["c_mla_sin"] = mc, ms
    p64 = np.zeros((64, 64), np.float32)
    for m in range(64):
        p64[(m + 32) % 64, m] = 1.0
    o["c_perm64"] = p64
    p_ = np.arange(128)
    o["c_hg_rowmask"] = (p_[:, None] // 32 == np.arange(4)[None, :]).astype(np.float32)
    o["c_hg_reset"] = np.broadcast_to((t % 32 != 0).astype(np.float32)[None, :], (128, L)).copy()
    same = (p_[:, None] // 32) == (p_[None, :] // 32)
    mk = np.zeros((128, 2, 128), np.float32)
    mk[:, 0, :] = (same & (p_[:, None] <= p_[None, :])).astype(np.float32)
    mk[:, 1, :] = (same & (p_[:, None] >= p_[None, :])).astype(np.float32)
    o["c_hg_masks"] = mk
    fr = (10000.0 ** (-np.arange(128, dtype=np.float32) / np.float32(128))).astype(np.float32)
    ang = (fr[:, None] * tf[None, :]).astype(np.float32)
    o["c_ret_cos"], o["c_ret_sin"] = np.cos(ang).astype(np.float32), np.sin(ang).astype(np.float32)
    W = 2 * L - 128
    o["c_ret_diff"] = (np.arange(W, dtype=np.float32)[None, :] - np.arange(128, dtype=np.float32)[:, None] - np.float32(L - 128)).astype(np.float32)
    return o


def host_layout(inputs, L):
    o = {}
    for n in ("norm_mix", "norm_mem", "norm_memtok", "norm_mlp", "mem_w_q", "mem_w_kv", "mem_w_out", "mlp_w1", "mlp_w2",
              "gqa_w_in", "gqa_w_out", "mla_w_in", "mla_w_qb", "mla_w_kvb", "mla_w_out",
              "ret_w_in", "ret_w_out", "ret_decay", "hg_w_in", "hg_w_out"):
        o[n] = np.ascontiguousarray(inputs[n], dtype=np.float32)
    g = np.asarray(inputs["mem_qk_norm"], dtype=np.float32)
    o["memg"] = np.ascontiguousarray(g.transpose(0, 2, 1))
    o["gqag"] = np.ascontiguousarray(np.asarray(inputs["gqa_qk_norm"], dtype=np.float32)[0].T)
    o["hglb"] = np.ascontiguousarray(np.asarray(inputs["hg_lb"], np.float32).reshape(4, 16, 128).transpose(2, 0, 1))
    o["hgg"] = np.ascontiguousarray(np.asarray(inputs["hg_out_norm"], np.float32)[0].reshape(128, 1))
    o["retg"] = np.ascontiguousarray(np.asarray(inputs["ret_out_norm"], np.float32)[0].reshape(4, 128).T)
    lat = np.concatenate([np.asarray(inputs["mla_q_norm"], np.float32)[0].reshape(4, 128),
                          np.asarray(inputs["mla_kv_norm"], np.float32)[0].reshape(4, 128)], axis=0)
    o["mlalat"] = np.ascontiguousarray(lat.T)
    qk = np.asarray(inputs["mla_qk_norm"], np.float32)[0]
    mg = np.ones((128, 4), np.float32)
    mg[:, 0] = qk[0, :128]
    mg[:64, 1] = qk[0, 128:]
    mg[:, 2] = qk[1, :128]
    mg[:64, 3] = qk[1, 128:]
    o["mlag"] = mg
    o.update(rope_tables(L))
    return o


def kernel(**inputs):
    L, NSEQ = 2048, 3
    b = build(L, NSEQ)
    shared = host_layout(inputs, L)
    xp, xs = np.asarray(inputs["x_prompt"]), np.asarray(inputs["x_sample"])
    mp, ms = np.asarray(inputs["mem_prompt"]), np.asarray(inputs["mem_sample"])
    in_maps = []
    for c in range(8):
        m = dict(shared)
        m["x"] = np.ascontiguousarray(np.stack([xp[c], xs[2 * c], xs[2 * c + 1]]))
        m["mem"] = np.ascontiguousarray(np.stack([mp[c], ms[2 * c], ms[2 * c + 1]]))
        in_maps.append(m)
    res = run_bass_kernel_spmd(b.nc, in_maps, core_ids=list(range(8)))
    yp = np.stack([res.results[c]["y"][0] for c in range(8)])
    ys = np.stack([res.results[c]["y"][j] for c in range(8) for j in (1, 2)])
    return (yp.astype(np.float32), ys.astype(np.float32))
```

```python
import numpy as np
from contextlib import ExitStack
import concourse.bass as bass
import concourse.mybir as mybir
from concourse.bass_utils import run_bass_kernel_spmd

F32 = mybir.dt.float32
BF16 = mybir.dt.bfloat16
AF = mybir.ActivationFunctionType
ALU = mybir.AluOpType
AX = mybir.AxisListType

D = 2048
MEMT = 256
EPS = 1e-6
DFF = 8192
SEM_EPOCH = 20000
NDMASEM = 12


class Res:
    __slots__ = ("w", "r")

    def __init__(self):
        self.w = None
        self.r = {}


class Rot:
    def __init__(self, items):
        self.items = items
        self.i = 0

    def next(self):
        x = self.items[self.i % len(self.items)]
        self.i += 1
        return x


class Sched:
    def __init__(self, nc, stack, self_sync=True):
        self.nc = nc
        self.stack = stack
        self.self_sync = self_sync
        self.eng = {"pe": nc.tensor, "act": nc.scalar, "dve": nc.vector, "pool": nc.gpsimd, "sp": nc.sync}
        self.sem = {}
        self.cnt = {}
        self.nsem = 0
        for e in self.eng:
            self._new_sem(e)
        self.known = {e: {} for e in self.eng}
        self.res = {}
        self.dsem = {}
        self.dcnt = {}
        self.dnext = {}
        for q in ("sp", "pool"):
            self.dsem[q] = [self._alloc_sem(f"d{q}{i}") for i in range(NDMASEM)]
            self.dcnt[q] = [0] * NDMASEM
            self.dnext[q] = 0
        self.psum = [self.stack.enter_context(nc.psum_tensor(f"ps{i}", [128, 512], F32)) for i in range(8)]
        self.psum_i = 0
        self.ninst = {e: 0 for e in self.eng}

    def _alloc_sem(self, name):
        self.nsem += 1
        return self.stack.enter_context(self.nc.semaphore(name))

    def _new_sem(self, e):
        k = self.nsem
        h = self._alloc_sem(f"s{e}{k}")
        self.sem[e] = (f"{e}{k}", h)
        self.cnt[e] = 0

    def psum_next(self):
        p = self.psum[self.psum_i % 8]
        self.psum_i += 1
        return p

    def _r(self, key):
        if not isinstance(key, (str, tuple, int)):
            key = ("T", key.name)
        r = self.res.get(key)
        if r is None:
            r = self.res[key] = Res()
        return r

    def _wait(self, e, t):
        if t is None:
            return
        k, h, v = t
        own = self.sem[e][0] == k
        if own and (e == "pe" or not self.self_sync):
            return
        if self.known[e].get(k, 0) >= v:
            return
        self.known[e][k] = v
        self.eng[e].wait_ge(h, v)

    def emit(self, e, fn, reads=(), writes=(), dmaq=None):
        rs = [self._r(k) for k in reads]
        ws = [self._r(k) for k in writes]
        for r in rs:
            self._wait(e, r.w)
        for w in ws:
            self._wait(e, w.w)
            for t in w.r.values():
                self._wait(e, t)
        if dmaq is not None:
            i = self.dnext[dmaq] % NDMASEM
            self.dnext[dmaq] += 1
            h = self.dsem[dmaq][i]
            k = f"d{dmaq}{i}"
            prev = self.dcnt[dmaq][i]
            if prev:
                self._wait(e, (k, h, prev))
            self.dcnt[dmaq][i] = prev + 16
            t = (k, h, prev + 16)
            fn().then_inc(h, 16)
        else:
            if self.cnt[e] >= SEM_EPOCH:
                self._new_sem(e)
            k, h = self.sem[e]
            self.cnt[e] += 1
            t = (k, h, self.cnt[e])
            fn().then_inc(h, 1)
        self.ninst[e] += 1
        for r in rs:
            r.r[t[0]] = t
        for w in ws:
            w.w = t
            w.r = {}
        return t

    def barrier(self):
        ticks = []
        for e in self.eng:
            k, h = self.sem[e]
            if self.cnt[e]:
                ticks.append((k, h, self.cnt[e]))
        for q in self.dsem:
            for i in range(NDMASEM):
                if self.dcnt[q][i]:
                    ticks.append((f"d{q}{i}", self.dsem[q][i], self.dcnt[q][i]))
        for e in self.eng:
            for t in ticks:
                self._wait(e, t)
        self.res = {}

    def dma(self, q, out, in_, reads, writes):
        eng = self.eng[q]
        return self.emit(q, lambda: eng.dma_start(out=out, in_=in_), reads, writes, dmaq=q)

    def mm(self, out, lhsT, rhs, start, stop, reads, writes):
        return self.emit("pe", lambda: self.nc.tensor.matmul(out, lhsT, rhs, start=start, stop=stop), reads, writes)

    def tr(self, out, in_, ident, reads, writes):
        return self.emit("pe", lambda: self.nc.tensor.transpose(out, in_, ident), reads, writes)

    def act(self, out, in_, func, reads, writes, **kw):
        return self.emit("act", lambda: self.nc.scalar.activation(out, in_, func, **kw), reads, writes)

    def copy(self, e, out, in_, reads, writes):
        if e == "act":
            return self.emit("act", lambda: self.nc.scalar.copy(out, in_), reads, writes)
        eng = self.eng[e]
        return self.emit(e, lambda: eng.tensor_copy(out, in_), reads, writes)

    def tt(self, e, out, in0, in1, op, reads, writes):
        eng = self.eng[e]
        return self.emit(e, lambda: eng.tensor_tensor(out, in0, in1, op), reads, writes)

    def ts(self, e, out, in0, s1, s2, op0, op1, reads, writes):
        eng = self.eng[e]
        if op1 is None:
            return self.emit(e, lambda: eng.tensor_scalar(out, in0, s1, None, op0), reads, writes)
        return self.emit(e, lambda: eng.tensor_scalar(out, in0, s1, s2, op0, op1), reads, writes)

    def stt(self, out, in0, scalar, in1, op0, op1, reads, writes):
        return self.emit("dve", lambda: self.nc.vector.scalar_tensor_tensor(out, in0, scalar, in1, op0, op1), reads, writes)


class Phase:
    _n = [0]

    def __init__(self, b):
        self.b = b
        self.st = ExitStack()

    def __enter__(self):
        self.st.__enter__()
        return self

    def __exit__(self, *a):
        self.b.S.barrier()
        return self.st.__exit__(*a)

    def t(self, name, shape, dt):
        Phase._n[0] += 1
        return self.st.enter_context(self.b.nc.sbuf_tensor(f"{name}_{Phase._n[0]}", shape, dt))

    def rot(self, name, n, shape, dt):
        return Rot([self.t(f"{name}{i}", shape, dt) for i in range(n)])


class Builder:
    def __init__(self, L, NSEQ, dbg=()):
        self.L = L
        self.NSEQ = NSEQ
        self.dbg = set(dbg)
        self.nc = bass.Bass("TRN2", target_bir_lowering=False)
        self.inp = {}
        self.scr = {}
        self.stack = ExitStack()
        self.flags = ()

    def din(self, name, shape, dt=F32):
        self.inp[name] = self.nc.dram_tensor(name, list(shape), dt, kind="ExternalInput").ap()
        return self.inp[name]

    def dscr(self, name, shape, dt, out=False):
        kind = "ExternalOutput" if (out or name in self.dbg) else "Internal"
        self.scr[name] = self.nc.dram_tensor(name, list(shape), dt, kind=kind).ap()
        return self.scr[name]

    def setup_consts(self):
        S, nc = self.S, self.nc
        st = self.stack
        self.ident = st.enter_context(nc.sbuf_tensor("ident", [128, 128], F32))
        S.dma("sp", self.ident[:], self.inp["c_ident"], [], [self.ident])
        self.identb = st.enter_context(nc.sbuf_tensor("identb", [128, 128], BF16))
        S.copy("dve", self.identb[:], self.ident[:], [self.ident], [self.identb])
        self.ones = st.enter_context(nc.sbuf_tensor("ones", [128, 128], BF16))
        S.emit("dve", lambda: nc.vector.memset(self.ones[:], 1.0), [], [self.ones])
        self.epsb = st.enter_context(nc.sbuf_tensor("epsb", [128, 1], F32))
        S.emit("dve", lambda: nc.vector.memset(self.epsb[:], EPS), [], [self.epsb])
        self.onecol = st.enter_context(nc.sbuf_tensor("onecol", [128, 1], F32))
        S.emit("dve", lambda: nc.vector.memset(self.onecol[:], 1.0), [], [self.onecol])
        S.barrier()

    def rstd_from_ss(self, out, ss, n, tmp, reads, writes):
        S = self.S
        p = ss.shape[0]
        S.act(tmp, ss, AF.Ln, reads, writes, scale=1.0 / n, bias=self.epsb[:p, :])
        S.act(out, tmp, AF.Exp, writes, writes, scale=-0.5)

    def norm_tile(self, ph, xt, xkey, i, gbc, hT, hkey, tmps):
        S, nc = self.S, self.nc
        junk, strot, hnrot = tmps
        s = strot.next()
        S.act(junk[:], xt, AF.Square, [xkey], [junk, s], accum_out=s[:, 0:1])
        self.rstd_from_ss(s[:, 2:3], s[:, 0:1], D, s[:, 1:2], [s], [s])
        hn = hnrot.next()
        S.stt(hn[:], xt, s[:, 2:3], gbc[:], ALU.mult, ALU.mult, [xkey, s, gbc], [hn])
        for g in range(4):
            ps = S.psum_next()
            for j in range(4):
                kc = g * 4 + j
                S.tr(ps[:, j * 128:(j + 1) * 128], hn[:, kc * 128:(kc + 1) * 128], self.ident[:], [hn], [ps])
            S.copy("act" if g % 2 == 0 else "dve", hT[:, g * 4:(g + 1) * 4, i * 128:(i + 1) * 128],
                   ps[:, :].rearrange("p (j t) -> p j t", j=4), [ps], [(hkey, i)])

    def norm_tmps(self, ph):
        return (ph.t("junk", [128, D], BF16), ph.rot("st", 2, [128, 4], F32), ph.rot("hn", 2, [128, D], F32))

    def load_gain(self, ph, row_ap):
        g = ph.t("gbc", [128, D], F32)
        self.S.dma("sp", g[:], row_ap.partition_broadcast(128), [], [g])
        return g

    def lin_fm(self, ph, W, blocks, actT, KC, ntt, akeys, epi, wbufs=None, sb=1):
        S = self.S
        if wbufs is None:
            wbufs = ph.rot("wfm", 3 if sb == 1 else 2, [128, KC, 128 * sb], BF16)
            if any(m < 128 for _, m in blocks):
                for w_ in wbufs.items:
                    self.zero(w_)
        Wv = W.rearrange("(kc p) n -> p kc n", p=128)
        groups = []
        for bi, (c0, m) in enumerate(blocks):
            if groups and m == 128 and len(groups[-1]) < sb and groups[-1][-1][2] == 128 and groups[-1][-1][1] + 128 == c0:
                groups[-1].append((bi, c0, m))
            else:
                groups.append([(bi, c0, m)])
        for grp in groups:
            wb = wbufs.next()
            c00 = grp[0][1]
            ncol = sum(m for _, _, m in grp)
            S.dma("pool", wb[:, :, :ncol], Wv[:, :, c00:c00 + ncol], [], [wb])
            for j, (bi, c0, m) in enumerate(grp):
                for tt in range(ntt):
                    ps = S.psum_next()
                    for kc in range(KC):
                        S.mm(ps[:, :], wb[:, kc, j * 128:(j + 1) * 128], actT[:, kc, tt * 512:(tt + 1) * 512], kc == 0, kc == KC - 1,
                             [wb] + akeys(tt), [ps])
                    epi(bi, tt, ps)

    def lin_tm(self, ph, W, n0, ncols, actT, KC, ntok, akeys, epi, wbufs=None, G=8, wload=None):
        S = self.S
        if wbufs is None:
            wbufs = ph.rot("wtm", 3, [128, G, 512], BF16)
        Wv = W.rearrange("(kc p) n -> p kc n", p=128) if wload is None else None
        banks = [S.psum_next() for _ in range(ntok)]
        ng = (KC + G - 1) // G
        for g in range(ng):
            k0 = g * G
            kn = min(G, KC - k0)
            wg = wbufs.next()
            if wload is None:
                S.dma("pool", wg[:, :kn, :ncols], Wv[:, k0:k0 + kn, n0:n0 + ncols], [], [wg])
                extra = []
            else:
                extra = wload(wg, k0, kn)
            for tt in range(ntok):
                for j in range(kn):
                    kc = k0 + j
                    S.mm(banks[tt][:, :ncols], actT[:, kc, tt * 128:(tt + 1) * 128], wg[:, j, :ncols],
                         kc == 0, kc == KC - 1, [wg] + extra + akeys(tt), [banks[tt]])
        for tt in range(ntok):
            epi(tt, banks[tt])

    def memkv(self, layer, s):
        S, nc = self.S, self.nc
        inp = self.inp
        with Phase(self) as ph:
            gbc = self.load_gain(ph, inp["norm_memtok"][layer, :])
            tmps = self.norm_tmps(ph)
            mT = ph.t("mT", [128, 16, 512], BF16)
            xts = ph.rot("xt", 2, [128, D], F32)
            for i in range(2):
                xt = xts.next()
                S.dma("sp", xt[:], inp["mem"][s, i * 128:(i + 1) * 128, :], [], [xt])
                self.norm_tile(ph, xt[:], xt, i, gbc, mT, "mT", tmps)
            akeys = lambda tt: [("mT", 0), ("mT", 1)]
            gk = inp["memg"][layer]
            gcol = ph.t("gcol", [128, 2], F32)
            S.dma("sp", gcol[:], gk, [], [gcol])
            sq = ph.rot("sq", 2, [128, 256], BF16)
            rs = ph.rot("rs", 2, [128, 256], F32)
            tmp = ph.rot("tmp", 2, [128, 256], F32)
            ko = ph.rot("ko", 2, [128, 256], BF16)
            wkv = inp["mem_w_kv"][layer]

            def epi_k(bi, tt, ps):
                q = sq.next()
                S.act(q[:], ps[:, :256], AF.Square, [ps], [q])
                ps2 = S.psum_next()
                S.mm(ps2[:, :256], self.ones[:], q[:], True, True, [q, self.ones], [ps2])
                r, t_ = rs.next(), tmp.next()
                self.rstd_from_ss(r[:], ps2[:, :256], 128, t_[:], [ps2], [r, t_])
                o = ko.next()
                S.stt(o[:], ps[:, :256], gcol[:, 1:2], r[:], ALU.mult, ALU.mult, [ps, r, gcol], [o])
                S.dma("sp", self.scr["MK"][s, bi * 128:(bi + 1) * 128, :], o[:], [o], [])

            wb = ph.rot("wfm", 3, [128, 16, 128], BF16)
            Wv = wkv.rearrange("(kc p) n -> p kc n", p=128)
            for h in range(4):
                w_ = wb.next()
                S.dma("pool", w_[:], Wv[:, :, h * 128:(h + 1) * 128], [], [w_])
                ps = S.psum_next()
                for kc in range(16):
                    S.mm(ps[:, :256], w_[:, kc, :], mT[:, kc, 0:256], kc == 0, kc == 15, [w_] + akeys(0), [ps])
                epi_k(h, 0, ps)
            vo = ph.rot("vo", 2, [128, 512], BF16)

            def epi_v(tt, ps):
                o = vo.next()
                S.copy("act", o[:], ps[:, :], [ps], [o])
                S.dma("sp", self.scr["MV"][s, tt * 128:(tt + 1) * 128, :], o[:], [o], [])

            self.lin_tm(ph, wkv, 512, 512, mT, 16, 2, akeys, epi_v)

    def tail_block(self, layer, s, tb, x_src, x_dst, oT_src, Ko, w_out, parts=("mix", "mem", "mlp")):
        S, nc = self.S, self.nc
        inp = self.inp
        with Phase(self) as ph:
            xres = ph.t("xres", [128, 4, D], F32)
            for i in range(4):
                S.dma("sp", xres[:, i, :], x_src[s, tb * 512 + i * 128: tb * 512 + (i + 1) * 128, :], [], [("xres", i)])

            def epi_res(nb):
                def f(tt, ps):
                    sl = xres[:, tt, nb * 512:(nb + 1) * 512]
                    S.tt("dve", sl, ps[:, :], sl, ALU.add, [ps, ("xres", tt)], [("xres", tt)])
                return f

            if "mix" in parts and oT_src is not None:
                with Phase(self) as p2:
                    KCo = Ko // 128
                    oT = p2.t("oT", [128, KCo, 512], BF16)
                    S.dma("sp", oT[:], oT_src.rearrange("(kc p) t -> p kc t", p=128)[:, :, tb * 512:(tb + 1) * 512], [], [oT])
                    wb = p2.rot("wtm", 3, [128, 8, 512], BF16)
                    for nb in range(4):
                        self.lin_tm(p2, w_out, nb * 512, 512, oT, KCo, 4, lambda tt: [oT], epi_res(nb), wbufs=wb)
            if "mem" in parts:
                with Phase(self) as p2:
                    gbc = self.load_gain(p2, inp["norm_mem"][layer, :])
                    tmps = self.norm_tmps(p2)
                    hT = p2.t("hT", [128, 16, 512], BF16)
                    for i in range(4):
                        self.norm_tile(p2, xres[:, i, :], ("xres", i), i, gbc, hT, "hT", tmps)
                    hkeys = lambda tt: [("hT", i) for i in range(4)]
                    gcol = p2.t("gcol", [128, 2], F32)
                    S.dma("sp", gcol[:], inp["memg"][layer], [], [gcol])
                    mk = p2.t("mk", [128, 4, 256], BF16)
                    S.dma("sp", mk[:], self.scr["MK"][s].rearrange("(h p) m -> p h m", p=128), [], [mk])
                    mv = p2.t("mv", [128, 2, 512], BF16)
                    S.dma("sp", mv[:], self.scr["MV"][s].rearrange("(c p) e -> p c e", p=128), [], [mv])
                    qT = p2.t("qT", [128, 4, 512], BF16)
                    sq = p2.rot("sq", 2, [128, 512], BF16)
                    rs = p2.rot("rs", 2, [128, 512], F32)
                    tmp = p2.rot("tmp", 2, [128, 512], F32)

                    def epi_q(bi, tt, ps):
                        q = sq.next()
                        S.act(q[:], ps[:, :], AF.Square, [ps], [q])
                        ps2 = S.psum_next()
                        S.mm(ps2[:, :], self.ones[:], q[:], True, True, [q, self.ones], [ps2])
                        r, t_ = rs.next(), tmp.next()
                        self.rstd_from_ss(r[:], ps2[:, :], 128, t_[:], [ps2], [r, t_])
                        S.stt(qT[:, bi, :], ps[:, :], gcol[:, 0:1], r[:], ALU.mult, ALU.mult, [ps, r, gcol], [("qT", bi)])

                    self.lin_fm(p2, inp["mem_w_q"][layer], [(h * 128, 128) for h in range(4)], hT, 16, 1, hkeys, epi_q, sb=4)
                    oT = p2.t("oTm", [128, 4, 512], BF16)
                    pT = p2.rot("pT", 4, [128, 512], BF16)
                    rden = p2.rot("rden", 2, [128, 512], F32)
                    for h in range(4):
                        pts = []
                        for mc in range(2):
                            ps = S.psum_next()
                            S.mm(ps[:, :], mk[:, h, mc * 128:(mc + 1) * 128], qT[:, h, :], True, True, [mk, ("qT", h)], [ps])
                            p_ = pT.next()
                            S.act(p_[:], ps[:, :], AF.Exp, [ps], [p_], scale=128 ** -0.5)
                            pts.append(p_)
                        pso = S.psum_next()
                        psd = S.psum_next()
                        for mc in range(2):
                            S.mm(pso[:, :], mv[:, mc, h * 128:(h + 1) * 128], pts[mc][:], mc == 0, mc == 1, [mv, pts[mc]], [pso])
                        for mc in range(2):
                            S.mm(psd[:, :], self.ones[:], pts[mc][:], mc == 0, mc == 1, [self.ones, pts[mc]], [psd])
                        rd = rden.next()
                        S.emit("dve", lambda: nc.vector.reciprocal(rd[:], psd[:, :]), [psd], [rd])
                        S.tt("dve", oT[:, h, :], pso[:, :], rd[:], ALU.mult, [pso, rd], [("oTm", h)])
                    wb = p2.rot("wtm", 3, [128, 8, 512], BF16)
                    okeys = lambda tt: [("oTm", h) for h in range(4)]
                    for nb in range(4):
                        self.lin_tm(p2, inp["mem_w_out"][layer], nb * 512, 512, oT, 4, 4, okeys, epi_res(nb), wbufs=wb)
            if "mlp" in parts:
                with Phase(self) as p2:
                    gbc = self.load_gain(p2, inp["norm_mlp"][layer, :])
                    aT = p2.t("aT", [128, 64, 512], BF16)
                    hT = p2.t("hT", [128, 16, 512], BF16)
                    with Phase(self) as p3:
                        tmps = self.norm_tmps(p3)
                        for i in range(4):
                            self.norm_tile(p3, xres[:, i, :], ("xres", i), i, gbc, hT, "hT", tmps)
                    hkeys = lambda tt: [("hT", i) for i in range(4)]
                    rl = p2.rot("rl", 3, [128, 512], F32)

                    def epi_a(bi, tt, ps):
                        r = rl.next()
                        S.act(r[:], ps[:, :], AF.Relu, [ps], [r])
                        S.tt("dve", aT[:, bi, :], r[:], r[:], ALU.mult, [r], [("aT", bi)])

                    self.lin_fm(p2, inp["mlp_w1"][layer], [(j * 128, 128) for j in range(64)], hT, 16, 1, hkeys, epi_a)
                    wb = p2.rot("wtm", 3, [128, 8, 512], BF16)
                    for nb in range(4):
                        self.lin_tm(p2, inp["mlp_w2"][layer], nb * 512, 512, aT, 64, 4,
                                    lambda tt: [("aT", j) for j in range(64)] if tt == 0 else [], epi_res(nb), wbufs=wb)
            for i in range(4):
                S.dma("sp", x_dst[s, tb * 512 + i * 128: tb * 512 + (i + 1) * 128, :], xres[:, i, :], [("xres", i)], [])


    def mlp_block(self, layer, s, tb, x_src, x_dst):
        S, nc, inp = self.S, self.nc, self.inp
        r0 = tb * 1024
        with Phase(self) as ph:
            aT = ph.t("aT", [128, 64, 1024], BF16)
            with Phase(self) as p2:
                gbc = self.load_gain(p2, inp["norm_mlp"][layer, :])
                hT = p2.t("hT", [128, 16, 1024], BF16)
                with Phase(self) as p3:
                    tmps = self.norm_tmps(p3)
                    xts = p3.rot("xt", 2, [128, D], F32)
                    for i in range(8):
                        xt = xts.next()
                        S.dma("sp", xt[:], x_src[s, r0 + i * 128:r0 + (i + 1) * 128, :], [], [xt])
                        self.norm_tile(p3, xt[:], xt, i, gbc, hT, "hT", tmps)
                hkeys = lambda tt: [("hT", tt * 4 + i) for i in range(4)]
                rl = p2.rot("rl", 3, [128, 512], F32)

                def epi_a(bi, tt, ps):
                    r = rl.next()
                    S.act(r[:], ps[:, :], AF.Relu, [ps], [r])
                    S.tt("dve", aT[:, bi, tt * 512:(tt + 1) * 512], r[:], r[:], ALU.mult, [r], [("aT", bi, tt)])

                self.lin_fm(p2, inp["mlp_w1"][layer], [(j * 128, 128) for j in range(64)], hT, 16, 2, hkeys, epi_a, sb=4)
            with Phase(self) as p2:
                wb = p2.rot("wtm", 3, [128, 8, 512], BF16)
                xin = p2.rot("xin", 16, [128, 512], F32)
                xo = p2.rot("xo", 4, [128, 512], F32)
                for nb in range(4):
                    cs = slice(nb * 512, (nb + 1) * 512)
                    tiles = []
                    for tt in range(8):
                        xi = xin.next()
                        S.dma("sp", xi[:], x_src[s, r0 + tt * 128:r0 + (tt + 1) * 128, cs], [], [xi])
                        tiles.append(xi)

                    def epi(tt, ps, tiles=tiles, cs=cs):
                        o = xo.next()
                        S.tt("dve", o[:], ps[:, :], tiles[tt][:], ALU.add, [ps, tiles[tt]], [o])
                        S.dma("sp", x_dst[s, r0 + tt * 128:r0 + (tt + 1) * 128, cs], o[:], [o], [])

                    self.lin_tm(p2, inp["mlp_w2"][layer], nb * 512, 512, aT, 64, 8, lambda tt: [], epi, wbufs=wb)

    def attn_core(self, ph, kq, V, vkey, nsc, ntt, scale, store):
        S, nc = self.S, self.nc
        if not hasattr(ph, "abufs"):
            ph.abufs = (ph.rot("pT", 4, [128, 512], BF16), ph.rot("rden", 2, [128, 512], F32), ph.rot("ot", 3, [128, 512], BF16))
        pT, rden, ot = ph.abufs
        acc = Rot([(S.psum[4], S.psum[5]), (S.psum[6], S.psum[7])])
        sbank = Rot([S.psum[i] for i in range(4)])
        for tt in range(ntt):
            pso, psd = acc.next()
            for sc in range(nsc):
                ps = sbank.next()
                for pi, (kf, qf, keys) in enumerate(kq):
                    S.mm(ps[:, :], kf(sc), qf(tt), pi == 0, pi == len(kq) - 1, keys, [ps])
                p_ = pT.next()
                S.act(p_[:], ps[:, :], AF.Exp, [ps], [p_], scale=scale)
                S.mm(pso[:, :], V(sc), p_[:], sc == 0, sc == nsc - 1, [p_, vkey], [pso])
                S.mm(psd[:, :], self.ones[:], p_[:], sc == 0, sc == nsc - 1, [p_, self.ones], [psd])
            rd = rden.next()
            S.emit("dve", lambda: nc.vector.reciprocal(rd[:], psd[:, :]), [psd], [rd])
            o = ot.next()
            S.tt("dve", o[:], pso[:, :], rd[:], ALU.mult, [pso, rd], [o])
            store(tt, o)

    def load_x_norm(self, ph, x_src, s, gain_row, hT, hkey, L):
        S = self.S
        gbc = self.load_gain(ph, gain_row)
        with Phase(self) as p3:
            tmps = self.norm_tmps(p3)
            xts = p3.rot("xt", 2, [128, D], F32)
            for i in range(L // 128):
                xt = xts.next()
                S.dma("sp", xt[:], x_src[s, i * 128:(i + 1) * 128, :], [], [xt])
                self.norm_tile(p3, xt[:], xt, i, gbc, hT, hkey, tmps)

    def qk_rope_epi(self, ph, cosT, sinT, perm, P):
        S = self.S
        sq = ph.rot("sq", 2, [128, 512], BF16)
        rs = ph.rot("rs", 2, [128, 512], F32)
        tmp = ph.rot("tmp", 2, [128, 512], F32)
        qg = ph.rot("qg", 2, [128, 512], BF16)
        if P < 128:
            for q_ in qg.items:
                self.zero(q_)
        t1 = ph.rot("t1", 2, [128, 512], F32)
        t2 = ph.rot("t2", 2, [128, 512], F32)

        def f(ps, gcol, gkey, tt, out_ap, okey, rstd=None, pskey=None):
            pk = pskey if pskey is not None else ps
            if rstd is None:
                assert P == 128
                q = sq.next()
                S.act(q[:P, :], ps[:P, :], AF.Square, [ps], [q])
                ps2 = S.psum_next()
                S.mm(ps2[:, :], self.ones[:P, :], q[:P, :], True, True, [q, self.ones], [ps2])
                r, t_ = rs.next(), tmp.next()
                self.rstd_from_ss(r[:], ps2[:, :], P, t_[:], [ps2], [r, t_])
                rstd = r
            g = qg.next()
            S.act(g[:P, :], ps[:P, :], AF.Copy, [pk, gkey], [g], scale=gcol)
            ps3 = S.psum_next()
            S.mm(ps3[:, :], perm[:, :], g[:, :], True, True, [g, perm], [ps3])
            a, b_ = t1.next(), t2.next()
            S.tt("dve", a[:P, :], g[:P, :], cosT[:P, tt * 512:(tt + 1) * 512], ALU.mult, [g, cosT], [a])
            S.tt("dve", b_[:P, :], ps3[:P, :], sinT[:P, tt * 512:(tt + 1) * 512], ALU.mult, [ps3, sinT], [b_])
            S.tt("dve", a[:P, :], a[:P, :], b_[:P, :], ALU.add, [a, b_], [a])
            if rstd == "none":
                S.copy("act", out_ap, a[:P, :], [a], [okey])
            else:
                S.tt("dve", out_ap, a[:P, :], rstd[:P, :], ALU.mult, [a, rstd], [okey])
            return rstd
        return f

    def zero(self, t):
        self.S.emit("pool", lambda: self.nc.gpsimd.memset(t[:], 0.0), [], [t])

    def load_perm(self, ph, name, P):
        S = self.S
        pf = ph.t("permf", [128, 128], F32)
        self.zero(pf)
        S.dma("sp", pf[:P, :P], self.inp[name], [], [pf])
        pb = ph.t("permb", [128, 128], BF16)
        S.copy("dve", pb[:], pf[:], [pf], [pb])
        return pb

    def gqa_head(self, layer, s, x_src):
        S, nc, L, inp = self.S, self.nc, self.L, self.inp
        ntt = L // 512
        w_in = inp["gqa_w_in"][0]
        with Phase(self) as ph:
            hT = ph.t("hT", [128, 16, L], BF16)
            self.load_x_norm(ph, x_src, s, inp["norm_mix"][layer, :], hT, "hT", L)
            hk = lambda tt: [("hT", tt * 4 + i) for i in range(4)]
            cosT = ph.t("cosT", [128, L], F32)
            sinT = ph.t("sinT", [128, L], F32)
            S.dma("sp", cosT[:], inp["c_gqa_cos"], [], [cosT])
            S.dma("sp", sinT[:], inp["c_gqa_sin"], [], [sinT])
            perm = self.load_perm(ph, "c_perm128", 128)
            gcol = ph.t("gcol", [128, 2], F32)
            S.dma("sp", gcol[:], inp["gqag"], [], [gcol])
            epi = self.qk_rope_epi(ph, cosT, sinT, perm, 128)
            ob = ph.rot("ob", 3, [128, 512], BF16)

            def epi_qk(bi, tt, ps):
                isq = bi < 16
                o = ob.next()
                epi(ps, gcol[:, 0:1] if isq else gcol[:, 1:2], gcol, tt, o[:], o)
                dst = self.scr["QT"][bi * 128:(bi + 1) * 128, :] if isq else self.scr["KT"][(bi - 16) * 128:(bi - 15) * 128, :]
                S.dma("sp", dst[:, tt * 512:(tt + 1) * 512], o[:], [o], [])

            if "noqk" not in self.flags:
                self.lin_fm(ph, w_in, [(h * 128, 128) for h in range(20)], hT, 16, ntt, hk, epi_qk, sb=4)
            vo = ph.rot("vo", 3, [128, 512], BF16)
            wbv = ph.rot("wtm", 3, [128, 8, 512], BF16)
            for t4 in range(L // 512 if "nov" not in self.flags else 0):
                def epi_v(tt, ps, t4=t4):
                    o = vo.next()
                    S.copy("act", o[:], ps[:, :], [ps], [o])
                    r0 = t4 * 512 + tt * 128
                    S.dma("sp", self.scr["VV"][r0:r0 + 128, 0:512], o[:], [o], [])
                self.lin_tm(ph, w_in, 2560, 512, hT[:, :, t4 * 512:(t4 + 1) * 512], 16, 4,
                            lambda tt, t4=t4: [("hT", t4 * 4 + tt)], epi_v, wbufs=wbv)

    def gqa_mix(self, s):
        S, nc, L = self.S, self.nc, self.L
        nsc, ntt = L // 128, L // 512
        with Phase(self) as ph:
            kTs = ph.rot("kT", 2, [128, L], BF16)
            vs = ph.rot("v", 2, [128, nsc, 128], BF16)
            qTs = ph.rot("qT", 2, [128, L], BF16)
            for g in range(4):
                kT, v = kTs.next(), vs.next()
                S.dma("sp", kT[:], self.scr["KT"][g * 128:(g + 1) * 128, :], [], [kT])
                S.dma("sp", v[:], self.scr["VV"][:, g * 128:(g + 1) * 128].rearrange("(c p) e -> p c e", p=128), [], [v])
                for hh in range(4):
                    h = g * 4 + hh
                    qT = qTs.next()
                    S.dma("sp", qT[:], self.scr["QT"][h * 128:(h + 1) * 128, :], [], [qT])

                    def store(tt, o, h=h):
                        S.dma("sp", self.scr["OT"][h * 128:(h + 1) * 128, tt * 512:(tt + 1) * 512], o[:], [o], [])

                    kq = [(lambda sc, kT=kT: kT[:, sc * 128:(sc + 1) * 128], lambda tt, qT=qT: qT[:, tt * 512:(tt + 1) * 512], [kT, qT])]
                    self.attn_core(ph, kq, lambda sc, v=v: v[:, sc, :], v, nsc, ntt, 128 ** -0.5, store)


    def mla_head(self, layer, s, x_src):
        S, nc, L, inp = self.S, self.nc, self.L, self.inp
        ntt = L // 512
        with Phase(self) as ph:
            craw = ph.t("craw", [128, 8, L], F32)
            krraw = ph.t("krraw", [64, L], F32)
            with Phase(self) as pa:
                hT = pa.t("hT", [128, 16, L], BF16)
                self.load_x_norm(pa, x_src, s, inp["norm_mix"][layer, :], hT, "hT", L)
                hk = lambda tt: [("hT", tt * 4 + i) for i in range(4)]

                def epi_c(bi, tt, ps):
                    if bi < 8:
                        S.copy("act" if (bi + tt) % 2 else "dve", craw[:, bi, tt * 512:(tt + 1) * 512], ps[:, :], [ps], [("craw", bi, tt)])
                    else:
                        S.copy("act", krraw[:, tt * 512:(tt + 1) * 512], ps[:64, :], [ps], [("krraw", tt)])

                self.lin_fm(pa, inp["mla_w_in"][0], [(j * 128, 128) for j in range(8)] + [(1024, 64)], hT, 16, ntt, hk, epi_c)
            cn = ph.t("cn", [128, 8, L], BF16)
            lg = ph.t("lg", [128, 8], F32)
            S.dma("sp", lg[:], inp["mlalat"], [], [lg])
            mg = ph.t("mg", [128, 4], F32)
            S.dma("sp", mg[:], inp["mlag"], [], [mg])
            sq = ph.rot("sq", 3, [128, 512], BF16)
            rs = ph.rot("rs", 2, [128, 512], F32)
            tmp = ph.rot("tmp", 2, [128, 512], F32)
            for half in range(2):
                for tt in range(ntt):
                    ps2 = S.psum_next()
                    for j in range(4):
                        q = sq.next()
                        c = half * 4 + j
                        S.act(q[:], craw[:, c, tt * 512:(tt + 1) * 512], AF.Square, [("craw", c, tt)], [q])
                        S.mm(ps2[:, :], self.ones[:], q[:], j == 0, j == 3, [q, self.ones], [ps2])
                    r, t_ = rs.next(), tmp.next()
                    self.rstd_from_ss(r[:], ps2[:, :], 512, t_[:], [ps2], [r, t_])
                    for j in range(4):
                        c = half * 4 + j
                        S.stt(cn[:, c, tt * 512:(tt + 1) * 512], craw[:, c, tt * 512:(tt + 1) * 512], lg[:, c:c + 1], r[:],
                              ALU.mult, ALU.mult, [("craw", c, tt), lg, r], [("cn", c, tt)])
            cosT = ph.t("cosT", [64, L], F32)
            sinT = ph.t("sinT", [64, L], F32)
            S.dma("sp", cosT[:], inp["c_mla_cos"], [], [cosT])
            S.dma("sp", sinT[:], inp["c_mla_sin"], [], [sinT])
            perm = self.load_perm(ph, "c_perm64", 64)
            epi = self.qk_rope_epi(ph, cosT, sinT, perm, 64)
            Rk = ph.t("Rk", [64, L], F32)
            sqkr = ph.t("sqkr", [128, L], BF16)
            self.zero(sqkr)
            S.barrier()
            for tt in range(ntt):
                sl = slice(tt * 512, (tt + 1) * 512)
                S.act(sqkr[:64, sl], krraw[:, sl], AF.Square, [("krraw", tt)], [("sqkr", tt)])
                epi(krraw[:, sl], mg[:64, 3:4], mg, tt, Rk[:, sl], ("Rk", tt), rstd="none", pskey=("krraw", tt))
            ob = ph.rot("ob", 3, [128, 512], BF16)
            ob2 = ph.rot("ob2", 3, [64, 512], BF16)
            wq = ph.rot("wq", 2, [128, 4, 256], BF16)
            for w_ in wq.items:
                self.zero(w_)
            Wq = inp["mla_w_qb"][0].rearrange("(kc p) n -> p kc n", p=128)
            Wkv = inp["mla_w_kvb"][0].rearrange("(kc p) n -> p kc n", p=128)

            def norm192(psn, sqr_ap, sqr_key):
                q = sq.next()
                S.act(q[:], psn[:, :], AF.Square, [psn], [q])
                ps2 = S.psum_next()
                S.mm(ps2[:, :], self.ones[:], q[:], True, False, [q, self.ones], [ps2])
                S.mm(ps2[:, :], self.ones[:, :], sqr_ap, False, True, [sqr_key, self.ones], [ps2])
                r, t_ = rs.next(), tmp.next()
                self.rstd_from_ss(r[:], ps2[:, :], 192, t_[:], [ps2], [r, t_])
                return r

            sqr = ph.rot("sqr", 2, [128, 512], BF16)
            for q_ in sqr.items:
                self.zero(q_)
            for h in range(16):
                w_ = wq.next()
                S.dma("pool", w_[:, :, 0:192], Wq[:, :, h * 192:(h + 1) * 192], [], [w_])
                for tt in range(ntt):
                    sl = slice(tt * 512, (tt + 1) * 512)
                    ck = [("cn", j, tt) for j in range(4)]
                    psn, psr = S.psum_next(), S.psum_next()
                    for j in range(4):
                        S.mm(psn[:, :], w_[:, j, 0:128], cn[:, j, sl], j == 0, j == 3, [w_] + ck, [psn])
                    for j in range(4):
                        S.mm(psr[:, :], w_[:, j, 128:256], cn[:, j, sl], j == 0, j == 3, [w_] + ck, [psr])
                    q2 = sqr.next()
                    S.act(q2[:64, :], psr[:64, :], AF.Square, [psr], [q2])
                    r = norm192(psn, q2[:], q2)
                    o = ob.next()
                    S.stt(o[:], psn[:, :], mg[:, 0:1], r[:], ALU.mult, ALU.mult, [psn, mg, r], [o])
                    S.dma("sp", self.scr["QT"][h * 192:h * 192 + 128, sl], o[:], [o], [])
                    o2 = ob2.next()
                    epi(psr[:64, :], mg[:64, 1:2], mg, tt, o2[:], o2, rstd=r, pskey=psr)
                    S.dma("sp", self.scr["QT"][h * 192 + 128:(h + 1) * 192, sl], o2[:], [o2], [])
            wk = ph.rot("wk", 2, [128, 4, 128], BF16)
            for h in range(16):
                w_ = wk.next()
                S.dma("pool", w_[:], Wkv[:, 4:8, h * 256:h * 256 + 128] if False else
                      inp["mla_w_kvb"][0].rearrange("(kc p) n -> p kc n", p=128)[:, :, h * 256:h * 256 + 128], [], [w_])
                for tt in range(ntt):
                    sl = slice(tt * 512, (tt + 1) * 512)
                    ck = [("cn", 4 + j, tt) for j in range(4)]
                    psn = S.psum_next()
                    for j in range(4):
                        S.mm(psn[:, :], w_[:, j, :], cn[:, 4 + j, sl], j == 0, j == 3, [w_] + ck, [psn])
                    r = norm192(psn, sqkr[:, sl], ("sqkr", tt))
                    o = ob.next()
                    S.stt(o[:], psn[:, :], mg[:, 2:3], r[:], ALU.mult, ALU.mult, [psn, mg, r], [o])
                    S.dma("sp", self.scr["KT"][h * 128:(h + 1) * 128, sl], o[:], [o], [])
                    o2 = ob2.next()
                    S.tt("dve", o2[:], Rk[:, sl], r[:64, :], ALU.mult, [("Rk", tt), r], [o2])
                    S.dma("sp", self.scr["KR"][h * 64:(h + 1) * 64, sl], o2[:], [o2], [])
            vo = ph.rot("vo", 3, [128, 512], BF16)
            Wv5 = inp["mla_w_kvb"][0].rearrange("(kc p) (h two e) -> p kc h two e", p=128, two=2, e=128)
            wbv = ph.rot("wtm", 3, [128, 8, 512], BF16)
            for nb in range(4):
                def wload(wg, k0, kn, nb=nb):
                    ks = []
                    for k in range(4):
                        key = ("wgk", wg.name, k)
                        S.dma("pool", wg[:, k, :].rearrange("p (h e) -> p h e", e=128), Wv5[:, k, nb * 4:(nb + 1) * 4, 1, :], [], [wg, key])
                        ks.append(key)
                    return ks
                for t4 in range(L // 512):
                    def epi_v(tt, ps, t4=t4, nb=nb):
                        o = vo.next()
                        S.copy("act", o[:], ps[:, :], [ps], [o])
                        r0 = t4 * 512 + tt * 128
                        S.dma("sp", self.scr["VV"][r0:r0 + 128, nb * 512:(nb + 1) * 512], o[:], [o], [])
                    self.lin_tm(ph, None, 0, 512, cn[:, 4:8, t4 * 512:(t4 + 1) * 512], 4, 4,
                                lambda tt, t4=t4: [("cn", 4 + j, t4) for j in range(4)], epi_v, wload=wload, wbufs=wbv)

    def mla_mix(self, s):
        S, nc, L = self.S, self.nc, self.L
        nsc, ntt = L // 128, L // 512
        with Phase(self) as ph:
            kTs = ph.rot("kT", 2, [128, L], BF16)
            kRs = ph.rot("kR", 2, [128, L], BF16)
            vs = ph.rot("v", 2, [128, nsc, 128], BF16)
            qTs = ph.rot("qT", 2, [128, L], BF16)
            qRs = ph.rot("qR", 2, [128, L], BF16)
            for t_ in kRs.items + qRs.items:
                self.zero(t_)
            S.barrier()
            for h in range(16):
                kT, kR, v, qT, qR = kTs.next(), kRs.next(), vs.next(), qTs.next(), qRs.next()
                S.dma("sp", kT[:], self.scr["KT"][h * 128:(h + 1) * 128, :], [], [kT])
                S.dma("sp", kR[:64, :], self.scr["KR"][h * 64:(h + 1) * 64, :], [], [kR])
                S.dma("sp", v[:], self.scr["VV"][:, h * 128:(h + 1) * 128].rearrange("(c p) e -> p c e", p=128), [], [v])
                S.dma("sp", qT[:], self.scr["QT"][h * 192:h * 192 + 128, :], [], [qT])
                S.dma("sp", qR[:64, :], self.scr["QT"][h * 192 + 128:(h + 1) * 192, :], [], [qR])

                def store(tt, o, h=h):
                    S.dma("sp", self.scr["OT"][h * 128:(h + 1) * 128, tt * 512:(tt + 1) * 512], o[:], [o], [])

                kq = [(lambda sc, kT=kT: kT[:, sc * 128:(sc + 1) * 128], lambda tt, qT=qT: qT[:, tt * 512:(tt + 1) * 512], [kT, qT]),
                      (lambda sc, kR=kR: kR[:, sc * 128:(sc + 1) * 128], lambda tt, qR=qR: qR[:, tt * 512:(tt + 1) * 512], [kR, qR])]
                self.attn_core(ph, kq, lambda sc, v=v: v[:, sc, :], v, nsc, ntt, 192 ** -0.5, store)


    def ret_head(self, layer, s, x_src):
        S, nc, L, inp = self.S, self.nc, self.L, self.inp
        ntt = L // 512
        w_in = inp["ret_w_in"][0]
        Wv = w_in.rearrange("(kc p) n -> p kc n", p=128)
        with Phase(self) as ph:
            hT = ph.t("hT", [128, 16, L], BF16)
            self.load_x_norm(ph, x_src, s, inp["norm_mix"][layer, :], hT, "hT", L)
            hk = lambda tt: [("hT", tt * 4 + i) for i in range(4)]
            with Phase(self) as p2:
                cosT = p2.t("cosT", [128, L], F32)
                sinT = p2.t("sinT", [128, L], F32)
                S.dma("sp", cosT[:], inp["c_ret_cos"], [], [cosT])
                S.dma("sp", sinT[:], inp["c_ret_sin"], [], [sinT])
                wq = p2.rot("wq", 2, [128, 16, 256], BF16)
                t1 = p2.rot("t1", 2, [128, 512], F32)
                t2 = p2.rot("t2", 2, [128, 512], F32)
                ob = p2.rot("ob", 4, [128, 512], BF16)
                for which in range(2):
                    dst = self.scr["QT"] if which == 0 else self.scr["KT"]
                    for h in range(8):
                        w_ = wq.next()
                        c0 = which * 2048 + h * 256
                        S.dma("pool", w_[:], Wv[:, :, c0:c0 + 256], [], [w_])
                        for tt in range(ntt):
                            sl = slice(tt * 512, (tt + 1) * 512)
                            psa, psb = S.psum_next(), S.psum_next()
                            for kc in range(16):
                                S.mm(psa[:, :], w_[:, kc, 0:128], hT[:, kc, sl], kc == 0, kc == 15, [w_] + hk(tt), [psa])
                            for kc in range(16):
                                S.mm(psb[:, :], w_[:, kc, 128:256], hT[:, kc, sl], kc == 0, kc == 15, [w_] + hk(tt), [psb])
                            a, b_ = t1.next(), t2.next()
                            S.tt("dve", a[:], psa[:, :], cosT[:, sl], ALU.mult, [psa, cosT], [a])
                            S.tt("dve", b_[:], psb[:, :], sinT[:, sl], ALU.mult, [psb, sinT], [b_])
                            o1 = ob.next()
                            S.tt("dve", o1[:], a[:], b_[:], ALU.subtract, [a, b_], [o1])
                            S.dma("sp", dst[h * 256:h * 256 + 128, sl], o1[:], [o1], [])
                            a, b_ = t1.next(), t2.next()
                            S.tt("dve", a[:], psa[:, :], sinT[:, sl], ALU.mult, [psa, sinT], [a])
                            S.tt("dve", b_[:], psb[:, :], cosT[:, sl], ALU.mult, [psb, cosT], [b_])
                            o2 = ob.next()
                            S.tt("dve", o2[:], a[:], b_[:], ALU.add, [a, b_], [o2])
                            S.dma("sp", dst[h * 256 + 128:(h + 1) * 256, sl], o2[:], [o2], [])
            with Phase(self) as p2:
                go = p2.rot("go", 3, [128, 512], BF16)

                def epi_g(bi, tt, ps):
                    o = go.next()
                    S.act(o[:], ps[:, :], AF.Silu, [ps], [o])
                    S.dma("sp", self.scr["GT"][bi * 128:(bi + 1) * 128, tt * 512:(tt + 1) * 512], o[:], [o], [])

                self.lin_fm(p2, w_in, [(8192 + j * 128, 128) for j in range(32)], hT, 16, ntt, hk, epi_g, sb=4)
                vo = p2.rot("vo", 3, [128, 512], BF16)
                wb = p2.rot("wtm", 3, [128, 8, 512], BF16)
                for nb in range(8):
                    for t4 in range(L // 512):
                        def epi_v(tt, ps, t4=t4, nb=nb):
                            o = vo.next()
                            S.copy("act", o[:], ps[:, :], [ps], [o])
                            r0 = t4 * 512 + tt * 128
                            S.dma("sp", self.scr["VV"][r0:r0 + 128, nb * 512:(nb + 1) * 512], o[:], [o], [])
                        self.lin_tm(p2, w_in, 4096 + nb * 512, 512, hT[:, :, t4 * 512:(t4 + 1) * 512], 16, 4,
                                    lambda tt, t4=t4: [("hT", t4 * 4 + tt)], epi_v, wbufs=wb)

    def ret_mix(self, s):
        S, nc, L, inp = self.S, self.nc, self.L, self.inp
        nsc, ntt = L // 128, L // 512
        W = 2 * L - 128
        with Phase(self) as ph:
            lgx = ph.t("lgx", [128, 16], F32)
            S.dma("sp", lgx[:], inp["ret_decay"].rearrange("a b h -> (a b h)").partition_broadcast(128), [], [lgx])
            ax = ph.t("ax", [128, 16], F32)
            S.act(ax[:], lgx[:], AF.Abs, [lgx], [ax])
            S.act(ax[:], ax[:], AF.Exp, [ax], [ax], scale=-1.0)
            S.act(ax[:], ax[:], AF.Ln, [ax], [ax], bias=self.onecol[:, :])
            lg = ph.t("lg", [128, 16], F32)
            S.ts("dve", lg[:], lgx[:], 0.0, None, ALU.min, None, [lgx], [lg])
            S.tt("dve", lg[:], lg[:], ax[:], ALU.subtract, [lg, ax], [lg])
            nlg = ph.t("nlg", [128, 16], F32)
            S.ts("dve", nlg[:], lg[:], -1.0, None, ALU.mult, None, [lg], [nlg])
            gout = ph.t("gout", [128, 4], F32)
            S.dma("sp", gout[:], inp["retg"], [], [gout])
            diff = ph.t("diff", [128, W], F32)
            S.dma("sp", diff[:], inp["c_ret_diff"], [], [diff])
            A = ph.t("A", [128, W], F32)
            E = ph.t("E", [128, W], F32)
            qTs = ph.rot("qT", 2, [128, 2, L], BF16)
            kTs = ph.rot("kT", 2, [128, 2, L], BF16)
            vs = ph.rot("v", 2, [128, nsc, 512], BF16)
            gs = ph.rot("g", 2, [128, 4, L], BF16)
            pT = ph.rot("pT", 4, [128, 512], BF16)
            sq = ph.rot("sq", 4, [128, 512], BF16)
            rs = ph.rot("rs", 2, [128, 512], F32)
            tmp = ph.rot("tmp", 2, [128, 512], F32)
            on = ph.rot("on", 2, [128, 512], F32)
            ob = ph.rot("ob", 3, [128, 512], BF16)
            sbank = Rot([S.psum[i] for i in range(4)])
            O = [S.psum[4 + i] for i in range(4)]
            for h in range(8):
                qT, kT, v, g = qTs.next(), kTs.next(), vs.next(), gs.next()
                S.dma("sp", qT[:], self.scr["QT"][h * 256:(h + 1) * 256, :].rearrange("(c p) t -> p c t", p=128), [], [qT])
                S.dma("sp", kT[:], self.scr["KT"][h * 256:(h + 1) * 256, :].rearrange("(c p) t -> p c t", p=128), [], [kT])
                S.dma("sp", v[:], self.scr["VV"][:, h * 512:(h + 1) * 512].rearrange("(c p) e -> p c e", p=128), [], [v])
                S.dma("sp", g[:], self.scr["GT"][h * 512:(h + 1) * 512, :].rearrange("(c p) t -> p c t", p=128), [], [g])
                S.act(A[:], diff[:], AF.Relu, [diff, nlg], [A], scale=nlg[:, h:h + 1])
                S.act(E[:], diff[:], AF.Relu, [diff, lg], [E], scale=lg[:, 8 + h:9 + h])
                S.tt("dve", A[:], A[:], E[:], ALU.add, [A, E], [A])
                S.act(E[:], A[:], AF.Exp, [A], [E], scale=-1.0)
                S.ts("dve", A[:], diff[:], 0.0, 1.0 / 16, ALU.is_equal, ALU.mult, [diff], [A])
                S.stt(E[:], A[:], 1.0 / 16, E[:], ALU.add, ALU.mult, [A, E], [E])
                for tt in range(ntt):
                    sl = slice(tt * 512, (tt + 1) * 512)
                    for sc in range(nsc):
                        ps = sbank.next()
                        for c in range(2):
                            S.mm(ps[:, :], kT[:, c, sc * 128:(sc + 1) * 128], qT[:, c, sl], c == 0, c == 1, [kT, qT], [ps])
                        p_ = pT.next()
                        off = tt * 512 - sc * 128 + (L - 128)
                        S.tt("dve", p_[:], ps[:, :], E[:, off:off + 512], ALU.mult, [ps, E], [p_])
                        for ec in range(4):
                            S.mm(O[ec][:, :], v[:, sc, ec * 128:(ec + 1) * 128], p_[:], sc == 0, sc == nsc - 1, [v, p_], [O[ec]])
                    ps2 = sbank.next()
                    for ec in range(4):
                        q = sq.next()
                        S.act(q[:], O[ec][:, :], AF.Square, [O[ec]], [q])
                        S.mm(ps2[:, :], self.ones[:], q[:], ec == 0, ec == 3, [q, self.ones], [ps2])
                    r, t_ = rs.next(), tmp.next()
                    self.rstd_from_ss(r[:], ps2[:, :], 512, t_[:], [ps2], [r, t_])
                    for ec in range(4):
                        n_ = on.next()
                        S.tt("dve", n_[:], O[ec][:, :], r[:], ALU.mult, [O[ec], r], [n_])
                        o = ob.next()
                        S.stt(o[:], n_[:], gout[:, ec:ec + 1], g[:, ec, sl], ALU.mult, ALU.mult, [n_, gout, g], [o])
                        S.dma("sp", self.scr["OT"][h * 512 + ec * 128:h * 512 + (ec + 1) * 128, sl], o[:], [o], [])


    def hg_head(self, layer, s, x_src):
        S, nc, L, inp = self.S, self.nc, self.L, self.inp
        ntt = L // 512
        w_in = inp["hg_w_in"][0]
        with Phase(self) as ph:
            hT = ph.t("hT", [128, 16, L], BF16)
            self.load_x_norm(ph, x_src, s, inp["norm_mix"][layer, :], hT, "hT", L)
            hk = lambda tt: [("hT", tt * 4 + i) for i in range(4)]
            lraw = ph.t("lraw", [128, 4, 16], F32)
            S.dma("sp", lraw[:], inp["hglb"], [], [lraw])
            mx = ph.t("mx", [128, 16], F32)
            S.tt("dve", mx[:], lraw[:, 0, :], lraw[:, 1, :], ALU.max, [lraw], [mx])
            S.tt("dve", mx[:], mx[:], lraw[:, 2, :], ALU.max, [lraw, mx], [mx])
            S.tt("dve", mx[:], mx[:], lraw[:, 3, :], ALU.max, [lraw, mx], [mx])
            for j in range(4):
                S.tt("dve", lraw[:, j, :], lraw[:, j, :], mx[:], ALU.subtract, [lraw, mx], [lraw])
            S.act(lraw[:], lraw[:], AF.Exp, [lraw], [lraw])
            sm = ph.t("sm", [128, 16], F32)
            S.tt("dve", sm[:], lraw[:, 0, :], lraw[:, 1, :], ALU.add, [lraw], [sm])
            S.tt("dve", sm[:], sm[:], lraw[:, 2, :], ALU.add, [lraw, sm], [sm])
            S.tt("dve", sm[:], sm[:], lraw[:, 3, :], ALU.add, [lraw, sm], [sm])
            S.emit("dve", lambda: nc.vector.reciprocal(sm[:], sm[:]), [sm], [sm])
            lb = ph.t("lb", [128, 16], F32)
            S.copy("dve", lb[:], lraw[:, 1, :], [lraw], [lb])
            for j in range(2, layer + 1):
                S.tt("dve", lb[:], lb[:], lraw[:, j, :], ALU.add, [lraw, lb], [lb])
            S.tt("dve", lb[:], lb[:], sm[:], ALU.mult, [lb, sm], [lb])
            oml = ph.t("oml", [128, 16], F32)
            S.ts("dve", oml[:], lb[:], -1.0, 1.0, ALU.mult, ALU.add, [lb], [oml])
            qo = ph.rot("qo", 3, [128, 512], BF16)
            sg = ph.rot("sg", 3, [128, 512], F32)
            lfo = ph.rot("lfo", 3, [128, 512], F32)

            def epi(bi, tt, ps):
                sl = slice(tt * 512, (tt + 1) * 512)
                if bi < 16:
                    o = qo.next()
                    S.copy("dve", o[:], ps[:, :], [ps], [o])
                    S.dma("sp", self.scr["QT"][bi * 128:(bi + 1) * 128, sl], o[:], [o], [])
                elif bi < 48:
                    h = (bi - 16) % 16
                    g_ = sg.next()
                    S.act(g_[:], ps[:, :], AF.Sigmoid, [ps], [g_])
                    o = lfo.next()
                    S.act(o[:], g_[:], AF.Ln, [g_, oml, lb], [o], scale=oml[:, h:h + 1], bias=lb[:, h:h + 1])
                    S.dma("sp", self.scr["LF"][(bi - 16) * 128:(bi - 15) * 128, sl], o[:], [o], [])
                else:
                    o = qo.next()
                    S.act(o[:], ps[:, :], AF.Silu, [ps], [o])
                    S.dma("sp", self.scr["GT"][(bi - 48) * 128:(bi - 47) * 128, sl], o[:], [o], [])

            blocks = [(j * 128, 128) for j in range(48)] + [(8192 + j * 128, 128) for j in range(16)]
            self.lin_fm(ph, w_in, blocks, hT, 16, ntt, hk, epi, sb=4)
            vo = ph.rot("vo", 3, [128, 512], BF16)
            wb = ph.rot("wtm", 3, [128, 8, 512], BF16)
            for nb in range(4):
                for t4 in range(L // 512):
                    def epi_v(tt, ps, t4=t4, nb=nb):
                        o = vo.next()
                        S.copy("act", o[:], ps[:, :], [ps], [o])
                        r0 = t4 * 512 + tt * 128
                        S.dma("sp", self.scr["VV"][r0:r0 + 128, nb * 512:(nb + 1) * 512], o[:], [o], [])
                    self.lin_tm(ph, w_in, 6144 + nb * 512, 512, hT[:, :, t4 * 512:(t4 + 1) * 512], 16, 4,
                                lambda tt, t4=t4: [("hT", t4 * 4 + tt)], epi_v, wbufs=wb)

    def hg_mix(self, s):
        S, nc, L, inp = self.S, self.nc, self.L, self.inp
        nsc, ntt, nch = L // 128, L // 512, L // 32
        with Phase(self) as ph:
            rmask = ph.t("rmask", [128, 4], F32)
            S.dma("sp", rmask[:], inp["c_hg_rowmask"], [], [rmask])
            reset = ph.t("reset", [128, L], F32)
            S.dma("sp", reset[:], inp["c_hg_reset"], [], [reset])
            masks = ph.t("masks", [128, 2, 128], F32)
            S.dma("sp", masks[:], inp["c_hg_masks"], [], [masks])
            gout = ph.t("gout", [128, 1], F32)
            S.dma("sp", gout[:], inp["hgg"], [], [gout])
            qT = ph.t("qT", [128, L], BF16)
            lf = [ph.t("lf0", [128, L], F32), ph.t("lf1", [128, L], F32)]
            v = ph.t("v", [128, nsc, 128], BF16)
            vm = ph.t("vm", [128, nsc, 4, 128], BF16)
            G = ph.t("G", [128, L], BF16)
            b_ = ph.t("b", [128, L], F32)
            e1 = ph.t("e1", [128, L], F32)
            e2 = ph.t("e2", [128, L], F32)
            kg = ph.t("kg", [128, L], F32)
            etot = [ph.t("etot0", [128, nch], F32), ph.t("etot1", [128, nch], F32)]
            q_t = [ph.t("qt0", [128, L], BF16), ph.t("qt1", [128, L], BF16)]
            k_t = [ph.t("kt0", [128, L], BF16), ph.t("kt1", [128, L], BF16)]
            kdT = ph.t("kdT", [128, nsc, 128], BF16)
            U = ph.t("U", [128, nch, 128], F32)
            Sall = [ph.t("Sall0", [128, nch, 128], BF16), ph.t("Sall1", [128, nch, 128], BF16)]
            pm = ph.rot("pm", 4, [128, 128], BF16)
            sq = ph.rot("sq", 2, [128, 512], BF16)
            rs = ph.rot("rs", 2, [128, 512], F32)
            tmp = ph.rot("tmp", 2, [128, 512], F32)
            on = ph.rot("on", 2, [128, 512], F32)
            ob = ph.rot("ob", 2, [128, 512], BF16)
            b3 = lambda t: t[:, :].rearrange("p (c k) -> p c k", k=32)
            obank = Rot([S.psum[6], S.psum[7]])
            sbank = Rot([S.psum[i] for i in range(4)])
            nbank = Rot([S.psum[4], S.psum[5]])
            for h in range(16):
                S.dma("sp", qT[:], self.scr["QT"][h * 128:(h + 1) * 128, :], [], [qT])
                for d in range(2):
                    S.dma("sp", lf[d][:], self.scr["LF"][d * 2048 + h * 128:d * 2048 + (h + 1) * 128, :], [], [lf[d]])
                S.dma("sp", v[:], self.scr["VV"][:, h * 128:(h + 1) * 128].rearrange("(c p) e -> p c e", p=128), [], [v])
                S.dma("sp", G[:], self.scr["GT"][h * 128:(h + 1) * 128, :], [], [G])
                for c in range(4):
                    S.act(vm[:, :, c, :], v[:, :, :], AF.Copy, [v, rmask], [vm], scale=rmask[:, c:c + 1])
                for d in range(2):
                    S.emit("dve", lambda: nc.vector.tensor_tensor_scan(b_[:], reset[:], lf[d][:], 0.0, ALU.mult, ALU.add),
                           [reset, lf[d]], [b_])
                    S.act(etot[d][:], b3(b_)[:, :, 31], AF.Exp, [b_], [etot[d]])
                    if d == 1:
                        S.tt("dve", e1[:], lf[d][:], b_[:], ALU.subtract, [lf[d], b_], [e1])
                        S.tt("dve", b3(b_), b3(e1), b3(b_)[:, :, 31:32].to_broadcast([128, nch, 32]), ALU.add, [e1, b_], [b_])
                    S.act(e1[:], b_[:], AF.Exp, [b_], [e1])
                    S.act(e2[:], b_[:], AF.Exp, [b_], [e2], scale=-1.0)
                    S.act(kg[:], lf[d][:], AF.Exp, [lf[d]], [kg])
                    S.ts("dve", kg[:], kg[:], -1.0, 1.0, ALU.mult, ALU.add, [kg], [kg])
                    S.tt("dve", q_t[d][:], qT[:], e1[:], ALU.mult, [qT, e1], [q_t[d]])
                    S.tt("dve", kg[:], kg[:], e2[:], ALU.mult, [kg, e2], [kg])
                    S.copy("act", k_t[d][:], kg[:], [kg], [k_t[d]])
                    S.tt("dve", b3(e1), b3(kg), etot[d][:, :].to_broadcast([128, nch, 1]).to_broadcast([128, nch, 32]) if False else
                         etot[d][:, :].rearrange("p (c o) -> p c o", o=1).to_broadcast([128, nch, 32]), ALU.mult, [kg, etot[d]], [e1])
                    for g in range(nsc):
                        if g % 4 == 0:
                            pst = S.psum_next()
                        S.tr(pst[:, (g % 4) * 128:(g % 4 + 1) * 128], e1[:, g * 128:(g + 1) * 128], self.ident[:], [e1], [pst])
                        if g % 4 == 3:
                            S.copy("act", kdT[:, g - 3:g + 1, :], pst[:, :].rearrange("p (j t) -> p j t", j=4), [pst], [kdT])
                    for g in range(nsc):
                        psu = S.psum_next()
                        for cc in range(4):
                            S.mm(psu[:, cc * 128:(cc + 1) * 128], kdT[:, g, :], vm[:, g, cc, :], True, True, [kdT, vm], [psu])
                        S.copy("act", U[:, g * 4:(g + 1) * 4, :], psu[:, :].rearrange("p (j t) -> p j t", j=4), [psu], [U])
                    order = range(1, nch) if d == 0 else range(nch - 2, -1, -1)
                    for c in order:
                        pv = c - 1 if d == 0 else c + 1
                        S.stt(U[:, c, :], U[:, pv, :], etot[d][:, c:c + 1], U[:, c, :], ALU.mult, ALU.add, [U, etot[d]], [U])
                    for q4 in range(4):
                        n4 = nch // 4
                        S.copy("act" if q4 % 2 else "dve", Sall[d][:, q4 * n4:(q4 + 1) * n4, :], U[:, q4 * n4:(q4 + 1) * n4, :], [U], [Sall[d]])
                for tt in range(ntt):
                    pso = obank.next()
                    for gi in range(4):
                        g = tt * 4 + gi
                        gs = slice(g * 128, (g + 1) * 128)
                        col = slice(gi * 128, (gi + 1) * 128)
                        pms = []
                        for d in range(2):
                            pss = sbank.next()
                            S.mm(pss[:, 0:128], k_t[d][:, gs], q_t[d][:, gs], True, True, [k_t[d], q_t[d]], [pss])
                            p_ = pm.next()
                            S.tt("dve", p_[:], pss[:, 0:128], masks[:, d, :], ALU.mult, [pss, masks], [p_])
                            pms.append(p_)
                        mms = [(pso[:, col], v[:, g, :], pms[0][:], [v, pms[0]]), (pso[:, col], v[:, g, :], pms[1][:], [v, pms[1]])]
                        for cc in range(4):
                            c = g * 4 + cc
                            cs = slice(c * 32, (c + 1) * 32)
                            ocol = slice(gi * 128 + cc * 32, gi * 128 + (cc + 1) * 32)
                            if c >= 1:
                                mms.append((pso[:, ocol], Sall[0][:, c - 1, :], q_t[0][:, cs], [Sall[0], q_t[0]]))
                            if c <= nch - 2:
                                mms.append((pso[:, ocol], Sall[1][:, c + 1, :], q_t[1][:, cs], [Sall[1], q_t[1]]))
                        for mi, (o_, l_, r_, rd_) in enumerate(mms):
                            S.mm(o_, l_, r_, mi == 0, mi == len(mms) - 1, rd_, [pso])
                    sl = slice(tt * 512, (tt + 1) * 512)
                    q = sq.next()
                    S.act(q[:], pso[:, :], AF.Square, [pso], [q])
                    ps2 = nbank.next()
                    S.mm(ps2[:, :], self.ones[:], q[:], True, True, [q, self.ones], [ps2])
                    r, t_ = rs.next(), tmp.next()
                    self.rstd_from_ss(r[:], ps2[:, :], 128, t_[:], [ps2], [r, t_])
                    n_ = on.next()
                    S.tt("dve", n_[:], pso[:, :], r[:], ALU.mult, [pso, r], [n_])
                    o = ob.next()
                    S.stt(o[:], n_[:], gout[:, 0:1], G[:, sl], ALU.mult, ALU.mult, [n_, gout, G], [o])
                    S.dma("sp", self.scr["OT"][h * 128:(h + 1) * 128, sl], o[:], [o], [])


def declare_inputs(b, L, NSEQ):
    b.din("x", [NSEQ, L, D])
    b.din("mem", [NSEQ, MEMT, D])
    for n in ("norm_mix", "norm_mem", "norm_memtok", "norm_mlp"):
        b.din(n, [4, D])
    b.din("mem_w_q", [4, D, 512])
    b.din("mem_w_kv", [4, D, 1024])
    b.din("mem_w_out", [4, 512, D])
    b.din("memg", [4, 128, 2])
    b.din("mlp_w1", [4, D, DFF])
    b.din("mlp_w2", [4, DFF, D])
    b.din("c_ident", [128, 128])
    b.din("gqa_w_in", [1, D, 3072])
    b.din("gqa_w_out", [1, D, D])
    b.din("gqag", [128, 2])
    b.din("c_gqa_cos", [128, L])
    b.din("c_gqa_sin", [128, L])
    b.din("c_perm128", [128, 128])
    b.din("ret_w_in", [1, D, 12288])
    b.din("ret_w_out", [1, 4096, D])
    b.din("ret_decay", [1, 2, 8])
    b.din("retg", [128, 4])
    b.din("c_ret_cos", [128, L])
    b.din("c_ret_sin", [128, L])
    b.din("c_ret_diff", [128, 2 * L - 128])
    b.din("hg_w_in", [1, D, 10240])
    b.din("hg_w_out", [1, D, D])
    b.din("hglb", [128, 4, 16])
    b.din("hgg", [128, 1])
    b.din("c_hg_rowmask", [128, 4])
    b.din("c_hg_reset", [128, L])
    b.din("c_hg_masks", [128, 2, 128])
    b.din("mla_w_in", [1, D, 1088])
    b.din("mla_w_qb", [1, 512, 3072])
    b.din("mla_w_kvb", [1, 512, 4096])
    b.din("mla_w_out", [1, D, D])
    b.din("mlalat", [128, 8])
    b.din("mlag", [128, 4])
    b.din("c_mla_cos", [64, L])
    b.din("c_mla_sin", [64, L])
    b.din("c_perm64", [64, 64])


def build(L=2048, NSEQ=3, layers=(0, 1, 2, 3), dbg=(), parts=("mix", "mem", "mlp")):
    b = Builder(L, NSEQ, dbg)
    b.flags = parts
    nc = b.nc
    declare_inputs(b, L, NSEQ)
    y = b.dscr("y", [NSEQ, L, D], F32, out=True)
    xr = b.dscr("XR", [NSEQ, L, D], F32)
    b.dscr("MK", [NSEQ, 512, MEMT], BF16)
    b.dscr("MV", [NSEQ, MEMT, 512], BF16)
    b.dscr("QT", [3072, L], BF16)
    b.dscr("KT", [2048 + 64, L], BF16)
    b.dscr("KR", [1024, L], BF16)
    b.dscr("GT", [4096, L], BF16)
    b.dscr("LF", [4096, L], F32)
    b.dscr("OT", [4096, L], BF16)
    b.dscr("VV", [L, 4096], BF16)
    with b.stack:
        b.S = Sched(nc, b.stack)
        b.setup_consts()
        nl = len(layers)
        for li, layer in enumerate(layers):
            src = b.inp["x"] if li == 0 else xr
            dst = y if li == nl - 1 else xr
            kind = layer % 4
            for s in range(NSEQ):
                oT, Ko, w_out = None, 0, None
                if "mix" in parts:
                    if kind == 3:
                        if "nohead" not in parts:
                            b.gqa_head(layer, s, src)
                        if "nomix" not in parts:
                            b.gqa_mix(s)
                        if "noout" not in parts:
                            oT, Ko, w_out = b.scr["OT"][0:2048, :], 2048, b.inp["gqa_w_out"][0]
                    if kind == 0:
                        b.ret_head(layer, s, src)
                        b.ret_mix(s)
                        oT, Ko, w_out = b.scr["OT"], 4096, b.inp["ret_w_out"][0]
                    if kind == 1:
                        b.hg_head(layer, s, src)
                        if "nomix" not in parts:
                            b.hg_mix(s)
                            oT, Ko, w_out = b.scr["OT"][0:2048, :], 2048, b.inp["hg_w_out"][0]
                    if kind == 2:
                        b.mla_head(layer, s, src)
                        b.mla_mix(s)
                        oT, Ko, w_out = b.scr["OT"][0:2048, :], 2048, b.inp["mla_w_out"][0]
                if "mem" in parts:
                    b.memkv(layer, s)
                tparts = tuple(p_ for p_ in parts if p_ != "mlp")
                has_mlp = "mlp" in parts
                mid = xr if has_mlp else dst
                for tb in range(L // 512):
                    b.tail_block(layer, s, tb, src, mid, oT, Ko, w_out, parts=tparts)
                if has_mlp:
                    for tb in range(L // 1024):
                        b.mlp_block(layer, s, tb, xr, dst)
        b.S.barrier()
    print("instr counts", b.S.ninst, "sems", b.S.nsem)
    return b


def rope_tables(L):
    o = {"c_ident": np.eye(128, dtype=np.float32)}
    t = np.arange(L)
    f32 = (10000.0 ** (-np.arange(32, dtype=np.float32) / np.float32(32))).astype(np.float32)
    rows = (t // 64).astype(np.float32)
    cols = (t % 64).astype(np.float32)
    cos = np.zeros((128, L), np.float32)
    sin = np.zeros((128, L), np.float32)
    for d in range(128):
        pos = rows if d < 64 else cols
        ang = (pos * f32[d % 32]).astype(np.float32)
        sgn = -1.0 if (d % 64) < 32 else 1.0
        cos[d] = np.cos(ang)
        sin[d] = sgn * np.sin(ang)
    o["c_gqa_cos"], o["c_gqa_sin"] = cos, sin
    perm = np.zeros((128, 128), np.float32)
    for m in range(128):
        perm[64 * (m // 64) + ((m % 64) + 32) % 64, m] = 1.0
    o["c_perm128"] = perm
    tf = t.astype(np.float32)
    mc = np.zeros((64, L), np.float32)
    ms = np.zeros((64, L), np.float32)
    for d in range(64):
        ang = (tf * f32[d % 32]).astype(np.float32)
        mc[d] = np.cos(ang)
        ms[d] = (-1.0 if d < 32 else 1.0) * np.sin(ang)
    o["c_mla_cos"], o["c_mla_sin"] = mc, ms
    p64 = np.zeros((64, 64), np.float32)
    for m in range(64):
        p64[(m + 32) % 64, m] = 1.0
    o["c_perm64"] = p64
    p_ = np.arange(128)
    o["c_hg_rowmask"] = (p_[:, None] // 32 == np.arange(4)[None, :]).astype(np.float32)
    o["c_hg_reset"] = np.broadcast_to((t % 32 != 0).astype(np.float32)[None, :], (128, L)).copy()
    same = (p_[:, None] // 32) == (p_[None, :] // 32)
    mk = np.zeros((128, 2, 128), np.float32)
    mk[:, 0, :] = (same & (p_[:, None] <= p_[None, :])).astype(np.float32)
    mk[:, 1, :] = (same & (p_[:, None] >= p_[None, :])).astype(np.float32)
    o["c_hg_masks"] = mk
    fr = (10000.0 ** (-np.arange(128, dtype=np.float32) / np.float32(128))).astype(np.float32)
    ang = (fr[:, None] * tf[None, :]).astype(np.float32)
    o["c_ret_cos"], o["c_ret_sin"] = np.cos(ang).astype(np.float32), np.sin(ang).astype(np.float32)
    W = 2 * L - 128
    o["c_ret_diff"] = (np.arange(W, dtype=np.float32)[None, :] - np.arange(128, dtype=np.float32)[:, None] - np.float32(L - 128)).astype(np.float32)
    return o


def host_layout(inputs, L):
    o = {}
    for n in ("norm_mix", "norm_mem", "norm_memtok", "norm_mlp", "mem_w_q", "mem_w_kv", "mem_w_out", "mlp_w1", "mlp_w2",
              "gqa_w_in", "gqa_w_out", "mla_w_in", "mla_w_qb", "mla_w_kvb", "mla_w_out",
              "ret_w_in", "ret_w_out", "ret_decay", "hg_w_in", "hg_w_out"):
        o[n] = np.ascontiguousarray(inputs[n], dtype=np.float32)
    g = np.asarray(inputs["mem_qk_norm"], dtype=np.float32)
    o["memg"] = np.ascontiguousarray(g.transpose(0, 2, 1))
    o["gqag"] = np.ascontiguousarray(np.asarray(inputs["gqa_qk_norm"], dtype=np.float32)[0].T)
    o["hglb"] = np.ascontiguousarray(np.asarray(inputs["hg_lb"], np.float32).reshape(4, 16, 128).transpose(2, 0, 1))
    o["hgg"] = np.ascontiguousarray(np.asarray(inputs["hg_out_norm"], np.float32)[0].reshape(128, 1))
    o["retg"] = np.ascontiguousarray(np.asarray(inputs["ret_out_norm"], np.float32)[0].reshape(4, 128).T)
    lat = np.concatenate([np.asarray(inputs["mla_q_norm"], np.float32)[0].reshape(4, 128),
                          np.asarray(inputs["mla_kv_norm"], np.float32)[0].reshape(4, 128)], axis=0)
    o["mlalat"] = np.ascontiguousarray(lat.T)
    qk = np.asarray(inputs["mla_qk_norm"], np.float32)[0]
    mg = np.ones((128, 4), np.float32)
    mg[:, 0] = qk[0, :128]
    mg[:64, 1] = qk[0, 128:]
    mg[:, 2] = qk[1, :128]
    mg[:64, 3] = qk[1, 128:]
    o["mlag"] = mg
    o.update(rope_tables(L))
    return o


def kernel(**inputs):
    L, NSEQ = 2048, 3
    b = build(L, NSEQ)
    shared = host_layout(inputs, L)
    xp, xs = np.asarray(inputs["x_prompt"]), np.asarray(inputs["x_sample"])
    mp, ms = np.asarray(inputs["mem_prompt"]), np.asarray(inputs["mem_sample"])
    in_maps = []
    for c in range(8):
        m = dict(shared)
        m["x"] = np.ascontiguousarray(np.stack([xp[c], xs[2 * c], xs[2 * c + 1]]))
        m["mem"] = np.ascontiguousarray(np.stack([mp[c], ms[2 * c], ms[2 * c + 1]]))
        in_maps.append(m)
    res = run_bass_kernel_spmd(b.nc, in_maps, core_ids=list(range(8)))
    yp = np.stack([res.results[c]["y"][0] for c in range(8)])
    ys = np.stack([res.results[c]["y"][j] for c in range(8) for j in (1, 2)])
    return (yp.astype(np.float32), ys.astype(np.float32))
```

```python
import numpy as np
from contextlib import ExitStack
import concourse.bass as bass
import concourse.mybir as mybir
from concourse.bass_utils import run_bass_kernel_spmd

F32 = mybir.dt.float32
BF16 = mybir.dt.bfloat16
AF = mybir.ActivationFunctionType
ALU = mybir.AluOpType
AX = mybir.AxisListType

D = 2048
MEMT = 256
EPS = 1e-6
DFF = 8192
SEM_EPOCH = 20000
NDMASEM = 12


class Res:
    __slots__ = ("w", "r")

    def __init__(self):
        self.w = None
        self.r = {}


class Rot:
    def __init__(self, items):
        self.items = items
        self.i = 0

    def next(self):
        x = self.items[self.i % len(self.items)]
        self.i += 1
        return x


class Sched:
    def __init__(self, nc, stack, self_sync=True):
        self.nc = nc
        self.stack = stack
        self.self_sync = self_sync
        self.eng = {"pe": nc.tensor, "act": nc.scalar, "dve": nc.vector, "pool": nc.gpsimd, "sp": nc.sync}
        self.sem = {}
        self.cnt = {}
        self.nsem = 0
        for e in self.eng:
            self._new_sem(e)
        self.known = {e: {} for e in self.eng}
        self.res = {}
        self.dsem = {}
        self.dcnt = {}
        self.dnext = {}
        for q in ("sp", "pool"):
            self.dsem[q] = [self._alloc_sem(f"d{q}{i}") for i in range(NDMASEM)]
            self.dcnt[q] = [0] * NDMASEM
            self.dnext[q] = 0
        self.psum = [self.stack.enter_context(nc.psum_tensor(f"ps{i}", [128, 512], F32)) for i in range(8)]
        self.psum_i = 0
        self.ninst = {e: 0 for e in self.eng}

    def _alloc_sem(self, name):
        self.nsem += 1
        return self.stack.enter_context(self.nc.semaphore(name))

    def _new_sem(self, e):
        k = self.nsem
        h = self._alloc_sem(f"s{e}{k}")
        self.sem[e] = (f"{e}{k}", h)
        self.cnt[e] = 0

    def psum_next(self):
        p = self.psum[self.psum_i % 8]
        self.psum_i += 1
        return p

    def _r(self, key):
        if not isinstance(key, (str, tuple, int)):
            key = ("T", key.name)
        r = self.res.get(key)
        if r is None:
            r = self.res[key] = Res()
        return r

    def _wait(self, e, t):
        if t is None:
            return
        k, h, v = t
        own = self.sem[e][0] == k
        if own and (e == "pe" or not self.self_sync):
            return
        if self.known[e].get(k, 0) >= v:
            return
        self.known[e][k] = v
        self.eng[e].wait_ge(h, v)

    def emit(self, e, fn, reads=(), writes=(), dmaq=None):
        rs = [self._r(k) for k in reads]
        ws = [self._r(k) for k in writes]
        for r in rs:
            self._wait(e, r.w)
        for w in ws:
            self._wait(e, w.w)
            for t in w.r.values():
                self._wait(e, t)
        if dmaq is not None:
            i = self.dnext[dmaq] % NDMASEM
            self.dnext[dmaq] += 1
            h = self.dsem[dmaq][i]
            k = f"d{dmaq}{i}"
            prev = self.dcnt[dmaq][i]
            if prev:
                self._wait(e, (k, h, prev))
            self.dcnt[dmaq][i] = prev + 16
            t = (k, h, prev + 16)
            fn().then_inc(h, 16)
        else:
            if self.cnt[e] >= SEM_EPOCH:
                self._new_sem(e)
            k, h = self.sem[e]
            self.cnt[e] += 1
            t = (k, h, self.cnt[e])
            fn().then_inc(h, 1)
        self.ninst[e] += 1
        for r in rs:
            r.r[t[0]] = t
        for w in ws:
            w.w = t
            w.r = {}
        return t

    def barrier(self):
        ticks = []
        for e in self.eng:
            k, h = self.sem[e]
            if self.cnt[e]:
                ticks.append((k, h, self.cnt[e]))
        for q in self.dsem:
            for i in range(NDMASEM):
                if self.dcnt[q][i]:
                    ticks.append((f"d{q}{i}", self.dsem[q][i], self.dcnt[q][i]))
        for e in self.eng:
            for t in ticks:
                self._wait(e, t)
        self.res = {}

    def dma(self, q, out, in_, reads, writes):
        eng = self.eng[q]
        return self.emit(q, lambda: eng.dma_start(out=out, in_=in_), reads, writes, dmaq=q)

    def mm(self, out, lhsT, rhs, start, stop, reads, writes):
        return self.emit("pe", lambda: self.nc.tensor.matmul(out, lhsT, rhs, start=start, stop=stop), reads, writes)

    def tr(self, out, in_, ident, reads, writes):
        return self.emit("pe", lambda: self.nc.tensor.transpose(out, in_, ident), reads, writes)

    def act(self, out, in_, func, reads, writes, **kw):
        return self.emit("act", lambda: self.nc.scalar.activation(out, in_, func, **kw), reads, writes)

    def copy(self, e, out, in_, reads, writes):
        if e == "act":
            return self.emit("act", lambda: self.nc.scalar.copy(out, in_), reads, writes)
        eng = self.eng[e]
        return self.emit(e, lambda: eng.tensor_copy(out, in_), reads, writes)

    def tt(self, e, out, in0, in1, op, reads, writes):
        eng = self.eng[e]
        return self.emit(e, lambda: eng.tensor_tensor(out, in0, in1, op), reads, writes)

    def ts(self, e, out, in0, s1, s2, op0, op1, reads, writes):
        eng = self.eng[e]
        if op1 is None:
            return self.emit(e, lambda: eng.tensor_scalar(out, in0, s1, None, op0), reads, writes)
        return self.emit(e, lambda: eng.tensor_scalar(out, in0, s1, s2, op0, op1), reads, writes)

    def stt(self, out, in0, scalar, in1, op0, op1, reads, writes):
        return self.emit("dve", lambda: self.nc.vector.scalar_tensor_tensor(out, in0, scalar, in1, op0, op1), reads, writes)


class Phase:
    _n = [0]

    def __init__(self, b):
        self.b = b
        self.st = ExitStack()

    def __enter__(self):
        self.st.__enter__()
        return self

    def __exit__(self, *a):
        self.b.S.barrier()
        return self.st.__exit__(*a)

    def t(self, name, shape, dt):
        Phase._n[0] += 1
        return self.st.enter_context(self.b.nc.sbuf_tensor(f"{name}_{Phase._n[0]}", shape, dt))

    def rot(self, name, n, shape, dt):
        return Rot([self.t(f"{name}{i}", shape, dt) for i in range(n)])


class Builder:
    def __init__(self, L, NSEQ, dbg=()):
        self.L = L
        self.NSEQ = NSEQ
        self.dbg = set(dbg)
        self.nc = bass.Bass("TRN2", target_bir_lowering=False)
        self.inp = {}
        self.scr = {}
        self.stack = ExitStack()
        self.flags = ()

    def din(self, name, shape, dt=F32):
        self.inp[name] = self.nc.dram_tensor(name, list(shape), dt, kind="ExternalInput").ap()
        return self.inp[name]

    def dscr(self, name, shape, dt, out=False):
        kind = "ExternalOutput" if (out or name in self.dbg) else "Internal"
        self.scr[name] = self.nc.dram_tensor(name, list(shape), dt, kind=kind).ap()
        return self.scr[name]

    def setup_consts(self):
        S, nc = self.S, self.nc
        st = self.stack
        self.ident = st.enter_context(nc.sbuf_tensor("ident", [128, 128], F32))
        S.dma("sp", self.ident[:], self.inp["c_ident"], [], [self.ident])
        self.identb = st.enter_context(nc.sbuf_tensor("identb", [128, 128], BF16))
        S.copy("dve", self.identb[:], self.ident[:], [self.ident], [self.identb])
        self.ones = st.enter_context(nc.sbuf_tensor("ones", [128, 128], BF16))
        S.emit("dve", lambda: nc.vector.memset(self.ones[:], 1.0), [], [self.ones])
        self.epsb = st.enter_context(nc.sbuf_tensor("epsb", [128, 1], F32))
        S.emit("dve", lambda: nc.vector.memset(self.epsb[:], EPS), [], [self.epsb])
        self.onecol = st.enter_context(nc.sbuf_tensor("onecol", [128, 1], F32))
        S.emit("dve", lambda: nc.vector.memset(self.onecol[:], 1.0), [], [self.onecol])
        S.barrier()

    def rstd_from_ss(self, out, ss, n, tmp, reads, writes):
        S = self.S
        p = ss.shape[0]
        S.act(tmp, ss, AF.Ln, reads, writes, scale=1.0 / n, bias=self.epsb[:p, :])
        S.act(out, tmp, AF.Exp, writes, writes, scale=-0.5)

    def norm_tile(self, ph, xt, xkey, i, gbc, hT, hkey, tmps):
        S, nc = self.S, self.nc
        junk, strot, hnrot = tmps
        s = strot.next()
        S.act(junk[:], xt, AF.Square, [xkey], [junk, s], accum_out=s[:, 0:1])
        self.rstd_from_ss(s[:, 2:3], s[:, 0:1], D, s[:, 1:2], [s], [s])
        hn = hnrot.next()
        S.stt(hn[:], xt, s[:, 2:3], gbc[:], ALU.mult, ALU.mult, [xkey, s, gbc], [hn])
        for g in range(4):
            ps = S.psum_next()
            for j in range(4):
                kc = g * 4 + j
                S.tr(ps[:, j * 128:(j + 1) * 128], hn[:, kc * 128:(kc + 1) * 128], self.ident[:], [hn], [ps])
            S.copy("act" if g % 2 == 0 else "dve", hT[:, g * 4:(g + 1) * 4, i * 128:(i + 1) * 128],
                   ps[:, :].rearrange("p (j t) -> p j t", j=4), [ps], [(hkey, i)])

    def norm_tmps(self, ph):
        return (ph.t("junk", [128, D], BF16), ph.rot("st", 2, [128, 4], F32), ph.rot("hn", 2, [128, D], F32))

    def load_gain(self, ph, row_ap):
        g = ph.t("gbc", [128, D], F32)
        self.S.dma("sp", g[:], row_ap.partition_broadcast(128), [], [g])
        return g

    def lin_fm(self, ph, W, blocks, actT, KC, ntt, akeys, epi, wbufs=None, sb=1):
        S = self.S
        if wbufs is None:
            wbufs = ph.rot("wfm", 3 if sb == 1 else 2, [128, KC, 128 * sb], BF16)
            if any(m < 128 for _, m in blocks):
                for w_ in wbufs.items:
                    self.zero(w_)
        Wv = W.rearrange("(kc p) n -> p kc n", p=128)
        groups = []
        for bi, (c0, m) in enumerate(blocks):
            if groups and m == 128 and len(groups[-1]) < sb and groups[-1][-1][2] == 128 and groups[-1][-1][1] + 128 == c0:
                groups[-1].append((bi, c0, m))
            else:
                groups.append([(bi, c0, m)])
        for grp in groups:
            wb = wbufs.next()
            c00 = grp[0][1]
            ncol = sum(m for _, _, m in grp)
            S.dma("pool", wb[:, :, :ncol], Wv[:, :, c00:c00 + ncol], [], [wb])
            for j, (bi, c0, m) in enumerate(grp):
                for tt in range(ntt):
                    ps = S.psum_next()
                    for kc in range(KC):
                        S.mm(ps[:, :], wb[:, kc, j * 128:(j + 1) * 128], actT[:, kc, tt * 512:(tt + 1) * 512], kc == 0, kc == KC - 1,
                             [wb] + akeys(tt), [ps])
                    epi(bi, tt, ps)

    def lin_tm(self, ph, W, n0, ncols, actT, KC, ntok, akeys, epi, wbufs=None, G=8, wload=None):
        S = self.S
        if wbufs is None:
            wbufs = ph.rot("wtm", 3, [128, G, 512], BF16)
        Wv = W.rearrange("(kc p) n -> p kc n", p=128) if wload is None else None
        banks = [S.psum_next() for _ in range(ntok)]
        ng = (KC + G - 1) // G
        for g in range(ng):
            k0 = g * G
            kn = min(G, KC - k0)
            wg = wbufs.next()
            if wload is None:
                S.dma("pool", wg[:, :kn, :ncols], Wv[:, k0:k0 + kn, n0:n0 + ncols], [], [wg])
                extra = []
            else:
                extra = wload(wg, k0, kn)
            for tt in range(ntok):
                for j in range(kn):
                    kc = k0 + j
                    S.mm(banks[tt][:, :ncols], actT[:, kc, tt * 128:(tt + 1) * 128], wg[:, j, :ncols],
                         kc == 0, kc == KC - 1, [wg] + extra + akeys(tt), [banks[tt]])
        for tt in range(ntok):
            epi(tt, banks[tt])

    def memkv(self, layer, s):
        S, nc = self.S, self.nc
        inp = self.inp
        with Phase(self) as ph:
            gbc = self.load_gain(ph, inp["norm_memtok"][layer, :])
            tmps = self.norm_tmps(ph)
            mT = ph.t("mT", [128, 16, 512], BF16)
            xts = ph.rot("xt", 2, [128, D], F32)
            for i in range(2):
                xt = xts.next()
                S.dma("sp", xt[:], inp["mem"][s, i * 128:(i + 1) * 128, :], [], [xt])
                self.norm_tile(ph, xt[:], xt, i, gbc, mT, "mT", tmps)
            akeys = lambda tt: [("mT", 0), ("mT", 1)]
            gk = inp["memg"][layer]
            gcol = ph.t("gcol", [128, 2], F32)
            S.dma("sp", gcol[:], gk, [], [gcol])
            sq = ph.rot("sq", 2, [128, 256], BF16)
            rs = ph.rot("rs", 2, [128, 256], F32)
            tmp = ph.rot("tmp", 2, [128, 256], F32)
            ko = ph.rot("ko", 2, [128, 256], BF16)
            wkv = inp["mem_w_kv"][layer]

            def epi_k(bi, tt, ps):
                q = sq.next()
                S.act(q[:], ps[:, :256], AF.Square, [ps], [q])
                ps2 = S.psum_next()
                S.mm(ps2[:, :256], self.ones[:], q[:], True, True, [q, self.ones], [ps2])
                r, t_ = rs.next(), tmp.next()
                self.rstd_from_ss(r[:], ps2[:, :256], 128, t_[:], [ps2], [r, t_])
                o = ko.next()
                S.stt(o[:], ps[:, :256], gcol[:, 1:2], r[:], ALU.mult, ALU.mult, [ps, r, gcol], [o])
                S.dma("sp", self.scr["MK"][s, bi * 128:(bi + 1) * 128, :], o[:], [o], [])

            wb = ph.rot("wfm", 3, [128, 16, 128], BF16)
            Wv = wkv.rearrange("(kc p) n -> p kc n", p=128)
            for h in range(4):
                w_ = wb.next()
                S.dma("pool", w_[:], Wv[:, :, h * 128:(h + 1) * 128], [], [w_])
                ps = S.psum_next()
                for kc in range(16):
                    S.mm(ps[:, :256], w_[:, kc, :], mT[:, kc, 0:256], kc == 0, kc == 15, [w_] + akeys(0), [ps])
                epi_k(h, 0, ps)
            vo = ph.rot("vo", 2, [128, 512], BF16)

            def epi_v(tt, ps):
                o = vo.next()
                S.copy("act", o[:], ps[:, :], [ps], [o])
                S.dma("sp", self.scr["MV"][s, tt * 128:(tt + 1) * 128, :], o[:], [o], [])

            self.lin_tm(ph, wkv, 512, 512, mT, 16, 2, akeys, epi_v)

    def tail_block(self, layer, s, tb, x_src, x_dst, oT_src, Ko, w_out, parts=("mix", "mem", "mlp")):
        S, nc = self.S, self.nc
        inp = self.inp
        with Phase(self) as ph:
            xres = ph.t("xres", [128, 4, D], F32)
            for i in range(4):
                S.dma("sp", xres[:, i, :], x_src[s, tb * 512 + i * 128: tb * 512 + (i + 1) * 128, :], [], [("xres", i)])

            def epi_res(nb):
                def f(tt, ps):
                    sl = xres[:, tt, nb * 512:(nb + 1) * 512]
                    S.tt("dve", sl, ps[:, :], sl, ALU.add, [ps, ("xres", tt)], [("xres", tt)])
                return f

            if "mix" in parts and oT_src is not None:
                with Phase(self) as p2:
                    KCo = Ko // 128
                    oT = p2.t("oT", [128, KCo, 512], BF16)
                    S.dma("sp", oT[:], oT_src.rearrange("(kc p) t -> p kc t", p=128)[:, :, tb * 512:(tb + 1) * 512], [], [oT])
                    wb = p2.rot("wtm", 3, [128, 8, 512], BF16)
                    for nb in range(4):
                        self.lin_tm(p2, w_out, nb * 512, 512, oT, KCo, 4, lambda tt: [oT], epi_res(nb), wbufs=wb)
            if "mem" in parts:
                with Phase(self) as p2:
                    gbc = self.load_gain(p2, inp["norm_mem"][layer, :])
                    tmps = self.norm_tmps(p2)
                    hT = p2.t("hT", [128, 16, 512], BF16)
                    for i in range(4):
                        self.norm_tile(p2, xres[:, i, :], ("xres", i), i, gbc, hT, "hT", tmps)
                    hkeys = lambda tt: [("hT", i) for i in range(4)]
                    gcol = p2.t("gcol", [128, 2], F32)
                    S.dma("sp", gcol[:], inp["memg"][layer], [], [gcol])
                    mk = p2.t("mk", [128, 4, 256], BF16)
                    S.dma("sp", mk[:], self.scr["MK"][s].rearrange("(h p) m -> p h m", p=128), [], [mk])
                    mv = p2.t("mv", [128, 2, 512], BF16)
                    S.dma("sp", mv[:], self.scr["MV"][s].rearrange("(c p) e -> p c e", p=128), [], [mv])
                    qT = p2.t("qT", [128, 4, 512], BF16)
                    sq = p2.rot("sq", 2, [128, 512], BF16)
                    rs = p2.rot("rs", 2, [128, 512], F32)
                    tmp = p2.rot("tmp", 2, [128, 512], F32)

                    def epi_q(bi, tt, ps):
                        q = sq.next()
                        S.act(q[:], ps[:, :], AF.Square, [ps], [q])
                        ps2 = S.psum_next()
                        S.mm(ps2[:, :], self.ones[:], q[:], True, True, [q, self.ones], [ps2])
                        r, t_ = rs.next(), tmp.next()
                        self.rstd_from_ss(r[:], ps2[:, :], 128, t_[:], [ps2], [r, t_])
                        S.stt(qT[:, bi, :], ps[:, :], gcol[:, 0:1], r[:], ALU.mult, ALU.mult, [ps, r, gcol], [("qT", bi)])

                    self.lin_fm(p2, inp["mem_w_q"][layer], [(h * 128, 128) for h in range(4)], hT, 16, 1, hkeys, epi_q, sb=4)
                    oT = p2.t("oTm", [128, 4, 512], BF16)
                    pT = p2.rot("pT", 4, [128, 512], BF16)
                    rden = p2.rot("rden", 2, [128, 512], F32)
                    for h in range(4):
                        pts = []
                        for mc in range(2):
                            ps = S.psum_next()
                            S.mm(ps[:, :], mk[:, h, mc * 128:(mc + 1) * 128], qT[:, h, :], True, True, [mk, ("qT", h)], [ps])
                            p_ = pT.next()
                            S.act(p_[:], ps[:, :], AF.Exp, [ps], [p_], scale=128 ** -0.5)
                            pts.append(p_)
                        pso = S.psum_next()
                        psd = S.psum_next()
                        for mc in range(2):
                            S.mm(pso[:, :], mv[:, mc, h * 128:(h + 1) * 128], pts[mc][:], mc == 0, mc == 1, [mv, pts[mc]], [pso])
                        for mc in range(2):
                            S.mm(psd[:, :], self.ones[:], pts[mc][:], mc == 0, mc == 1, [self.ones, pts[mc]], [psd])
                        rd = rden.next()
                        S.emit("dve", lambda: nc.vector.reciprocal(rd[:], psd[:, :]), [psd], [rd])
                        S.tt("dve", oT[:, h, :], pso[:, :], rd[:], ALU.mult, [pso, rd], [("oTm", h)])
                    wb = p2.rot("wtm", 3, [128, 8, 512], BF16)
                    okeys = lambda tt: [("oTm", h) for h in range(4)]
                    for nb in range(4):
                        self.lin_tm(p2, inp["mem_w_out"][layer], nb * 512, 512, oT, 4, 4, okeys, epi_res(nb), wbufs=wb)
            if "mlp" in parts:
                with Phase(self) as p2:
                    gbc = self.load_gain(p2, inp["norm_mlp"][layer, :])
                    aT = p2.t("aT", [128, 64, 512], BF16)
                    hT = p2.t("hT", [128, 16, 512], BF16)
                    with Phase(self) as p3:
                        tmps = self.norm_tmps(p3)
                        for i in range(4):
                            self.norm_tile(p3, xres[:, i, :], ("xres", i), i, gbc, hT, "hT", tmps)
                    hkeys = lambda tt: [("hT", i) for i in range(4)]
                    rl = p2.rot("rl", 3, [128, 512], F32)

                    def epi_a(bi, tt, ps):
                        r = rl.next()
                        S.act(r[:], ps[:, :], AF.Relu, [ps], [r])
                        S.tt("dve", aT[:, bi, :], r[:], r[:], ALU.mult, [r], [("aT", bi)])

                    self.lin_fm(p2, inp["mlp_w1"][layer], [(j * 128, 128) for j in range(64)], hT, 16, 1, hkeys, epi_a)
                    wb = p2.rot("wtm", 3, [128, 8, 512], BF16)
                    for nb in range(4):
                        self.lin_tm(p2, inp["mlp_w2"][layer], nb * 512, 512, aT, 64, 4,
                                    lambda tt: [("aT", j) for j in range(64)] if tt == 0 else [], epi_res(nb), wbufs=wb)
            for i in range(4):
                S.dma("sp", x_dst[s, tb * 512 + i * 128: tb * 512 + (i + 1) * 128, :], xres[:, i, :], [("xres", i)], [])


    def mlp_block(self, layer, s, tb, x_src, x_dst):
        S, nc, inp = self.S, self.nc, self.inp
        r0 = tb * 1024
        with Phase(self) as ph:
            aT = ph.t("aT", [128, 64, 1024], BF16)
            with Phase(self) as p2:
                gbc = self.load_gain(p2, inp["norm_mlp"][layer, :])
                hT = p2.t("hT", [128, 16, 1024], BF16)
                with Phase(self) as p3:
                    tmps = self.norm_tmps(p3)
                    xts = p3.rot("xt", 2, [128, D], F32)
                    for i in range(8):
                        xt = xts.next()
                        S.dma("sp", xt[:], x_src[s, r0 + i * 128:r0 + (i + 1) * 128, :], [], [xt])
                        self.norm_tile(p3, xt[:], xt, i, gbc, hT, "hT", tmps)
                hkeys = lambda tt: [("hT", tt * 4 + i) for i in range(4)]
                rl = p2.rot("rl", 3, [128, 512], F32)

                def epi_a(bi, tt, ps):
                    r = rl.next()
                    S.act(r[:], ps[:, :], AF.Relu, [ps], [r])
                    S.tt("dve", aT[:, bi, tt * 512:(tt + 1) * 512], r[:], r[:], ALU.mult, [r], [("aT", bi, tt)])

                self.lin_fm(p2, inp["mlp_w1"][layer], [(j * 128, 128) for j in range(64)], hT, 16, 2, hkeys, epi_a, sb=4)
            with Phase(self) as p2:
                wb = p2.rot("wtm", 3, [128, 8, 512], BF16)
                xin = p2.rot("xin", 16, [128, 512], F32)
                xo = p2.rot("xo", 4, [128, 512], F32)
                for nb in range(4):
                    cs = slice(nb * 512, (nb + 1) * 512)
                    tiles = []
                    for tt in range(8):
                        xi = xin.next()
                        S.dma("sp", xi[:], x_src[s, r0 + tt * 128:r0 + (tt + 1) * 128, cs], [], [xi])
                        tiles.append(xi)

                    def epi(tt, ps, tiles=tiles, cs=cs):
                        o = xo.next()
                        S.tt("dve", o[:], ps[:, :], tiles[tt][:], ALU.add, [ps, tiles[tt]], [o])
                        S.dma("sp", x_dst[s, r0 + tt * 128:r0 + (tt + 1) * 128, cs], o[:], [o], [])

                    self.lin_tm(p2, inp["mlp_w2"][layer], nb * 512, 512, aT, 64, 8, lambda tt: [], epi, wbufs=wb)

    def attn_core(self, ph, kq, V, vkey, nsc, ntt, scale, store):
        S, nc = self.S, self.nc
        if not hasattr(ph, "abufs"):
            ph.abufs = (ph.rot("pT", 7, [128, 512], BF16), ph.rot("rden", 2, [128, 512], F32), ph.rot("ot", 3, [128, 512], BF16))
        pT, rden, ot = ph.abufs
        acc = Rot([(S.psum[4], S.psum[5]), (S.psum[6], S.psum[7])])
        sbank = Rot([S.psum[i] for i in range(4)])
        LA = 3
        for tt in range(ntt):
            pso, psd = acc.next()
            pend = []

            def flush_one(pso=pso, psd=psd):
                sc0, p0 = pend.pop(0)
                S.mm(pso[:, :], V(sc0), p0[:], sc0 == 0, sc0 == nsc - 1, [p0, vkey], [pso])
                S.mm(psd[:, :], self.ones[:], p0[:], sc0 == 0, sc0 == nsc - 1, [p0, self.ones], [psd])

            for sc in range(nsc):
                ps = sbank.next()
                for pi, (kf, qf, keys) in enumerate(kq):
                    S.mm(ps[:, :], kf(sc), qf(tt), pi == 0, pi == len(kq) - 1, keys, [ps])
                p_ = pT.next()
                S.act(p_[:], ps[:, :], AF.Exp, [ps], [p_], scale=scale)
                pend.append((sc, p_))
                if len(pend) > LA:
                    flush_one()
            while pend:
                flush_one()
            rd = rden.next()
            S.emit("dve", lambda: nc.vector.reciprocal(rd[:], psd[:, :]), [psd], [rd])
            o = ot.next()
            S.tt("dve", o[:], pso[:, :], rd[:], ALU.mult, [pso, rd], [o])
            store(tt, o)

    def load_x_norm(self, ph, x_src, s, gain_row, hT, hkey, L):
        S = self.S
        gbc = self.load_gain(ph, gain_row)
        with Phase(self) as p3:
            tmps = self.norm_tmps(p3)
            xts = p3.rot("xt", 2, [128, D], F32)
            for i in range(L // 128):
                xt = xts.next()
                S.dma("sp", xt[:], x_src[s, i * 128:(i + 1) * 128, :], [], [xt])
                self.norm_tile(p3, xt[:], xt, i, gbc, hT, hkey, tmps)

    def qk_rope_epi(self, ph, cosT, sinT, perm, P):
        S = self.S
        sq = ph.rot("sq", 2, [128, 512], BF16)
        rs = ph.rot("rs", 2, [128, 512], F32)
        tmp = ph.rot("tmp", 2, [128, 512], F32)
        qg = ph.rot("qg", 2, [128, 512], BF16)
        if P < 128:
            for q_ in qg.items:
                self.zero(q_)
        t1 = ph.rot("t1", 2, [128, 512], F32)
        t2 = ph.rot("t2", 2, [128, 512], F32)

        def f(ps, gcol, gkey, tt, out_ap, okey, rstd=None, pskey=None):
            pk = pskey if pskey is not None else ps
            if rstd is None:
                assert P == 128
                q = sq.next()
                S.act(q[:P, :], ps[:P, :], AF.Square, [ps], [q])
                ps2 = S.psum_next()
                S.mm(ps2[:, :], self.ones[:P, :], q[:P, :], True, True, [q, self.ones], [ps2])
                r, t_ = rs.next(), tmp.next()
                self.rstd_from_ss(r[:], ps2[:, :], P, t_[:], [ps2], [r, t_])
                rstd = r
            g = qg.next()
            S.act(g[:P, :], ps[:P, :], AF.Copy, [pk, gkey], [g], scale=gcol)
            ps3 = S.psum_next()
            S.mm(ps3[:, :], perm[:, :], g[:, :], True, True, [g, perm], [ps3])
            a, b_ = t1.next(), t2.next()
            S.tt("dve", a[:P, :], g[:P, :], cosT[:P, tt * 512:(tt + 1) * 512], ALU.mult, [g, cosT], [a])
            S.tt("dve", b_[:P, :], ps3[:P, :], sinT[:P, tt * 512:(tt + 1) * 512], ALU.mult, [ps3, sinT], [b_])
            S.tt("dve", a[:P, :], a[:P, :], b_[:P, :], ALU.add, [a, b_], [a])
            if rstd == "none":
                S.copy("act", out_ap, a[:P, :], [a], [okey])
            else:
                S.tt("dve", out_ap, a[:P, :], rstd[:P, :], ALU.mult, [a, rstd], [okey])
            return rstd
        return f

    def zero(self, t):
        self.S.emit("pool", lambda: self.nc.gpsimd.memset(t[:], 0.0), [], [t])

    def load_perm(self, ph, name, P):
        S = self.S
        pf = ph.t("permf", [128, 128], F32)
        self.zero(pf)
        S.dma("sp", pf[:P, :P], self.inp[name], [], [pf])
        pb = ph.t("permb", [128, 128], BF16)
        S.copy("dve", pb[:], pf[:], [pf], [pb])
        return pb

    def gqa_head(self, layer, s, x_src):
        S, nc, L, inp = self.S, self.nc, self.L, self.inp
        ntt = L // 512
        w_in = inp["gqa_w_in"][0]
        with Phase(self) as ph:
            hT = ph.t("hT", [128, 16, L], BF16)
            self.load_x_norm(ph, x_src, s, inp["norm_mix"][layer, :], hT, "hT", L)
            hk = lambda tt: [("hT", tt * 4 + i) for i in range(4)]
            cosT = ph.t("cosT", [128, L], F32)
            sinT = ph.t("sinT", [128, L], F32)
            S.dma("sp", cosT[:], inp["c_gqa_cos"], [], [cosT])
            S.dma("sp", sinT[:], inp["c_gqa_sin"], [], [sinT])
            perm = self.load_perm(ph, "c_perm128", 128)
            gcol = ph.t("gcol", [128, 2], F32)
            S.dma("sp", gcol[:], inp["gqag"], [], [gcol])
            epi = self.qk_rope_epi(ph, cosT, sinT, perm, 128)
            ob = ph.rot("ob", 3, [128, 512], BF16)

            def epi_qk(bi, tt, ps):
                isq = bi < 16
                o = ob.next()
                epi(ps, gcol[:, 0:1] if isq else gcol[:, 1:2], gcol, tt, o[:], o)
                dst = self.scr["QT"][bi * 128:(bi + 1) * 128, :] if isq else self.scr["KT"][(bi - 16) * 128:(bi - 15) * 128, :]
                S.dma("sp", dst[:, tt * 512:(tt + 1) * 512], o[:], [o], [])

            if "noqk" not in self.flags:
                self.lin_fm(ph, w_in, [(h * 128, 128) for h in range(20)], hT, 16, ntt, hk, epi_qk, sb=4)
            vo = ph.rot("vo", 3, [128, 512], BF16)
            wbv = ph.rot("wtm", 3, [128, 8, 512], BF16)
            for t4 in range(L // 512 if "nov" not in self.flags else 0):
                def epi_v(tt, ps, t4=t4):
                    o = vo.next()
                    S.copy("act", o[:], ps[:, :], [ps], [o])
                    r0 = t4 * 512 + tt * 128
                    S.dma("sp", self.scr["VV"][r0:r0 + 128, 0:512], o[:], [o], [])
                self.lin_tm(ph, w_in, 2560, 512, hT[:, :, t4 * 512:(t4 + 1) * 512], 16, 4,
                            lambda tt, t4=t4: [("hT", t4 * 4 + tt)], epi_v, wbufs=wbv)

    def gqa_mix(self, s):
        S, nc, L = self.S, self.nc, self.L
        nsc, ntt = L // 128, L // 512
        with Phase(self) as ph:
            kTs = ph.rot("kT", 2, [128, L], BF16)
            vs = ph.rot("v", 2, [128, nsc, 128], BF16)
            qTs = ph.rot("qT", 2, [128, L], BF16)
            for g in range(4):
                kT, v = kTs.next(), vs.next()
                S.dma("sp", kT[:], self.scr["KT"][g * 128:(g + 1) * 128, :], [], [kT])
                S.dma("sp", v[:], self.scr["VV"][:, g * 128:(g + 1) * 128].rearrange("(c p) e -> p c e", p=128), [], [v])
                for hh in range(4):
                    h = g * 4 + hh
                    qT = qTs.next()
                    S.dma("sp", qT[:], self.scr["QT"][h * 128:(h + 1) * 128, :], [], [qT])

                    def store(tt, o, h=h):
                        S.dma("sp", self.scr["OT"][h * 128:(h + 1) * 128, tt * 512:(tt + 1) * 512], o[:], [o], [])

                    kq = [(lambda sc, kT=kT: kT[:, sc * 128:(sc + 1) * 128], lambda tt, qT=qT: qT[:, tt * 512:(tt + 1) * 512], [kT, qT])]
                    self.attn_core(ph, kq, lambda sc, v=v: v[:, sc, :], v, nsc, ntt, 128 ** -0.5, store)


    def mla_head(self, layer, s, x_src):
        S, nc, L, inp = self.S, self.nc, self.L, self.inp
        ntt = L // 512
        with Phase(self) as ph:
            craw = ph.t("craw", [128, 8, L], F32)
            krraw = ph.t("krraw", [64, L], F32)
            with Phase(self) as pa:
                hT = pa.t("hT", [128, 16, L], BF16)
                self.load_x_norm(pa, x_src, s, inp["norm_mix"][layer, :], hT, "hT", L)
                hk = lambda tt: [("hT", tt * 4 + i) for i in range(4)]

                def epi_c(bi, tt, ps):
                    if bi < 8:
                        S.copy("act" if (bi + tt) % 2 else "dve", craw[:, bi, tt * 512:(tt + 1) * 512], ps[:, :], [ps], [("craw", bi, tt)])
                    else:
                        S.copy("act", krraw[:, tt * 512:(tt + 1) * 512], ps[:64, :], [ps], [("krraw", tt)])

                self.lin_fm(pa, inp["mla_w_in"][0], [(j * 128, 128) for j in range(8)] + [(1024, 64)], hT, 16, ntt, hk, epi_c)
            cn = ph.t("cn", [128, 8, L], BF16)
            lg = ph.t("lg", [128, 8], F32)
            S.dma("sp", lg[:], inp["mlalat"], [], [lg])
            mg = ph.t("mg", [128, 4], F32)
            S.dma("sp", mg[:], inp["mlag"], [], [mg])
            sq = ph.rot("sq", 3, [128, 512], BF16)
            rs = ph.rot("rs", 2, [128, 512], F32)
            tmp = ph.rot("tmp", 2, [128, 512], F32)
            for half in range(2):
                for tt in range(ntt):
                    ps2 = S.psum_next()
                    for j in range(4):
                        q = sq.next()
                        c = half * 4 + j
                        S.act(q[:], craw[:, c, tt * 512:(tt + 1) * 512], AF.Square, [("craw", c, tt)], [q])
                        S.mm(ps2[:, :], self.ones[:], q[:], j == 0, j == 3, [q, self.ones], [ps2])
                    r, t_ = rs.next(), tmp.next()
                    self.rstd_from_ss(r[:], ps2[:, :], 512, t_[:], [ps2], [r, t_])
                    for j in range(4):
                        c = half * 4 + j
                        S.stt(cn[:, c, tt * 512:(tt + 1) * 512], craw[:, c, tt * 512:(tt + 1) * 512], lg[:, c:c + 1], r[:],
                              ALU.mult, ALU.mult, [("craw", c, tt), lg, r], [("cn", c, tt)])
            cosT = ph.t("cosT", [64, L], F32)
            sinT = ph.t("sinT", [64, L], F32)
            S.dma("sp", cosT[:], inp["c_mla_cos"], [], [cosT])
            S.dma("sp", sinT[:], inp["c_mla_sin"], [], [sinT])
            perm = self.load_perm(ph, "c_perm64", 64)
            epi = self.qk_rope_epi(ph, cosT, sinT, perm, 64)
            Rk = ph.t("Rk", [64, L], F32)
            sqkr = ph.t("sqkr", [128, L], BF16)
            self.zero(sqkr)
            S.barrier()
            for tt in range(ntt):
                sl = slice(tt * 512, (tt + 1) * 512)
                S.act(sqkr[:64, sl], krraw[:, sl], AF.Square, [("krraw", tt)], [("sqkr", tt)])
                epi(krraw[:, sl], mg[:64, 3:4], mg, tt, Rk[:, sl], ("Rk", tt), rstd="none", pskey=("krraw", tt))
            ob = ph.rot("ob", 3, [128, 512], BF16)
            ob2 = ph.rot("ob2", 3, [64, 512], BF16)
            wq = ph.rot("wq", 2, [128, 4, 256], BF16)
            for w_ in wq.items:
                self.zero(w_)
            Wq = inp["mla_w_qb"][0].rearrange("(kc p) n -> p kc n", p=128)
            Wkv = inp["mla_w_kvb"][0].rearrange("(kc p) n -> p kc n", p=128)

            def norm192(psn, sqr_ap, sqr_key):
                q = sq.next()
                S.act(q[:], psn[:, :], AF.Square, [psn], [q])
                ps2 = S.psum_next()
                S.mm(ps2[:, :], self.ones[:], q[:], True, False, [q, self.ones], [ps2])
                S.mm(ps2[:, :], self.ones[:, :], sqr_ap, False, True, [sqr_key, self.ones], [ps2])
                r, t_ = rs.next(), tmp.next()
                self.rstd_from_ss(r[:], ps2[:, :], 192, t_[:], [ps2], [r, t_])
                return r

            sqr = ph.rot("sqr", 2, [128, 512], BF16)
            for q_ in sqr.items:
                self.zero(q_)
            for h in range(16):
                w_ = wq.next()
                S.dma("pool", w_[:, :, 0:192], Wq[:, :, h * 192:(h + 1) * 192], [], [w_])
                for tt in range(ntt):
                    sl = slice(tt * 512, (tt + 1) * 512)
                    ck = [("cn", j, tt) for j in range(4)]
                    psn, psr = S.psum_next(), S.psum_next()
                    for j in range(4):
                        S.mm(psn[:, :], w_[:, j, 0:128], cn[:, j, sl], j == 0, j == 3, [w_] + ck, [psn])
                    for j in range(4):
                        S.mm(psr[:, :], w_[:, j, 128:256], cn[:, j, sl], j == 0, j == 3, [w_] + ck, [psr])
                    q2 = sqr.next()
                    S.act(q2[:64, :], psr[:64, :], AF.Square, [psr], [q2])
                    r = norm192(psn, q2[:], q2)
                    o = ob.next()
                    S.stt(o[:], psn[:, :], mg[:, 0:1], r[:], ALU.mult, ALU.mult, [psn, mg, r], [o])
                    S.dma("sp", self.scr["QT"][h * 192:h * 192 + 128, sl], o[:], [o], [])
                    o2 = ob2.next()
                    epi(psr[:64, :], mg[:64, 1:2], mg, tt, o2[:], o2, rstd=r, pskey=psr)
                    S.dma("sp", self.scr["QT"][h * 192 + 128:(h + 1) * 192, sl], o2[:], [o2], [])
            wk = ph.rot("wk", 2, [128, 4, 128], BF16)
            for h in range(16):
                w_ = wk.next()
                S.dma("pool", w_[:], Wkv[:, 4:8, h * 256:h * 256 + 128] if False else
                      inp["mla_w_kvb"][0].rearrange("(kc p) n -> p kc n", p=128)[:, :, h * 256:h * 256 + 128], [], [w_])
                for tt in range(ntt):
                    sl = slice(tt * 512, (tt + 1) * 512)
                    ck = [("cn", 4 + j, tt) for j in range(4)]
                    psn = S.psum_next()
                    for j in range(4):
                        S.mm(psn[:, :], w_[:, j, :], cn[:, 4 + j, sl], j == 0, j == 3, [w_] + ck, [psn])
                    r = norm192(psn, sqkr[:, sl], ("sqkr", tt))
                    o = ob.next()
                    S.stt(o[:], psn[:, :], mg[:, 2:3], r[:], ALU.mult, ALU.mult, [psn, mg, r], [o])
                    S.dma("sp", self.scr["KT"][h * 128:(h + 1) * 128, sl], o[:], [o], [])
                    o2 = ob2.next()
                    S.tt("dve", o2[:], Rk[:, sl], r[:64, :], ALU.mult, [("Rk", tt), r], [o2])
                    S.dma("sp", self.scr["KR"][h * 64:(h + 1) * 64, sl], o2[:], [o2], [])
            vo = ph.rot("vo", 3, [128, 512], BF16)
            Wv5 = inp["mla_w_kvb"][0].rearrange("(kc p) (h two e) -> p kc h two e", p=128, two=2, e=128)
            wbv = ph.rot("wtm", 3, [128, 8, 512], BF16)
            for nb in range(4):
                def wload(wg, k0, kn, nb=nb):
                    ks = []
                    for k in range(4):
                        key = ("wgk", wg.name, k)
                        S.dma("pool", wg[:, k, :].rearrange("p (h e) -> p h e", e=128), Wv5[:, k, nb * 4:(nb + 1) * 4, 1, :], [], [wg, key])
                        ks.append(key)
                    return ks
                for t4 in range(L // 512):
                    def epi_v(tt, ps, t4=t4, nb=nb):
                        o = vo.next()
                        S.copy("act", o[:], ps[:, :], [ps], [o])
                        r0 = t4 * 512 + tt * 128
                        S.dma("sp", self.scr["VV"][r0:r0 + 128, nb * 512:(nb + 1) * 512], o[:], [o], [])
                    self.lin_tm(ph, None, 0, 512, cn[:, 4:8, t4 * 512:(t4 + 1) * 512], 4, 4,
                                lambda tt, t4=t4: [("cn", 4 + j, t4) for j in range(4)], epi_v, wload=wload, wbufs=wbv)

    def mla_mix(self, s):
        S, nc, L = self.S, self.nc, self.L
        nsc, ntt = L // 128, L // 512
        with Phase(self) as ph:
            kTs = ph.rot("kT", 2, [128, L], BF16)
            kRs = ph.rot("kR", 2, [128, L], BF16)
            vs = ph.rot("v", 2, [128, nsc, 128], BF16)
            qTs = ph.rot("qT", 2, [128, L], BF16)
            qRs = ph.rot("qR", 2, [128, L], BF16)
            for t_ in kRs.items + qRs.items:
                self.zero(t_)
            S.barrier()
            for h in range(16):
                kT, kR, v, qT, qR = kTs.next(), kRs.next(), vs.next(), qTs.next(), qRs.next()
                S.dma("sp", kT[:], self.scr["KT"][h * 128:(h + 1) * 128, :], [], [kT])
                S.dma("sp", kR[:64, :], self.scr["KR"][h * 64:(h + 1) * 64, :], [], [kR])
                S.dma("sp", v[:], self.scr["VV"][:, h * 128:(h + 1) * 128].rearrange("(c p) e -> p c e", p=128), [], [v])
                S.dma("sp", qT[:], self.scr["QT"][h * 192:h * 192 + 128, :], [], [qT])
                S.dma("sp", qR[:64, :], self.scr["QT"][h * 192 + 128:(h + 1) * 192, :], [], [qR])

                def store(tt, o, h=h):
                    S.dma("sp", self.scr["OT"][h * 128:(h + 1) * 128, tt * 512:(tt + 1) * 512], o[:], [o], [])

                kq = [(lambda sc, kT=kT: kT[:, sc * 128:(sc + 1) * 128], lambda tt, qT=qT: qT[:, tt * 512:(tt + 1) * 512], [kT, qT]),
                      (lambda sc, kR=kR: kR[:, sc * 128:(sc + 1) * 128], lambda tt, qR=qR: qR[:, tt * 512:(tt + 1) * 512], [kR, qR])]
                self.attn_core(ph, kq, lambda sc, v=v: v[:, sc, :], v, nsc, ntt, 192 ** -0.5, store)


    def ret_head(self, layer, s, x_src):
        S, nc, L, inp = self.S, self.nc, self.L, self.inp
        ntt = L // 512
        w_in = inp["ret_w_in"][0]
        Wv = w_in.rearrange("(kc p) n -> p kc n", p=128)
        with Phase(self) as ph:
            hT = ph.t("hT", [128, 16, L], BF16)
            self.load_x_norm(ph, x_src, s, inp["norm_mix"][layer, :], hT, "hT", L)
            hk = lambda tt: [("hT", tt * 4 + i) for i in range(4)]
            with Phase(self) as p2:
                cosT = p2.t("cosT", [128, L], F32)
                sinT = p2.t("sinT", [128, L], F32)
                S.dma("sp", cosT[:], inp["c_ret_cos"], [], [cosT])
                S.dma("sp", sinT[:], inp["c_ret_sin"], [], [sinT])
                wq = p2.rot("wq", 2, [128, 16, 256], BF16)
                t1 = p2.rot("t1", 2, [128, 512], F32)
                t2 = p2.rot("t2", 2, [128, 512], F32)
                ob = p2.rot("ob", 4, [128, 512], BF16)
                for which in range(2):
                    dst = self.scr["QT"] if which == 0 else self.scr["KT"]
                    for h in range(8):
                        w_ = wq.next()
                        c0 = which * 2048 + h * 256
                        S.dma("pool", w_[:], Wv[:, :, c0:c0 + 256], [], [w_])
                        for tt in range(ntt):
                            sl = slice(tt * 512, (tt + 1) * 512)
                            psa, psb = S.psum_next(), S.psum_next()
                            for kc in range(16):
                                S.mm(psa[:, :], w_[:, kc, 0:128], hT[:, kc, sl], kc == 0, kc == 15, [w_] + hk(tt), [psa])
                            for kc in range(16):
                                S.mm(psb[:, :], w_[:, kc, 128:256], hT[:, kc, sl], kc == 0, kc == 15, [w_] + hk(tt), [psb])
                            a, b_ = t1.next(), t2.next()
                            S.tt("dve", a[:], psa[:, :], cosT[:, sl], ALU.mult, [psa, cosT], [a])
                            S.tt("dve", b_[:], psb[:, :], sinT[:, sl], ALU.mult, [psb, sinT], [b_])
                            o1 = ob.next()
                            S.tt("dve", o1[:], a[:], b_[:], ALU.subtract, [a, b_], [o1])
                            S.dma("sp", dst[h * 256:h * 256 + 128, sl], o1[:], [o1], [])
                            a, b_ = t1.next(), t2.next()
                            S.tt("dve", a[:], psa[:, :], sinT[:, sl], ALU.mult, [psa, sinT], [a])
                            S.tt("dve", b_[:], psb[:, :], cosT[:, sl], ALU.mult, [psb, cosT], [b_])
                            o2 = ob.next()
                            S.tt("dve", o2[:], a[:], b_[:], ALU.add, [a, b_], [o2])
                            S.dma("sp", dst[h * 256 + 128:(h + 1) * 256, sl], o2[:], [o2], [])
            with Phase(self) as p2:
                go = p2.rot("go", 3, [128, 512], BF16)

                def epi_g(bi, tt, ps):
                    o = go.next()
                    S.act(o[:], ps[:, :], AF.Silu, [ps], [o])
                    S.dma("sp", self.scr["GT"][bi * 128:(bi + 1) * 128, tt * 512:(tt + 1) * 512], o[:], [o], [])

                self.lin_fm(p2, w_in, [(8192 + j * 128, 128) for j in range(32)], hT, 16, ntt, hk, epi_g, sb=4)
                vo = p2.rot("vo", 3, [128, 512], BF16)
                wb = p2.rot("wtm", 3, [128, 8, 512], BF16)
                for nb in range(8):
                    for t4 in range(L // 512):
                        def epi_v(tt, ps, t4=t4, nb=nb):
                            o = vo.next()
                            S.copy("act", o[:], ps[:, :], [ps], [o])
                            r0 = t4 * 512 + tt * 128
                            S.dma("sp", self.scr["VV"][r0:r0 + 128, nb * 512:(nb + 1) * 512], o[:], [o], [])
                        self.lin_tm(p2, w_in, 4096 + nb * 512, 512, hT[:, :, t4 * 512:(t4 + 1) * 512], 16, 4,
                                    lambda tt, t4=t4: [("hT", t4 * 4 + tt)], epi_v, wbufs=wb)

    def ret_mix(self, s):
        S, nc, L, inp = self.S, self.nc, self.L, self.inp
        nsc, ntt = L // 128, L // 512
        W = 2 * L - 128
        with Phase(self) as ph:
            lgx = ph.t("lgx", [128, 16], F32)
            S.dma("sp", lgx[:], inp["ret_decay"].rearrange("a b h -> (a b h)").partition_broadcast(128), [], [lgx])
            ax = ph.t("ax", [128, 16], F32)
            S.act(ax[:], lgx[:], AF.Abs, [lgx], [ax])
            S.act(ax[:], ax[:], AF.Exp, [ax], [ax], scale=-1.0)
            S.act(ax[:], ax[:], AF.Ln, [ax], [ax], bias=self.onecol[:, :])
            lg = ph.t("lg", [128, 16], F32)
            S.ts("dve", lg[:], lgx[:], 0.0, None, ALU.min, None, [lgx], [lg])
            S.tt("dve", lg[:], lg[:], ax[:], ALU.subtract, [lg, ax], [lg])
            nlg = ph.t("nlg", [128, 16], F32)
            S.ts("dve", nlg[:], lg[:], -1.0, None, ALU.mult, None, [lg], [nlg])
            gout = ph.t("gout", [128, 4], F32)
            S.dma("sp", gout[:], inp["retg"], [], [gout])
            diff = ph.t("diff", [128, W], F32)
            S.dma("sp", diff[:], inp["c_ret_diff"], [], [diff])
            A = ph.t("A", [128, W], F32)
            E = ph.t("E", [128, W], F32)
            qTs = ph.rot("qT", 2, [128, 2, L], BF16)
            kTs = ph.rot("kT", 2, [128, 2, L], BF16)
            vs = ph.rot("v", 2, [128, nsc, 512], BF16)
            gs = ph.rot("g", 2, [128, 4, L], BF16)
            pT = ph.rot("pT", 6, [128, 512], BF16)
            sq = ph.rot("sq", 4, [128, 512], BF16)
            rs = ph.rot("rs", 2, [128, 512], F32)
            tmp = ph.rot("tmp", 2, [128, 512], F32)
            on = ph.rot("on", 2, [128, 512], F32)
            ob = ph.rot("ob", 3, [128, 512], BF16)
            sbank = Rot([S.psum[i] for i in range(4)])
            O = [S.psum[4 + i] for i in range(4)]
            for h in range(8):
                qT, kT, v, g = qTs.next(), kTs.next(), vs.next(), gs.next()
                S.dma("sp", qT[:], self.scr["QT"][h * 256:(h + 1) * 256, :].rearrange("(c p) t -> p c t", p=128), [], [qT])
                S.dma("sp", kT[:], self.scr["KT"][h * 256:(h + 1) * 256, :].rearrange("(c p) t -> p c t", p=128), [], [kT])
                S.dma("sp", v[:], self.scr["VV"][:, h * 512:(h + 1) * 512].rearrange("(c p) e -> p c e", p=128), [], [v])
                S.dma("sp", g[:], self.scr["GT"][h * 512:(h + 1) * 512, :].rearrange("(c p) t -> p c t", p=128), [], [g])
                S.act(A[:], diff[:], AF.Relu, [diff, nlg], [A], scale=nlg[:, h:h + 1])
                S.act(E[:], diff[:], AF.Relu, [diff, lg], [E], scale=lg[:, 8 + h:9 + h])
                S.tt("dve", A[:], A[:], E[:], ALU.add, [A, E], [A])
                S.act(E[:], A[:], AF.Exp, [A], [E], scale=-1.0)
                S.ts("dve", A[:], diff[:], 0.0, 1.0 / 16, ALU.is_equal, ALU.mult, [diff], [A])
                S.stt(E[:], A[:], 1.0 / 16, E[:], ALU.add, ALU.mult, [A, E], [E])
                for tt in range(ntt):
                    sl = slice(tt * 512, (tt + 1) * 512)
                    pend = []

                    def flush_one():
                        sc0, p0 = pend.pop(0)
                        for ec in range(4):
                            S.mm(O[ec][:, :], v[:, sc0, ec * 128:(ec + 1) * 128], p0[:], sc0 == 0, sc0 == nsc - 1, [v, p0], [O[ec]])

                    for sc in range(nsc):
                        ps = sbank.next()
                        for c in range(2):
                            S.mm(ps[:, :], kT[:, c, sc * 128:(sc + 1) * 128], qT[:, c, sl], c == 0, c == 1, [kT, qT], [ps])
                        p_ = pT.next()
                        off = tt * 512 - sc * 128 + (L - 128)
                        S.tt("dve", p_[:], ps[:, :], E[:, off:off + 512], ALU.mult, [ps, E], [p_])
                        pend.append((sc, p_))
                        if len(pend) > 2:
                            flush_one()
                    while pend:
                        flush_one()
                    ps2 = sbank.next()
                    for ec in range(4):
                        q = sq.next()
                        S.act(q[:], O[ec][:, :], AF.Square, [O[ec]], [q])
                        S.mm(ps2[:, :], self.ones[:], q[:], ec == 0, ec == 3, [q, self.ones], [ps2])
                    r, t_ = rs.next(), tmp.next()
                    self.rstd_from_ss(r[:], ps2[:, :], 512, t_[:], [ps2], [r, t_])
                    for ec in range(4):
                        n_ = on.next()
                        S.tt("dve", n_[:], O[ec][:, :], r[:], ALU.mult, [O[ec], r], [n_])
                        o = ob.next()
                        S.stt(o[:], n_[:], gout[:, ec:ec + 1], g[:, ec, sl], ALU.mult, ALU.mult, [n_, gout, g], [o])
                        S.dma("sp", self.scr["OT"][h * 512 + ec * 128:h * 512 + (ec + 1) * 128, sl], o[:], [o], [])


    def hg_head(self, layer, s, x_src):
        S, nc, L, inp = self.S, self.nc, self.L, self.inp
        ntt = L // 512
        w_in = inp["hg_w_in"][0]
        with Phase(self) as ph:
            hT = ph.t("hT", [128, 16, L], BF16)
            self.load_x_norm(ph, x_src, s, inp["norm_mix"][layer, :], hT, "hT", L)
            hk = lambda tt: [("hT", tt * 4 + i) for i in range(4)]
            lraw = ph.t("lraw", [128, 4, 16], F32)
            S.dma("sp", lraw[:], inp["hglb"], [], [lraw])
            mx = ph.t("mx", [128, 16], F32)
            S.tt("dve", mx[:], lraw[:, 0, :], lraw[:, 1, :], ALU.max, [lraw], [mx])
            S.tt("dve", mx[:], mx[:], lraw[:, 2, :], ALU.max, [lraw, mx], [mx])
            S.tt("dve", mx[:], mx[:], lraw[:, 3, :], ALU.max, [lraw, mx], [mx])
            for j in range(4):
                S.tt("dve", lraw[:, j, :], lraw[:, j, :], mx[:], ALU.subtract, [lraw, mx], [lraw])
            S.act(lraw[:], lraw[:], AF.Exp, [lraw], [lraw])
            sm = ph.t("sm", [128, 16], F32)
            S.tt("dve", sm[:], lraw[:, 0, :], lraw[:, 1, :], ALU.add, [lraw], [sm])
            S.tt("dve", sm[:], sm[:], lraw[:, 2, :], ALU.add, [lraw, sm], [sm])
            S.tt("dve", sm[:], sm[:], lraw[:, 3, :], ALU.add, [lraw, sm], [sm])
            S.emit("dve", lambda: nc.vector.reciprocal(sm[:], sm[:]), [sm], [sm])
            lb = ph.t("lb", [128, 16], F32)
            S.copy("dve", lb[:], lraw[:, 1, :], [lraw], [lb])
            for j in range(2, layer + 1):
                S.tt("dve", lb[:], lb[:], lraw[:, j, :], ALU.add, [lraw, lb], [lb])
            S.tt("dve", lb[:], lb[:], sm[:], ALU.mult, [lb, sm], [lb])
            oml = ph.t("oml", [128, 16], F32)
            S.ts("dve", oml[:], lb[:], -1.0, 1.0, ALU.mult, ALU.add, [lb], [oml])
            qo = ph.rot("qo", 3, [128, 512], BF16)
            sg = ph.rot("sg", 3, [128, 512], F32)
            lfo = ph.rot("lfo", 3, [128, 512], F32)

            def epi(bi, tt, ps):
                sl = slice(tt * 512, (tt + 1) * 512)
                if bi < 16:
                    o = qo.next()
                    S.copy("dve", o[:], ps[:, :], [ps], [o])
                    S.dma("sp", self.scr["QT"][bi * 128:(bi + 1) * 128, sl], o[:], [o], [])
                elif bi < 48:
                    h = (bi - 16) % 16
                    g_ = sg.next()
                    S.act(g_[:], ps[:, :], AF.Sigmoid, [ps], [g_])
                    o = lfo.next()
                    S.act(o[:], g_[:], AF.Ln, [g_, oml, lb], [o], scale=oml[:, h:h + 1], bias=lb[:, h:h + 1])
                    S.dma("sp", self.scr["LF"][(bi - 16) * 128:(bi - 15) * 128, sl], o[:], [o], [])
                else:
                    o = qo.next()
                    S.act(o[:], ps[:, :], AF.Silu, [ps], [o])
                    S.dma("sp", self.scr["GT"][(bi - 48) * 128:(bi - 47) * 128, sl], o[:], [o], [])

            blocks = [(j * 128, 128) for j in range(48)] + [(8192 + j * 128, 128) for j in range(16)]
            self.lin_fm(ph, w_in, blocks, hT, 16, ntt, hk, epi, sb=4)
            vo = ph.rot("vo", 3, [128, 512], BF16)
            wb = ph.rot("wtm", 3, [128, 8, 512], BF16)
            for nb in range(4):
                for t4 in range(L // 512):
                    def epi_v(tt, ps, t4=t4, nb=nb):
                        o = vo.next()
                        S.copy("act", o[:], ps[:, :], [ps], [o])
                        r0 = t4 * 512 + tt * 128
                        S.dma("sp", self.scr["VV"][r0:r0 + 128, nb * 512:(nb + 1) * 512], o[:], [o], [])
                    self.lin_tm(ph, w_in, 6144 + nb * 512, 512, hT[:, :, t4 * 512:(t4 + 1) * 512], 16, 4,
                                lambda tt, t4=t4: [("hT", t4 * 4 + tt)], epi_v, wbufs=wb)

    def hg_mix(self, s):
        S, nc, L, inp = self.S, self.nc, self.L, self.inp
        nsc, ntt, nch = L // 128, L // 512, L // 32
        with Phase(self) as ph:
            rmask = ph.t("rmask", [128, 4], F32)
            S.dma("sp", rmask[:], inp["c_hg_rowmask"], [], [rmask])
            reset = ph.t("reset", [128, L], F32)
            S.dma("sp", reset[:], inp["c_hg_reset"], [], [reset])
            masks = ph.t("masks", [128, 2, 128], F32)
            S.dma("sp", masks[:], inp["c_hg_masks"], [], [masks])
            gout = ph.t("gout", [128, 1], F32)
            S.dma("sp", gout[:], inp["hgg"], [], [gout])
            qT = ph.t("qT", [128, L], BF16)
            lf = [ph.t("lf0", [128, L], F32), ph.t("lf1", [128, L], F32)]
            v = ph.t("v", [128, nsc, 128], BF16)
            vm = ph.t("vm", [128, nsc, 4, 128], BF16)
            G = ph.t("G", [128, L], BF16)
            b_ = ph.t("b", [128, L], F32)
            e1 = ph.t("e1", [128, L], F32)
            e2 = ph.t("e2", [128, L], F32)
            kg = ph.t("kg", [128, L], F32)
            etot = [ph.t("etot0", [128, nch], F32), ph.t("etot1", [128, nch], F32)]
            q_t = [ph.t("qt0", [128, L], BF16), ph.t("qt1", [128, L], BF16)]
            k_t = [ph.t("kt0", [128, L], BF16), ph.t("kt1", [128, L], BF16)]
            kdT = ph.t("kdT", [128, nsc, 128], BF16)
            U = ph.t("U", [128, nch, 128], F32)
            Sall = [ph.t("Sall0", [128, nch, 128], BF16), ph.t("Sall1", [128, nch, 128], BF16)]
            pm = ph.rot("pm", 4, [128, 128], BF16)
            sq = ph.rot("sq", 2, [128, 512], BF16)
            rs = ph.rot("rs", 2, [128, 512], F32)
            tmp = ph.rot("tmp", 2, [128, 512], F32)
            on = ph.rot("on", 2, [128, 512], F32)
            ob = ph.rot("ob", 2, [128, 512], BF16)
            b3 = lambda t: t[:, :].rearrange("p (c k) -> p c k", k=32)
            obank = Rot([S.psum[6], S.psum[7]])
            sbank = Rot([S.psum[i] for i in range(4)])
            nbank = Rot([S.psum[4], S.psum[5]])
            for h in range(16):
                S.dma("sp", qT[:], self.scr["QT"][h * 128:(h + 1) * 128, :], [], [qT])
                for d in range(2):
                    S.dma("sp", lf[d][:], self.scr["LF"][d * 2048 + h * 128:d * 2048 + (h + 1) * 128, :], [], [lf[d]])
                S.dma("sp", v[:], self.scr["VV"][:, h * 128:(h + 1) * 128].rearrange("(c p) e -> p c e", p=128), [], [v])
                S.dma("sp", G[:], self.scr["GT"][h * 128:(h + 1) * 128, :], [], [G])
                for c in range(4):
                    S.act(vm[:, :, c, :], v[:, :, :], AF.Copy, [v, rmask], [vm], scale=rmask[:, c:c + 1])
                for d in range(2):
                    S.emit("dve", lambda: nc.vector.tensor_tensor_scan(b_[:], reset[:], lf[d][:], 0.0, ALU.mult, ALU.add),
                           [reset, lf[d]], [b_])
                    S.act(etot[d][:], b3(b_)[:, :, 31], AF.Exp, [b_], [etot[d]])
                    if d == 1:
                        S.tt("dve", e1[:], lf[d][:], b_[:], ALU.subtract, [lf[d], b_], [e1])
                        S.tt("dve", b3(b_), b3(e1), b3(b_)[:, :, 31:32].to_broadcast([128, nch, 32]), ALU.add, [e1, b_], [b_])
                    S.act(e1[:], b_[:], AF.Exp, [b_], [e1])
                    S.act(e2[:], b_[:], AF.Exp, [b_], [e2], scale=-1.0)
                    S.act(kg[:], lf[d][:], AF.Exp, [lf[d]], [kg])
                    S.ts("dve", kg[:], kg[:], -1.0, 1.0, ALU.mult, ALU.add, [kg], [kg])
                    S.tt("dve", q_t[d][:], qT[:], e1[:], ALU.mult, [qT, e1], [q_t[d]])
                    S.tt("dve", kg[:], kg[:], e2[:], ALU.mult, [kg, e2], [kg])
                    S.copy("act", k_t[d][:], kg[:], [kg], [k_t[d]])
                    S.tt("dve", b3(e1), b3(kg), etot[d][:, :].to_broadcast([128, nch, 1]).to_broadcast([128, nch, 32]) if False else
                         etot[d][:, :].rearrange("p (c o) -> p c o", o=1).to_broadcast([128, nch, 32]), ALU.mult, [kg, etot[d]], [e1])
                    for g in range(nsc):
                        if g % 4 == 0:
                            pst = S.psum_next()
                        S.tr(pst[:, (g % 4) * 128:(g % 4 + 1) * 128], e1[:, g * 128:(g + 1) * 128], self.ident[:], [e1], [pst])
                        if g % 4 == 3:
                            S.copy("act", kdT[:, g - 3:g + 1, :], pst[:, :].rearrange("p (j t) -> p j t", j=4), [pst], [kdT])
                    for g in range(nsc):
                        psu = S.psum_next()
                        for cc in range(4):
                            S.mm(psu[:, cc * 128:(cc + 1) * 128], kdT[:, g, :], vm[:, g, cc, :], True, True, [kdT, vm], [psu])
                        S.copy("act", U[:, g * 4:(g + 1) * 4, :], psu[:, :].rearrange("p (j t) -> p j t", j=4), [psu], [U])
                    order = range(1, nch) if d == 0 else range(nch - 2, -1, -1)
                    for c in order:
                        pv = c - 1 if d == 0 else c + 1
                        S.stt(U[:, c, :], U[:, pv, :], etot[d][:, c:c + 1], U[:, c, :], ALU.mult, ALU.add, [U, etot[d]], [U])
                    for q4 in range(4):
                        n4 = nch // 4
                        S.copy("act" if q4 % 2 else "dve", Sall[d][:, q4 * n4:(q4 + 1) * n4, :], U[:, q4 * n4:(q4 + 1) * n4, :], [U], [Sall[d]])
                for tt in range(ntt):
                    pso = obank.next()
                    for gi in range(4):
                        g = tt * 4 + gi
                        gs = slice(g * 128, (g + 1) * 128)
                        col = slice(gi * 128, (gi + 1) * 128)
                        pms = []
                        for d in range(2):
                            pss = sbank.next()
                            S.mm(pss[:, 0:128], k_t[d][:, gs], q_t[d][:, gs], True, True, [k_t[d], q_t[d]], [pss])
                            p_ = pm.next()
                            S.tt("dve", p_[:], pss[:, 0:128], masks[:, d, :], ALU.mult, [pss, masks], [p_])
                            pms.append(p_)
                        mms = [(pso[:, col], v[:, g, :], pms[0][:], [v, pms[0]]), (pso[:, col], v[:, g, :], pms[1][:], [v, pms[1]])]
                        for cc in range(4):
                            c = g * 4 + cc
                            cs = slice(c * 32, (c + 1) * 32)
                            ocol = slice(gi * 128 + cc * 32, gi * 128 + (cc + 1) * 32)
                            if c >= 1:
                                mms.append((pso[:, ocol], Sall[0][:, c - 1, :], q_t[0][:, cs], [Sall[0], q_t[0]]))
                            if c <= nch - 2:
                                mms.append((pso[:, ocol], Sall[1][:, c + 1, :], q_t[1][:, cs], [Sall[1], q_t[1]]))
                        for mi, (o_, l_, r_, rd_) in enumerate(mms):
                            S.mm(o_, l_, r_, mi == 0, mi == len(mms) - 1, rd_, [pso])
                    sl = slice(tt * 512, (tt + 1) * 512)
                    q = sq.next()
                    S.act(q[:], pso[:, :], AF.Square, [pso], [q])
                    ps2 = nbank.next()
                    S.mm(ps2[:, :], self.ones[:], q[:], True, True, [q, self.ones], [ps2])
                    r, t_ = rs.next(), tmp.next()
                    self.rstd_from_ss(r[:], ps2[:, :], 128, t_[:], [ps2], [r, t_])
                    n_ = on.next()
                    S.tt("dve", n_[:], pso[:, :], r[:], ALU.mult, [pso, r], [n_])
                    o = ob.next()
                    S.stt(o[:], n_[:], gout[:, 0:1], G[:, sl], ALU.mult, ALU.mult, [n_, gout, G], [o])
                    S.dma("sp", self.scr["OT"][h * 128:(h + 1) * 128, sl], o[:], [o], [])


def declare_inputs(b, L, NSEQ):
    b.din("x", [NSEQ, L, D])
    b.din("mem", [NSEQ, MEMT, D])
    for n in ("norm_mix", "norm_mem", "norm_memtok", "norm_mlp"):
        b.din(n, [4, D])
    b.din("mem_w_q", [4, D, 512])
    b.din("mem_w_kv", [4, D, 1024])
    b.din("mem_w_out", [4, 512, D])
    b.din("memg", [4, 128, 2])
    b.din("mlp_w1", [4, D, DFF])
    b.din("mlp_w2", [4, DFF, D])
    b.din("c_ident", [128, 128])
    b.din("gqa_w_in", [1, D, 3072])
    b.din("gqa_w_out", [1, D, D])
    b.din("gqag", [128, 2])
    b.din("c_gqa_cos", [128, L])
    b.din("c_gqa_sin", [128, L])
    b.din("c_perm128", [128, 128])
    b.din("ret_w_in", [1, D, 12288])
    b.din("ret_w_out", [1, 4096, D])
    b.din("ret_decay", [1, 2, 8])
    b.din("retg", [128, 4])
    b.din("c_ret_cos", [128, L])
    b.din("c_ret_sin", [128, L])
    b.din("c_ret_diff", [128, 2 * L - 128])
    b.din("hg_w_in", [1, D, 10240])
    b.din("hg_w_out", [1, D, D])
    b.din("hglb", [128, 4, 16])
    b.din("hgg", [128, 1])
    b.din("c_hg_rowmask", [128, 4])
    b.din("c_hg_reset", [128, L])
    b.din("c_hg_masks", [128, 2, 128])
    b.din("mla_w_in", [1, D, 1088])
    b.din("mla_w_qb", [1, 512, 3072])
    b.din("mla_w_kvb", [1, 512, 4096])
    b.din("mla_w_out", [1, D, D])
    b.din("mlalat", [128, 8])
    b.din("mlag", [128, 4])
    b.din("c_mla_cos", [64, L])
    b.din("c_mla_sin", [64, L])
    b.din("c_perm64", [64, 64])


def build(L=2048, NSEQ=3, layers=(0, 1, 2, 3), dbg=(), parts=("mix", "mem", "mlp")):
    b = Builder(L, NSEQ, dbg)
    b.flags = parts
    nc = b.nc
    declare_inputs(b, L, NSEQ)
    y = b.dscr("y", [NSEQ, L, D], F32, out=True)
    xr = b.dscr("XR", [NSEQ, L, D], F32)
    b.dscr("MK", [NSEQ, 512, MEMT], BF16)
    b.dscr("MV", [NSEQ, MEMT, 512], BF16)
    b.dscr("QT", [3072, L], BF16)
    b.dscr("KT", [2048 + 64, L], BF16)
    b.dscr("KR", [1024, L], BF16)
    b.dscr("GT", [4096, L], BF16)
    b.dscr("LF", [4096, L], F32)
    b.dscr("OT", [4096, L], BF16)
    b.dscr("VV", [L, 4096], BF16)
    with b.stack:
        b.S = Sched(nc, b.stack)
        b.setup_consts()
        nl = len(layers)
        for li, layer in enumerate(layers):
            src = b.inp["x"] if li == 0 else xr
            dst = y if li == nl - 1 else xr
            kind = layer % 4
            for s in range(NSEQ):
                oT, Ko, w_out = None, 0, None
                if "mix" in parts:
                    if kind == 3:
                        if "nohead" not in parts:
                            b.gqa_head(layer, s, src)
                        if "nomix" not in parts:
                            b.gqa_mix(s)
                        if "noout" not in parts:
                            oT, Ko, w_out = b.scr["OT"][0:2048, :], 2048, b.inp["gqa_w_out"][0]
                    if kind == 0:
                        b.ret_head(layer, s, src)
                        b.ret_mix(s)
                        oT, Ko, w_out = b.scr["OT"], 4096, b.inp["ret_w_out"][0]
                    if kind == 1:
                        b.hg_head(layer, s, src)
                        if "nomix" not in parts:
                            b.hg_mix(s)
                            oT, Ko, w_out = b.scr["OT"][0:2048, :], 2048, b.inp["hg_w_out"][0]
                    if kind == 2:
                        b.mla_head(layer, s, src)
                        b.mla_mix(s)
                        oT, Ko, w_out = b.scr["OT"][0:2048, :], 2048, b.inp["mla_w_out"][0]
                if "mem" in parts:
                    b.memkv(layer, s)
                tparts = tuple(p_ for p_ in parts if p_ != "mlp")
                has_mlp = "mlp" in parts
                mid = xr if has_mlp else dst
                for tb in range(L // 512):
                    b.tail_block(layer, s, tb, src, mid, oT, Ko, w_out, parts=tparts)
                if has_mlp:
                    for tb in range(L // 1024):
                        b.mlp_block(layer, s, tb, xr, dst)
        b.S.barrier()
    print("instr counts", b.S.ninst, "sems", b.S.nsem)
    return b


def rope_tables(L):
    o = {"c_ident": np.eye(128, dtype=np.float32)}
    t = np.arange(L)
    f32 = (10000.0 ** (-np.arange(32, dtype=np.float32) / np.float32(32))).astype(np.float32)
    rows = (t // 64).astype(np.float32)
    cols = (t % 64).astype(np.float32)
    cos = np.zeros((128, L), np.float32)
    sin = np.zeros((128, L), np.float32)
    for d in range(128):
        pos = rows if d < 64 else cols
        ang = (pos * f32[d % 32]).astype(np.float32)
        sgn = -1.0 if (d % 64) < 32 else 1.0
        cos[d] = np.cos(ang)
        sin[d] = sgn * np.sin(ang)
    o["c_gqa_cos"], o["c_gqa_sin"] = cos, sin
    perm = np.zeros((128, 128), np.float32)
    for m in range(128):
        perm[64 * (m // 64) + ((m % 64) + 32) % 64, m] = 1.0
    o["c_perm128"] = perm
    tf = t.astype(np.float32)
    mc = np.zeros((64, L), np.float32)
    ms = np.zeros((64, L), np.float32)
    for d in range(64):
        ang = (tf * f32[d % 32]).astype(np.float32)
        mc[d] = np.cos(ang)
        ms[d] = (-1.0 if d < 32 else 1.0) * np.sin(ang)
    o["c_mla_cos"], o["c_mla_sin"] = mc, ms
    p64 = np.zeros((64, 64), np.float32)
    for m in range(64):
        p64[(m + 32) % 64, m] = 1.0
    o["c_perm64"] = p64
    p_ = np.arange(128)
    o["c_hg_rowmask"] = (p_[:, None] // 32 == np.arange(4)[None, :]).astype(np.float32)
    o["c_hg_reset"] = np.broadcast_to((t % 32 != 0).astype(np.float32)[None, :], (128, L)).copy()
    same = (p_[:, None] // 32) == (p_[None, :] // 32)
    mk = np.zeros((128, 2, 128), np.float32)
    mk[:, 0, :] = (same & (p_[:, None] <= p_[None, :])).astype(np.float32)
    mk[:, 1, :] = (same & (p_[:, None] >= p_[None, :])).astype(np.float32)
    o["c_hg_masks"] = mk
    fr = (10000.0 ** (-np.arange(128, dtype=np.float32) / np.float32(128))).astype(np.float32)
    ang = (fr[:, None] * tf[None, :]).astype(np.float32)
    o["c_ret_cos"], o["c_ret_sin"] = np.cos(ang).astype(np.float32), np.sin(ang).astype(np.float32)
    W = 2 * L - 128
    o["c_ret_diff"] = (np.arange(W, dtype=np.float32)[None, :] - np.arange(128, dtype=np.float32)[:, None] - np.float32(L - 128)).astype(np.float32)
    return o


def host_layout(inputs, L):
    o = {}
    for n in ("norm_mix", "norm_mem", "norm_memtok", "norm_mlp", "mem_w_q", "mem_w_kv", "mem_w_out", "mlp_w1", "mlp_w2",
              "gqa_w_in", "gqa_w_out", "mla_w_in", "mla_w_qb", "mla_w_kvb", "mla_w_out",
              "ret_w_in", "ret_w_out", "ret_decay", "hg_w_in", "hg_w_out"):
        o[n] = np.ascontiguousarray(inputs[n], dtype=np.float32)
    g = np.asarray(inputs["mem_qk_norm"], dtype=np.float32)
    o["memg"] = np.ascontiguousarray(g.transpose(0, 2, 1))
    o["gqag"] = np.ascontiguousarray(np.asarray(inputs["gqa_qk_norm"], dtype=np.float32)[0].T)
    o["hglb"] = np.ascontiguousarray(np.asarray(inputs["hg_lb"], np.float32).reshape(4, 16, 128).transpose(2, 0, 1))
    o["hgg"] = np.ascontiguousarray(np.asarray(inputs["hg_out_norm"], np.float32)[0].reshape(128, 1))
    o["retg"] = np.ascontiguousarray(np.asarray(inputs["ret_out_norm"], np.float32)[0].reshape(4, 128).T)
    lat = np.concatenate([np.asarray(inputs["mla_q_norm"], np.float32)[0].reshape(4, 128),
                          np.asarray(inputs["mla_kv_norm"], np.float32)[0].reshape(4, 128)], axis=0)
    o["mlalat"] = np.ascontiguousarray(lat.T)
    qk = np.asarray(inputs["mla_qk_norm"], np.float32)[0]
    mg = np.ones((128, 4), np.float32)
    mg[:, 0] = qk[0, :128]
    mg[:64, 1] = qk[0, 128:]
    mg[:, 2] = qk[1, :128]
    mg[:64, 3] = qk[1, 128:]
    o["mlag"] = mg
    o.update(rope_tables(L))
    return o


def kernel(**inputs):
    L, NSEQ = 2048, 3
    b = build(L, NSEQ)
    shared = host_layout(inputs, L)
    xp, xs = np.asarray(inputs["x_prompt"]), np.asarray(inputs["x_sample"])
    mp, ms = np.asarray(inputs["mem_prompt"]), np.asarray(inputs["mem_sample"])
    in_maps = []
    for c in range(8):
        m = dict(shared)
        m["x"] = np.ascontiguousarray(np.stack([xp[c], xs[2 * c], xs[2 * c + 1]]))
        m["mem"] = np.ascontiguousarray(np.stack([mp[c], ms[2 * c], ms[2 * c + 1]]))
        in_maps.append(m)
    res = run_bass_kernel_spmd(b.nc, in_maps, core_ids=list(range(8)))
    yp = np.stack([res.results[c]["y"][0] for c in range(8)])
    ys = np.stack([res.results[c]["y"][j] for c in range(8) for j in (1, 2)])
    return (yp.astype(np.float32), ys.astype(np.float32))
```
